# Optimizing a Trainium2 kernel written in Bass

```python
import math
import jax, jax.numpy as jnp
from jax import lax
import numpy as np

D_MODEL = 1024
BATCH = 8
SEQ = 8192
DEPTH = 1
DEC_BATCH = 8
DEC_SEQ = 64
PAST_LEN = 2048

CHUNK = 64
D_MIX = D_MODEL
SSM_WIDTH = D_MIX // 2
SSM_GROUP = 16
SSM_GROUPS = SSM_WIDTH // SSM_GROUP
SSM_STATE = 64
SSM_BLOCK = 128
ATTN_WIDTH = D_MIX - SSM_WIDTH
HEAD_DIM = 64
N_HEADS = ATTN_WIDTH // HEAD_DIM
Q_BLOCK = 128
ATTN_SCALE = HEAD_DIM ** -0.5
NORM_EPS = 1e-6
IN_WIDTH = 2 * SSM_WIDTH + 4 * ATTN_WIDTH + N_HEADS
IN_SPLITS = [SSM_WIDTH, 2 * SSM_WIDTH, 2 * SSM_WIDTH + ATTN_WIDTH, 2 * SSM_WIDTH + 2 * ATTN_WIDTH,
             2 * SSM_WIDTH + 3 * ATTN_WIDTH, 2 * SSM_WIDTH + 4 * ATTN_WIDTH]

kernel_name = 'hymba_s5_fox_stream_step'


def _rmsnorm(x, g):
    xf = x.astype(jnp.float32)
    y = xf * lax.rsqrt(jnp.mean(xf * xf, axis=-1, keepdims=True) + NORM_EPS)
    return (y * g.astype(jnp.float32)).astype(x.dtype)


def _zoh(log_dt, a_re, a_im, b_re, b_im):
    f32 = jnp.float32
    dt = jnp.exp(log_dt.astype(f32))[:, None]
    a_re = a_re.astype(f32)
    a_im = a_im.astype(f32)
    mag = jnp.exp(a_re * dt)
    ang = a_im * dt
    abar_re = mag * jnp.cos(ang)
    abar_im = mag * jnp.sin(ang)
    den = a_re * a_re + a_im * a_im
    n_re = abar_re - 1.0
    n_im = abar_im
    q_re = (n_re * a_re + n_im * a_im) / den
    q_im = (n_im * a_re - n_re * a_im) / den
    b_re = b_re.astype(f32)
    b_im = b_im.astype(f32)
    bbar_re = q_re[..., None] * b_re - q_im[..., None] * b_im
    bbar_im = q_re[..., None] * b_im + q_im[..., None] * b_re
    return abar_re, abar_im, bbar_re, bbar_im


def _complex_combine(e1, e2):
    a1r, a1i, b1r, b1i = e1
    a2r, a2i, b2r, b2i = e2
    return (a2r * a1r - a2i * a1i, a2r * a1i + a2i * a1r,
            a2r * b1r - a2i * b1i + b2r, a2r * b1i + a2i * b1r + b2i)


def _s5_segment(u, h_re, h_im, abar_re, abar_im, bbar_re, bbar_im, c_re, c_im, d_skip):
    bu_re = jnp.einsum('btgp,gnp->btgn', u, bbar_re)
    bu_im = jnp.einsum('btgp,gnp->btgn', u, bbar_im)
    bu_re = bu_re.at[:, 0].add(abar_re * h_re - abar_im * h_im)
    bu_im = bu_im.at[:, 0].add(abar_re * h_im + abar_im * h_re)
    a_re = jnp.broadcast_to(abar_re, bu_re.shape)
    a_im = jnp.broadcast_to(abar_im, bu_im.shape)
    _, _, x_re, x_im = lax.associative_scan(_complex_combine, (a_re, a_im, bu_re, bu_im), axis=1)
    y = (jnp.einsum('btgn,gpn->btgp', x_re, c_re) - jnp.einsum('btgn,gpn->btgp', x_im, c_im)
         + d_skip * u)
    return y, x_re[:, -1], x_im[:, -1]


def _fox_attend(q, cum_q, pos_q, k, v, cum_k, pos_k):
    s = jnp.einsum('bqhd,bkhd->bhqk', q, k).astype(jnp.float32) * ATTN_SCALE
    s = s + jnp.transpose(cum_q, (0, 2, 1))[:, :, :, None] - jnp.transpose(cum_k, (0, 2, 1))[:, :, None, :]
    mask = pos_k[None, :] <= pos_q[:, None]
    s = jnp.where(mask[None, None], s, -jnp.inf)
    p = jax.nn.softmax(s, axis=-1)
    return jnp.einsum('bhqk,bkhd->bqhd', p.astype(v.dtype), v)


def _layer(x, c, h_re, h_im, past, w_ada, b_ada, norm_g, w_in, b_f, q_norm_g, k_norm_g,
           ssm_log_dt, ssm_a_re, ssm_a_im, ssm_b_re, ssm_b_im, ssm_c_re, ssm_c_im, ssm_d,
           w_glu, b_glu, w_out):
    f32 = jnp.float32
    n_b, n_t, _ = x.shape
    mod = jax.nn.silu(c) @ w_ada + b_ada
    shift, scale, gate = jnp.split(mod, 3, axis=-1)
    h = _rmsnorm(x, norm_g) * (1.0 + scale[:, None, :]) + shift[:, None, :]
    proj = h @ w_in
    u, z_s, q, k, v, z_a, f_logit = jnp.split(proj, IN_SPLITS, axis=-1)

    abar_re, abar_im, bbar_re, bbar_im = _zoh(ssm_log_dt, ssm_a_re, ssm_a_im, ssm_b_re, ssm_b_im)
    c_re = ssm_c_re.astype(f32)
    c_im = ssm_c_im.astype(f32)
    d_skip = ssm_d.astype(f32)
    ug = u.astype(f32).reshape(n_b, n_t, SSM_GROUPS, SSM_GROUP)
    if h_re is None:
        n_blk = n_t // SSM_BLOCK
        ub = ug.reshape(n_b, n_blk, SSM_BLOCK, SSM_GROUPS, SSM_GROUP).transpose(1, 0, 2, 3, 4)
        h0 = jnp.zeros((n_b, SSM_GROUPS, SSM_STATE), f32)

        def step(carry, u_blk):
            y_blk, hr, hi = _s5_segment(u_blk, carry[0], carry[1], abar_re, abar_im,
                                        bbar_re, bbar_im, c_re, c_im, d_skip)
            return (hr, hi), y_blk

        (h_re_new, h_im_new), ys = lax.scan(step, (h0, h0), ub)
        y_ssm = ys.transpose(1, 0, 2, 3, 4).reshape(n_b, n_t, SSM_WIDTH)
    else:
        y_ssm, h_re_new, h_im_new = _s5_segment(ug, h_re.astype(f32), h_im.astype(f32), abar_re, abar_im,
                                                bbar_re, bbar_im, c_re, c_im, d_skip)
        y_ssm = y_ssm.reshape(n_b, n_t, SSM_WIDTH)
    y_ssm = jax.nn.gelu(y_ssm)
    y_ssm = y_ssm * jax.nn.sigmoid(y_ssm @ w_glu.astype(f32) + b_glu.astype(f32))
    y_ssm = (y_ssm * jax.nn.silu(z_s.astype(f32))).astype(x.dtype)

    q = _rmsnorm(q.reshape(n_b, n_t, N_HEADS, HEAD_DIM), q_norm_g)
    k = _rmsnorm(k.reshape(n_b, n_t, N_HEADS, HEAD_DIM), k_norm_g)
    v = v.reshape(n_b, n_t, N_HEADS, HEAD_DIM)
    logf = jax.nn.log_sigmoid((f_logit + b_f).astype(f32))
    if past is None:
        cum = jnp.cumsum(logf, axis=1)
        pos = jnp.arange(n_t)
        n_qb = n_t // Q_BLOCK
        qb = q.reshape(n_b, n_qb, Q_BLOCK, N_HEADS, HEAD_DIM).transpose(1, 0, 2, 3, 4)
        cb = cum.reshape(n_b, n_qb, Q_BLOCK, N_HEADS).transpose(1, 0, 2, 3)
        pb = pos.reshape(n_qb, Q_BLOCK)
        ob = lax.map(lambda blk: _fox_attend(blk[0], blk[1], blk[2], k, v, cum, pos), (qb, cb, pb))
        o = ob.transpose(1, 0, 2, 3, 4).reshape(n_b, n_t, ATTN_WIDTH)
    else:
        ck, cv, clogf = past
        n_past = ck.shape[1]
        k_all = jnp.concatenate([ck.astype(k.dtype), k], axis=1)
        v_all = jnp.concatenate([cv.astype(v.dtype), v], axis=1)
        cum = jnp.cumsum(jnp.concatenate([clogf.astype(f32), logf], axis=1), axis=1)
        pos_k = jnp.arange(n_past + n_t)
        o = _fox_attend(q, cum[:, n_past:], pos_k[n_past:], k_all, v_all, cum, pos_k)
        o = o.reshape(n_b, n_t, ATTN_WIDTH)
    y_att = (o.astype(f32) * jax.nn.silu(z_a.astype(f32))).astype(x.dtype)

    mixed = jnp.concatenate([y_ssm, y_att], axis=-1)
    y = x + gate[:, None, :] * (mixed @ w_out)
    return y, k, v, logf, h_re_new, h_im_new


def setup_inputs(seed: int = 0) -> dict:
    key = jax.random.key(seed)
    ks = jax.random.split(key, 32)
    f32 = jnp.float32

    def nrm(k, shape, s):
        return s * jax.random.normal(k, shape, f32)

    G, N, P = SSM_GROUPS, SSM_STATE, SSM_GROUP
    x_prompt = nrm(ks[0], (BATCH, SEQ, D_MODEL), 1.0)
    x_sample = nrm(ks[1], (DEC_BATCH, DEC_SEQ, D_MODEL), 1.0)
    cache_k = nrm(ks[2], (DEPTH, DEC_BATCH, PAST_LEN, N_HEADS, HEAD_DIM), 1.0)
    cache_v = nrm(ks[3], (DEPTH, DEC_BATCH, PAST_LEN, N_HEADS, HEAD_DIM), 1.0)
    cache_logf = jax.nn.log_sigmoid(2.0 + nrm(ks[4], (DEPTH, DEC_BATCH, PAST_LEN, N_HEADS), 0.5))
    state_ssm_re = nrm(ks[5], (DEPTH, DEC_BATCH, G, N), 0.1)
    state_ssm_im = nrm(ks[6], (DEPTH, DEC_BATCH, G, N), 0.1)
    c_prompt = nrm(ks[7], (BATCH, D_MODEL), 1.0)
    c_sample = nrm(ks[8], (DEC_BATCH, D_MODEL), 1.0)
    w_ada = nrm(ks[9], (DEPTH, D_MODEL, 3 * D_MODEL), 0.5 * D_MODEL ** -0.5)
    b_ada = nrm(ks[10], (DEPTH, 3 * D_MODEL), 0.02)
    norm_g = 1.0 + nrm(ks[11], (DEPTH, D_MODEL), 0.02)
    w_in = nrm(ks[12], (DEPTH, D_MODEL, IN_WIDTH), D_MODEL ** -0.5)
    b_f = jax.random.uniform(ks[13], (DEPTH, N_HEADS), f32, 1.0, 3.0)
    q_norm_g = 1.0 + nrm(ks[14], (DEPTH, HEAD_DIM), 0.02)
    k_norm_g = 1.0 + nrm(ks[15], (DEPTH, HEAD_DIM), 0.02)
    ssm_log_dt = jax.random.uniform(ks[16], (DEPTH, G), f32, math.log(1e-3), math.log(1e-1))
    ssm_a_re = -0.5 + nrm(ks[17], (DEPTH, G, N), 0.01)
    ssm_a_im = math.pi * jnp.arange(N, dtype=f32) + nrm(ks[18], (DEPTH, G, N), 0.01)
    ssm_b_re = nrm(ks[19], (DEPTH, G, N, P), (2 * P) ** -0.5)
    ssm_b_im = nrm(ks[20], (DEPTH, G, N, P), (2 * P) ** -0.5)
    ssm_c_re = nrm(ks[21], (DEPTH, G, P, N), N ** -0.5)
    ssm_c_im = nrm(ks[22], (DEPTH, G, P, N), N ** -0.5)
    ssm_d = nrm(ks[23], (DEPTH, G, P), 1.0)
    w_glu = nrm(ks[24], (DEPTH, SSM_WIDTH, SSM_WIDTH), SSM_WIDTH ** -0.5)
    b_glu = nrm(ks[25], (DEPTH, SSM_WIDTH), 0.02)
    w_out = nrm(ks[26], (DEPTH, D_MIX, D_MODEL), D_MIX ** -0.5)
    return {'x_prompt': x_prompt, 'x_sample': x_sample, 'cache_k': cache_k, 'cache_v': cache_v,
            'cache_logf': cache_logf, 'state_ssm_re': state_ssm_re, 'state_ssm_im': state_ssm_im,
            'c_prompt': c_prompt, 'c_sample': c_sample, 'w_ada': w_ada, 'b_ada': b_ada, 'norm_g': norm_g,
            'w_in': w_in, 'b_f': b_f, 'q_norm_g': q_norm_g, 'k_norm_g': k_norm_g, 'ssm_log_dt': ssm_log_dt,
            'ssm_a_re': ssm_a_re, 'ssm_a_im': ssm_a_im, 'ssm_b_re': ssm_b_re, 'ssm_b_im': ssm_b_im,
            'ssm_c_re': ssm_c_re, 'ssm_c_im': ssm_c_im, 'ssm_d': ssm_d, 'w_glu': w_glu, 'b_glu': b_glu,
            'w_out': w_out}


def reference(x_prompt, x_sample, cache_k, cache_v, cache_logf, state_ssm_re, state_ssm_im,
              c_prompt, c_sample, w_ada, b_ada, norm_g, w_in, b_f, q_norm_g, k_norm_g,
              ssm_log_dt, ssm_a_re, ssm_a_im, ssm_b_re, ssm_b_im, ssm_c_re, ssm_c_im, ssm_d,
              w_glu, b_glu, w_out):
    xp = x_prompt
    xs = x_sample
    kp, vp, fp, rp, ip = [], [], [], [], []
    ks_, vs_, fs_, rs_, is_ = [], [], [], [], []
    for l in range(DEPTH):
        lw = (w_ada[l], b_ada[l], norm_g[l], w_in[l], b_f[l], q_norm_g[l], k_norm_g[l],
              ssm_log_dt[l], ssm_a_re[l], ssm_a_im[l], ssm_b_re[l], ssm_b_im[l],
              ssm_c_re[l], ssm_c_im[l], ssm_d[l], w_glu[l], b_glu[l], w_out[l])
        xp, k1, v1, f1, r1, i1 = _layer(xp, c_prompt, None, None, None, *lw)
        xs, k2, v2, f2, r2, i2 = _layer(xs, c_sample, state_ssm_re[l], state_ssm_im[l],
                                        (cache_k[l], cache_v[l], cache_logf[l]), *lw)
        kp.append(k1); vp.append(v1); fp.append(f1); rp.append(r1); ip.append(i1)
        ks_.append(k2); vs_.append(v2); fs_.append(f2); rs_.append(r2); is_.append(i2)
    return (xp, xs, jnp.stack(kp), jnp.stack(vp), jnp.stack(fp), jnp.stack(rp), jnp.stack(ip),
            jnp.stack(ks_), jnp.stack(vs_), jnp.stack(fs_), jnp.stack(rs_), jnp.stack(is_))
```

```python
import contextlib
import math
import numpy as np
import ml_dtypes
import concourse.bass as bass
import concourse.mybir as mybir
from concourse.bass_utils import run_bass_kernel_spmd

F32 = mybir.dt.float32
BF = mybir.dt.bfloat16
ALU = mybir.AluOpType
AF = mybir.ActivationFunctionType
AX = mybir.AxisListType

SAME_ENG_SYNC = True
D = 1024
NCORES = 8
T_P, T_S, NPAST = 8192, 64, 2048
EPS = 1e-6
PATHS = {
    "p": dict(pi=0, T=T_P, TT=128, SP=512, npast=0),
    "s": dict(pi=1, T=T_S, TT=64, SP=64, npast=NPAST),
}
SM = {}
_o = 0
for _n, _w in (("b_adaT", 16), ("norm_gT", 8), ("bf_bc", 8), ("dT", 4), ("b_gluT", 4), ("logdtT", 16),
               ("a_reT", 16), ("a_imT", 16), ("c2T", 16), ("stT_re", 16), ("stT_im", 16)):
    SM[_n] = (_o, _o + _w)
    _o += _w
NS = _o
CF_ID, CF_TRI, CF_SEL128, CF_SEL64, CF_IOTA = 0, 128, 256, 384, 512
NCF = 1024


class Ev:
    __slots__ = ("kind", "key", "val")

    def __init__(s, kind, key, val):
        s.kind, s.key, s.val = kind, key, val


class Rec:
    __slots__ = ("call",)

    def __init__(s):
        s.call = None

    def __getattr__(s, name):
        def f(*a, **k):
            s.call = (name, a, k)
            return s
        return f


class Buf:
    __slots__ = ("w", "r")

    def __init__(s):
        s.w = None
        s.r = {}


class Sched:
    ENGS = ("pe", "act", "dve", "pool", "sp")

    def __init__(s):
        s.q = {e: [] for e in s.ENGS}
        s.waited = {e: {} for e in s.ENGS}
        s.sig = {e: set() for e in s.ENGS}
        s.dma_cnt = {}
        s.dma_last = {}
        s.capture = None

    def op(s, eng, fn, reads=(), writes=(), dma=None, after=()):
        rec = Rec()
        fn(rec)
        assert rec.call is not None
        if s.capture is not None:
            s.capture.append((eng, rec.call, tuple(reads), tuple(writes), dma, tuple(after)))
            return None
        return s._sched(eng, rec.call, reads, writes, dma, after)

    def begin_capture(s):
        s.capture = []

    def end_capture(s):
        items, s.capture = s.capture, None
        return items

    def replay(s, items):
        for it in items:
            s._sched(*it)

    def _sched(s, eng, call, reads=(), writes=(), dma=None, after=()):
        deps = list(after)
        for b in reads:
            if b.w is not None:
                deps.append(b.w)
        for b in writes:
            if b.w is not None:
                deps.append(b.w)
            deps.extend(b.r.values())
        waits = []
        wd = s.waited[eng]
        for ev in deps:
            if ev.kind == "E" and ev.key == eng and dma is None and (eng == "pe" or not SAME_ENG_SYNC):
                continue
            k = (ev.kind, ev.key)
            if wd.get(k, -1) >= ev.val:
                continue
            wd[k] = ev.val
            waits.append(ev)
            if ev.kind == "E":
                s.sig[ev.key].add(ev.val)
        idx = len(s.q[eng])
        if dma is not None:
            n = s.dma_cnt.get(dma, 0) + 1
            s.dma_cnt[dma] = n
            ev = Ev("D", dma, 16 * n)
            s.dma_last[dma] = ev
        else:
            ev = Ev("E", eng, idx)
        s.q[eng].append((call, waits, dma))
        k = (ev.kind, ev.key)
        for b in reads:
            b.r[k] = ev
        for b in writes:
            b.w = ev
            b.r = {}
        return ev

    def barrier(s, engs=None):
        evs = []
        for e in s.ENGS:
            if s.q[e]:
                for idx in range(len(s.q[e]) - 1, -1, -1):
                    fn, _, dma = s.q[e][idx]
                    if fn is not None and dma is None:
                        evs.append(Ev("E", e, idx))
                        break
        evs.extend(s.dma_last.values())
        for e in (engs or s.ENGS):
            waits = []
            wd = s.waited[e]
            for ev in evs:
                if ev.kind == "E" and ev.key == e:
                    continue
                k = (ev.kind, ev.key)
                if wd.get(k, -1) >= ev.val:
                    continue
                wd[k] = ev.val
                waits.append(ev)
                if ev.kind == "E":
                    s.sig[ev.key].add(ev.val)
            s.q[e].append((None, waits, None))

    def emit(s, nc, st):
        sems = {e: st.enter_context(nc.semaphore("c_" + e)) for e in s.ENGS}
        dsems = {k: st.enter_context(nc.semaphore("d_" + k)) for k in s.dma_cnt}
        rank = {}
        for e in s.ENGS:
            rank[e] = {idx: i + 1 for i, idx in enumerate(sorted(s.sig[e]))}

        def run(eo, e):
            rk = rank[e]
            for idx, (fn, waits, dma) in enumerate(s.q[e]):
                for ev in waits:
                    if ev.kind == "E":
                        eo.wait_ge(sems[ev.key], rank[ev.key][ev.val])
                    else:
                        eo.wait_ge(dsems[ev.key], ev.val)
                if fn is None:
                    continue
                ins = getattr(eo, fn[0])(*fn[1], **fn[2])
                if dma is not None:
                    ins.then_inc(dsems[dma], 16)
                elif idx in rk:
                    ins.then_inc(sems[e], 1)

        with nc.Block() as block:
            @block.tensor
            def _(eo):
                run(eo, "pe")

            @block.scalar
            def _(eo):
                run(eo, "act")

            @block.vector
            def _(eo):
                run(eo, "dve")

            @block.gpsimd
            def _(eo):
                run(eo, "pool")

            @block.sync
            def _(eo):
                run(eo, "sp")


class Tl:
    __slots__ = ("t", "b")

    def __init__(s, t):
        s.t = t
        s.b = Buf()


def build_program(phases="0AB2C"):
    nc = bass.Bass("TRN2", target_bir_lowering=False)
    S = Sched()
    root = contextlib.ExitStack()

    def din(name, shape, dt=F32):
        return nc.dram_tensor(name, list(shape), dt, kind="ExternalInput").ap()

    def dout(name, shape):
        return nc.dram_tensor(name, list(shape), F32, kind="ExternalOutput").ap()

    def dscr(name, shape, dt=BF):
        return nc.dram_tensor(name, list(shape), dt).ap()

    w_ada = din("w_ada", [D, 3 * D]).rearrange("(k p) n -> p k n", p=128)
    w_in = din("w_in", [D, 3080]).rearrange("(k p) n -> p k n", p=128)
    w_out = din("w_out", [D, D]).rearrange("(k p) n -> p k n", p=128)
    w_glu = din("w_glu", [512, 512]).rearrange("(k p) n -> p k n", p=128)
    bcin = din("bc", [128, 2048])
    rep3 = din("rep3", [128, 3, 2048])
    BTlay = din("BTlay", [128, 2, 2048])
    CTlay = din("CTlay", [128, 2, 2048])
    cfin = din("cf", [128, NCF])
    cbin = din("cb", [128, 256], BF)
    smin = din("smalls", [128, NS])
    xin = {"p": din("x_p", [T_P, D]), "s": din("x_s", [T_S, D])}
    ckin = din("ck", [NPAST, 512])
    cvin = din("cv", [NPAST, 512])
    clfin = din("clf", [NPAST, 8])
    O = {}
    for pk, P in PATHS.items():
        O["y_" + pk] = dout("y_" + pk, [P["T"], D])
        O["k_" + pk] = dout("k_" + pk, [P["T"], 512])
        O["v_" + pk] = dout("v_" + pk, [P["T"], 512])
        O["lf_" + pk] = dout("lf_" + pk, [P["T"], 8])
        O["ssm_" + pk] = dout("ssm_" + pk, [128, 32])
    SC = {}
    for pk, P in PATHS.items():
        nk = P["npast"] + P["T"]
        SC[pk] = dict(qTa=dscr("qTa_" + pk, [8, 65, P["T"]]), kTa=dscr("kTa_" + pk, [8, 65, nk]),
                      va=dscr("va_" + pk, [nk, 8, 128]), zaG=dscr("zaG_" + pk, [512, P["T"]]),
                      yssm=dscr("yssm_" + pk, [512, P["T"]]), hT=dscr("hTs_" + pk, [1024, P["T"]]))

    tcount = [0]

    def tile(st, name, shape, dt=F32):
        tcount[0] += 1
        return Tl(st.enter_context(nc.sbuf_tensor("sb%d_%s" % (tcount[0], name), list(shape), dt)))

    PS = [Tl(root.enter_context(nc.psum_tensor("ps%d" % i, [128, 512], F32))) for i in range(8)]

    sm = tile(root, "sm", [128, NS])
    cf = tile(root, "cf", [128, NCF])
    cb = tile(root, "cb", [128, 256], BF)
    qkg = tile(root, "qkg", [128, 1024])
    gate_bc = tile(root, "gate_bc", [128, 2, 1024])
    modb = tile(root, "modb", [128, 16, 2])
    gsT = tile(root, "gsT", [128, 8, 2])
    negb8 = tile(root, "negb8", [128, 8])
    theta = tile(root, "theta", [128, 16])
    rho = tile(root, "rho", [128, 16])
    BT = tile(root, "BT", [128, 2, 2048], BF)
    CT = tile(root, "CT", [128, 3, 2048], BF)
    biasK = {"p": tile(root, "biasK_p", [128, 64, 8]), "s": tile(root, "biasK_s", [128, 17, 8])}
    ones_bf = tile(root, "ones_bf", [128, 128], BF)
    ones_f = tile(root, "ones_f", [128, 128])
    epsc = tile(root, "epsc", [128, 1])

    def smv(name):
        a, b = SM[name]
        return sm.t[:, a:b]

    ident_bf = cb.t[:, 0:128]
    maskT = cb.t[:, 128:256]
    ident_f = cf.t[:, CF_ID:CF_ID + 128]
    tri_f = cf.t[:, CF_TRI:CF_TRI + 128]
    sel128 = cf.t[:, CF_SEL128:CF_SEL128 + 128]
    sel64 = cf.t[:, CF_SEL64:CF_SEL64 + 128]
    iota_t1 = cf.t[:, CF_IOTA:CF_IOTA + 512]

    dcount = [0]

    def dname(prefix):
        dcount[0] += 1
        return "%s%d" % (prefix, dcount[0])

    TWO_PI_HI = 6.28125
    TWO_PI_LO = 2.0 * math.pi - 6.28125

    def range_sin(dst, ang, shift, tf, ti, bd, ba, btf, bti):
        off = shift + 16.0 * math.pi
        S.op("dve", lambda e: e.tensor_scalar(out=tf, in0=ang, scalar1=off, scalar2=1.0 / (2.0 * math.pi), op0=ALU.add, op1=ALU.mult),
             reads=[ba], writes=[btf])
        S.op("dve", lambda e: e.tensor_copy(out=ti, in_=tf), reads=[btf], writes=[bti])
        S.op("dve", lambda e: e.tensor_copy(out=tf, in_=ti), reads=[bti], writes=[btf])
        S.op("dve", lambda e: e.scalar_tensor_tensor(out=dst, in0=tf, scalar=-TWO_PI_HI, in1=ang, op0=ALU.mult, op1=ALU.add),
             reads=[btf, ba], writes=[bd])
        S.op("dve", lambda e: e.scalar_tensor_tensor(out=dst, in0=tf, scalar=-TWO_PI_LO, in1=dst, op0=ALU.mult, op1=ALU.add),
             reads=[btf, bd], writes=[bd])
        S.op("dve", lambda e: e.tensor_scalar(out=dst, in0=dst, scalar1=off, scalar2=None, op0=ALU.add), reads=[bd], writes=[bd])
        S.op("dve", lambda e: e.tensor_scalar(out=tf, in0=dst, scalar1=math.pi, scalar2=-2.0 * math.pi, op0=ALU.is_gt, op1=ALU.mult),
             reads=[bd], writes=[btf])
        S.op("dve", lambda e: e.tensor_tensor(out=dst, in0=dst, in1=tf, op=ALU.add), reads=[bd, btf], writes=[bd])
        S.op("dve", lambda e: e.tensor_scalar(out=dst, in0=dst, scalar1=3.14159, scalar2=-3.14159, op0=ALU.min, op1=ALU.max),
             reads=[bd], writes=[bd])
        S.op("act", lambda e: e.activation(out=dst, in_=dst, func=AF.Sin), reads=[bd], writes=[bd])

    def phase0():
        st = contextlib.ExitStack()
        bc = tile(st, "bc", [128, 2048])
        g = "ld0"
        S.op("sp", lambda e: e.dma_start(out=sm.t[:], in_=smin), writes=[sm.b], dma=g)
        S.op("sp", lambda e: e.dma_start(out=cf.t[:], in_=cfin), writes=[cf.b], dma=g)
        S.op("sp", lambda e: e.dma_start(out=cb.t[:], in_=cbin), writes=[cb.b], dma=g)
        ev = S.op("sp", lambda e: e.dma_start(out=bc.t[:], in_=bcin), writes=[bc.b], dma=g)
        for tl in (sm, cf, cb, bc):
            tl.b.w = ev
        r3 = tile(st, "r3", [128, 3, 2048])
        S.op("sp", lambda e: e.dma_start(out=r3.t[:], in_=rep3), writes=[r3.b], dma="ld_r3")
        bl = tile(st, "bl", [128, 2, 2048])
        S.op("sp", lambda e: e.dma_start(out=bl.t[:], in_=BTlay), writes=[bl.b], dma="ld_bl")
        S.op("dve", lambda e: e.memset(ones_bf.t[:], 1.0), writes=[ones_bf.b])
        S.op("dve", lambda e: e.memset(ones_f.t[:], 1.0), writes=[ones_f.b])
        S.op("dve", lambda e: e.memset(epsc.t[:], EPS), writes=[epsc.b])
        S.op("dve", lambda e: e.tensor_copy(out=qkg.t[:], in_=bc.t[:, 1024:2048]), reads=[bc.b], writes=[qkg.b])
        sc = tile(st, "sc", [128, 8, 2])
        sg = tile(st, "sg", [128, 16])
        screp = tile(st, "screp", [128, 16, 128])
        c2 = smv("c2T")
        S.op("act", lambda e: e.activation(out=sg.t[:], in_=c2, func=AF.Sigmoid), reads=[sm.b], writes=[sg.b])
        scf = sc.t[:].rearrange("p k t -> p (k t)")
        S.op("dve", lambda e: e.tensor_tensor(out=scf, in0=c2, in1=sg.t[:], op=ALU.mult), reads=[sg.b, sm.b], writes=[sc.b])
        for i in range(16):
            S.op("dve", lambda e, i=i: e.tensor_scalar(out=screp.t[:, i, :], in0=ones_f.t[:], scalar1=scf[:, i:i + 1],
                                                     scalar2=None, op0=ALU.mult),
                 reads=[sc.b, ones_f.b], writes=[screp.b])
        wa = [tile(st, "wa%d" % i, [128, 8, 512]) for i in range(2)]
        modps = PS[0]
        for pc in range(6):
            w = wa[pc % 2]
            S.op("sp", lambda e, w=w, pc=pc: e.dma_start(out=w.t[:], in_=w_ada[:, :, pc * 512:(pc + 1) * 512]),
                 writes=[w.b], dma="wa%d" % (pc % 2))
            if pc < 4:
                for jj in range(4):
                    j = pc * 4 + jj
                    for k in range(8):
                        S.op("pe", lambda e, w=w, jj=jj, j=j, k=k: e.matmul(
                            modps.t[:, 2 * j:2 * j + 2], lhsT=w.t[:, k, jj * 128:(jj + 1) * 128], rhs=sc.t[:, k, :],
                            start=(k == 0), stop=(k == 7)), reads=[w.b, sc.b], writes=[modps.b])
            else:
                half = pc - 4
                for pi in range(2):
                    ps = PS[1 + pi * 2 + half]
                    for k in range(8):
                        S.op("pe", lambda e, w=w, ps=ps, k=k, pi=pi: e.matmul(
                            ps.t[:, :], lhsT=screp.t[:, k * 2 + pi, :], rhs=w.t[:, k, :],
                            start=(k == 0), stop=(k == 7)), reads=[w.b, screp.b], writes=[ps.b])
                    S.op("dve", lambda e, ps=ps, pi=pi, half=half: e.tensor_tensor(
                        out=gate_bc.t[:, pi, half * 512:(half + 1) * 512], in0=ps.t[:, :],
                        in1=bc.t[:, half * 512:(half + 1) * 512], op=ALU.add), reads=[ps.b, bc.b], writes=[gate_bc.b])
        S.op("dve", lambda e: e.tensor_tensor(
            out=modb.t[:], in0=modps.t[:, 0:32].rearrange("p (j t) -> p j t", t=2),
            in1=smv("b_adaT").unsqueeze(2).to_broadcast([128, 16, 2]), op=ALU.add),
            reads=[modps.b, sm.b], writes=[modb.b])
        S.op("dve", lambda e: e.scalar_tensor_tensor(
            out=gsT.t[:], in0=modb.t[:, 8:16, :], scalar=1.0,
            in1=smv("norm_gT").unsqueeze(2).to_broadcast([128, 8, 2]), op0=ALU.add, op1=ALU.mult),
            reads=[modb.b, sm.b], writes=[gsT.b])
        g2 = tile(st, "g2", [128, 128])
        m2 = tile(st, "m2", [128, 2])
        S.op("dve", lambda e: e.tensor_tensor(out=g2.t[:, 0:64], in0=qkg.t[:, 0:64], in1=qkg.t[:, 0:64], op=ALU.mult),
             reads=[qkg.b], writes=[g2.b])
        S.op("dve", lambda e: e.tensor_tensor(out=g2.t[:, 64:128], in0=qkg.t[:, 512:576], in1=qkg.t[:, 512:576], op=ALU.mult),
             reads=[qkg.b], writes=[g2.b])
        S.op("dve", lambda e: e.reduce_max(out=m2.t[:, 0:1], in_=g2.t[:, 0:64], axis=AX.X), reads=[g2.b], writes=[m2.b])
        S.op("dve", lambda e: e.reduce_max(out=m2.t[:, 1:2], in_=g2.t[:, 64:128], axis=AX.X), reads=[g2.b], writes=[m2.b])
        S.op("dve", lambda e: e.tensor_tensor(out=m2.t[:, 0:1], in0=m2.t[:, 0:1], in1=m2.t[:, 1:2], op=ALU.mult),
             reads=[m2.b], writes=[m2.b])
        S.op("act", lambda e: e.activation(out=m2.t[:, 0:1], in_=m2.t[:, 0:1], func=AF.Ln), reads=[m2.b], writes=[m2.b])
        S.op("act", lambda e: e.activation(out=m2.t[:, 0:1], in_=m2.t[:, 0:1], func=AF.Exp, scale=0.5), reads=[m2.b], writes=[m2.b])
        S.op("dve", lambda e: e.tensor_scalar(out=m2.t[:, 0:1], in0=m2.t[:, 0:1], scalar1=-8.0, scalar2=None, op0=ALU.mult),
             reads=[m2.b], writes=[m2.b])
        S.op("dve", lambda e: e.tensor_copy(out=negb8.t[:], in_=m2.t[:, 0:1].to_broadcast([128, 8])),
             reads=[m2.b], writes=[negb8.b])
        dtT = tile(st, "dtT", [128, 16])
        S.op("act", lambda e: e.activation(out=dtT.t[:], in_=smv("logdtT"), func=AF.Exp), reads=[sm.b], writes=[dtT.b])
        S.op("dve", lambda e: e.tensor_tensor(out=theta.t[:], in0=smv("a_imT"), in1=dtT.t[:], op=ALU.mult),
             reads=[sm.b, dtT.b], writes=[theta.b])
        S.op("dve", lambda e: e.tensor_tensor(out=rho.t[:], in0=smv("a_reT"), in1=dtT.t[:], op=ALU.mult),
             reads=[sm.b, dtT.b], writes=[rho.b])
        S.op("act", lambda e: e.activation(out=rho.t[:], in_=rho.t[:], func=AF.Exp), reads=[rho.b], writes=[rho.b])
        tt = [tile(st, "tt%d" % i, [128, 2048]) for i in range(7)]
        ar, ai, ldt = r3.t[:, 0, :], r3.t[:, 1, :], r3.t[:, 2, :]
        dtr, mag, ang, c1, s1, den, t6 = tt
        TWO_PI = 2.0 * math.pi
        negpi = tile(st, "negpi", [128, 1])
        S.op("dve", lambda e: e.memset(negpi.t[:], -math.pi), writes=[negpi.b])
        S.op("act", lambda e: e.activation(out=dtr.t[:], in_=ldt, func=AF.Exp), reads=[r3.b], writes=[dtr.b])
        S.op("dve", lambda e: e.tensor_tensor(out=mag.t[:], in0=ar, in1=dtr.t[:], op=ALU.mult), reads=[r3.b, dtr.b], writes=[mag.b])
        S.op("act", lambda e: e.activation(out=mag.t[:], in_=mag.t[:], func=AF.Exp), reads=[mag.b], writes=[mag.b])
        S.op("dve", lambda e: e.tensor_tensor(out=ang.t[:], in0=ai, in1=dtr.t[:], op=ALU.mult), reads=[r3.b, dtr.b], writes=[ang.b])
        tiI = tile(st, "tiI", [128, 2048], mybir.dt.int32)
        range_sin(s1.t[:], ang.t[:], 0.0, den.t[:], tiI.t[:], s1.b, ang.b, den.b, tiI.b)
        range_sin(c1.t[:], ang.t[:], 0.5 * math.pi, den.t[:], tiI.t[:], c1.b, ang.b, den.b, tiI.b)
        S.op("dve", lambda e: e.tensor_tensor(out=c1.t[:], in0=c1.t[:], in1=mag.t[:], op=ALU.mult), reads=[c1.b, mag.b], writes=[c1.b])
        S.op("dve", lambda e: e.tensor_scalar(out=c1.t[:], in0=c1.t[:], scalar1=-1.0, scalar2=None, op0=ALU.add), reads=[c1.b], writes=[c1.b])
        S.op("dve", lambda e: e.tensor_tensor(out=s1.t[:], in0=s1.t[:], in1=mag.t[:], op=ALU.mult), reads=[s1.b, mag.b], writes=[s1.b])
        S.op("dve", lambda e: e.tensor_tensor(out=den.t[:], in0=ar, in1=ar, op=ALU.mult), reads=[r3.b], writes=[den.b])
        S.op("dve", lambda e: e.tensor_tensor(out=t6.t[:], in0=ai, in1=ai, op=ALU.mult), reads=[r3.b], writes=[t6.b])
        S.op("dve", lambda e: e.tensor_tensor(out=den.t[:], in0=den.t[:], in1=t6.t[:], op=ALU.add), reads=[den.b, t6.b], writes=[den.b])
        S.op("dve", lambda e: e.reciprocal(out=den.t[:], in_=den.t[:]), reads=[den.b], writes=[den.b])
        S.op("dve", lambda e: e.tensor_tensor(out=mag.t[:], in0=c1.t[:], in1=ar, op=ALU.mult), reads=[c1.b, r3.b], writes=[mag.b])
        S.op("dve", lambda e: e.tensor_tensor(out=t6.t[:], in0=s1.t[:], in1=ai, op=ALU.mult), reads=[s1.b, r3.b], writes=[t6.b])
        S.op("dve", lambda e: e.tensor_tensor(out=mag.t[:], in0=mag.t[:], in1=t6.t[:], op=ALU.add), reads=[mag.b, t6.b], writes=[mag.b])
        S.op("dve", lambda e: e.tensor_tensor(out=mag.t[:], in0=mag.t[:], in1=den.t[:], op=ALU.mult), reads=[mag.b, den.b], writes=[mag.b])
        S.op("dve", lambda e: e.tensor_tensor(out=ang.t[:], in0=s1.t[:], in1=ar, op=ALU.mult), reads=[s1.b, r3.b], writes=[ang.b])
        S.op("dve", lambda e: e.tensor_tensor(out=t6.t[:], in0=c1.t[:], in1=ai, op=ALU.mult), reads=[c1.b, r3.b], writes=[t6.b])
        S.op("dve", lambda e: e.tensor_tensor(out=ang.t[:], in0=ang.t[:], in1=t6.t[:], op=ALU.subtract), reads=[ang.b, t6.b], writes=[ang.b])
        S.op("dve", lambda e: e.tensor_tensor(out=ang.t[:], in0=ang.t[:], in1=den.t[:], op=ALU.mult), reads=[ang.b, den.b], writes=[ang.b])
        qre, qim = mag, ang
        S.op("dve", lambda e: e.tensor_tensor(out=c1.t[:], in0=qre.t[:], in1=bl.t[:, 0, :], op=ALU.mult), reads=[qre.b, bl.b], writes=[c1.b])
        S.op("dve", lambda e: e.tensor_tensor(out=s1.t[:], in0=qim.t[:], in1=bl.t[:, 1, :], op=ALU.mult), reads=[qim.b, bl.b], writes=[s1.b])
        S.op("dve", lambda e: e.tensor_tensor(out=BT.t[:, 0, :], in0=c1.t[:], in1=s1.t[:], op=ALU.subtract), reads=[c1.b, s1.b], writes=[BT.b])
        S.op("dve", lambda e: e.tensor_tensor(out=c1.t[:], in0=qre.t[:], in1=bl.t[:, 1, :], op=ALU.mult), reads=[qre.b, bl.b], writes=[c1.b])
        S.op("dve", lambda e: e.tensor_tensor(out=s1.t[:], in0=qim.t[:], in1=bl.t[:, 0, :], op=ALU.mult), reads=[qim.b, bl.b], writes=[s1.b])
        S.op("dve", lambda e: e.tensor_tensor(out=BT.t[:, 1, :], in0=c1.t[:], in1=s1.t[:], op=ALU.add), reads=[c1.b, s1.b], writes=[BT.b])
        S.op("sp", lambda e: e.dma_start(out=bl.t[:], in_=CTlay), writes=[bl.b], dma="ld_bl")
        S.op("act", lambda e: e.copy(out=CT.t[:, 0, :], in_=bl.t[:, 0, :]), reads=[bl.b], writes=[CT.b])
        S.op("act", lambda e: e.mul(out=CT.t[:, 1, :], in_=bl.t[:, 1, :], mul=-1.0), reads=[bl.b], writes=[CT.b])
        S.op("act", lambda e: e.mul(out=CT.t[:, 2, :], in_=bl.t[:, 0, :], mul=-1.0), reads=[bl.b], writes=[CT.b])
        S.barrier()
        st.close()

    TAB = {}

    def alloc_tables(stk):
        TAB["cosT"] = tile(stk, "cosT", [128, 16, 512], BF)
        TAB["sinT"] = tile(stk, "sinT", [128, 16, 512], BF)
        TAB["lastc"] = {"p": tile(stk, "lastc_p", [128, 3, 16]), "s": tile(stk, "lastc_s", [128, 3, 16])}

    def table_ops(st3):
        cosT, sinT, lastc = TAB["cosT"], TAB["sinT"], TAB["lastc"]
        angT = tile(st3, "angT", [128, 8, 512])
        tfT = tile(st3, "tfT", [128, 8, 512])
        tiT = tile(st3, "tiT", [128, 8, 512], mybir.dt.int32)
        d32 = tile(st3, "d32", [128, 8, 512])
        fl = lambda a_: a_.rearrange("p j t -> p (j t)")
        S.begin_capture()
        for hf in range(2):
            for jj in range(8):
                j = hf * 8 + jj
                S.op("dve", lambda e, j=j, jj=jj: e.tensor_scalar(out=angT.t[:, jj, :], in0=iota_t1, scalar1=theta.t[:, j:j + 1], scalar2=None,
                                                              op0=ALU.mult), reads=[cf.b, theta.b], writes=[angT.b])
            for which, tab, shift in ((1, sinT, 0.0), (0, cosT, 0.5 * math.pi)):
                range_sin(fl(d32.t[:]), fl(angT.t[:]), shift, fl(tfT.t[:]), fl(tiT.t[:]), d32.b, angT.b, tfT.b, tiT.b)
                S.op("act", lambda e, tab=tab, hf=hf: e.copy(out=tab.t[:, hf * 8:(hf + 1) * 8, :], in_=d32.t[:]), reads=[d32.b], writes=[tab.b])
                for pk, Lc in (("p", 512), ("s", 64)):
                    S.op("dve", lambda e, pk=pk, Lc=Lc, which=which, hf=hf: e.tensor_copy(
                        out=lastc[pk].t[:, which, hf * 8:(hf + 1) * 8], in_=d32.t[:, :, Lc - 1]), reads=[d32.b], writes=[lastc[pk].b])
        for pk in ("p", "s"):
            S.op("dve", lambda e, pk=pk: e.tensor_scalar(out=lastc[pk].t[:, 2, :], in0=lastc[pk].t[:, 1, :], scalar1=-1.0, scalar2=None,
                                                       op0=ALU.mult), reads=[lastc[pk].b], writes=[lastc[pk].b])
        return S.end_capture()

    def phaseN():
        st = contextlib.ExitStack()
        NX = 4
        xt = [tile(st, "nx%d" % i, [128, D]) for i in range(NX)]
        sqj = tile(st, "nsq", [128, D], BF)
        sst = [tile(st, "nss%d" % i, [128, 2]) for i in range(NX)]
        xst = [tile(st, "nxs%d" % i, [128, D], BF) for i in range(NX)]
        hTn = [tile(st, "n_hT%d" % i, [128, 8, 512], BF) for i in range(2)]
        tps = [PS[6], PS[7]]
        tiles = []
        for pk in ("s", "p"):
            P = PATHS[pk]
            for s0 in range(0, P["T"], P["SP"]):
                for ti in range(P["SP"] // P["TT"]):
                    tiles.append((pk, s0, ti))
        spans = {}
        for (pk, s0, ti) in tiles:
            spans.setdefault((pk, s0), len(spans))

        def load(i):
            if i >= len(tiles):
                return
            pk, s0, ti = tiles[i]
            TT = PATHS[pk]["TT"]
            t0 = s0 + ti * TT
            x = xt[i % NX]
            S.op("sp", lambda e: e.dma_start(out=x.t[0:TT, :], in_=xin[pk][t0:t0 + TT, :]), writes=[x.b], dma="nx%d" % (i % NX))

        def chain(i):
            if i >= len(tiles):
                return
            pk, s0, ti = tiles[i]
            TT = PATHS[pk]["TT"]
            x, xs, ss = xt[i % NX], xst[i % NX], sst[i % NX]
            S.op("act", lambda e: e.activation(out=sqj.t[0:TT, :], in_=x.t[0:TT, :], func=AF.Square, accum_out=ss.t[0:TT, 0:1]),
                 reads=[x.b], writes=[sqj.b, ss.b])
            S.op("act", lambda e: e.activation(out=ss.t[0:TT, 1:2], in_=ss.t[0:TT, 0:1], func=AF.Ln, scale=1.0 / D, bias=epsc.t[0:TT, 0:1]),
                 reads=[ss.b, epsc.b], writes=[ss.b])
            S.op("act", lambda e: e.activation(out=ss.t[0:TT, 1:2], in_=ss.t[0:TT, 1:2], func=AF.Exp, scale=-0.5), reads=[ss.b], writes=[ss.b])
            S.op("dve", lambda e: e.tensor_scalar(out=xs.t[0:TT, :], in0=x.t[0:TT, :], scalar1=ss.t[0:TT, 1:2], scalar2=None, op0=ALU.mult),
                 reads=[x.b, ss.b], writes=[xs.b])

        def transp(i):
            pk, s0, ti = tiles[i]
            P = PATHS[pk]
            TT, pi, SPn = P["TT"], P["pi"], P["SP"]
            xs = xst[i % NX]
            tp = tps[i % 2]
            tpb = tp.t[:, :].bitcast(BF)
            hT = hTn[spans[(pk, s0)] % 2]
            for k in range(8):
                S.op("pe", lambda e, k=k: e.transpose(out=tpb[:, k * 128:k * 128 + TT], in_=xs.t[0:TT, k * 128:(k + 1) * 128],
                                                      identity=ident_bf[0:TT, 0:TT]), reads=[xs.b, cb.b], writes=[tp.b])
            for k in range(8):
                if k % 2 == 0:
                    S.op("act", lambda e, k=k: e.activation(
                        out=hT.t[:, k, ti * TT:(ti + 1) * TT], in_=tpb[:, k * 128:k * 128 + TT], func=AF.Identity,
                        scale=gsT.t[:, k, pi:pi + 1], bias=modb.t[:, k, pi:pi + 1]), reads=[tp.b, gsT.b, modb.b], writes=[hT.b])
                else:
                    S.op("dve", lambda e, k=k: e.tensor_scalar(
                        out=hT.t[:, k, ti * TT:(ti + 1) * TT], in0=tpb[:, k * 128:k * 128 + TT],
                        scalar1=gsT.t[:, k, pi:pi + 1], scalar2=modb.t[:, k, pi:pi + 1], op0=ALU.mult, op1=ALU.add),
                        reads=[tp.b, gsT.b, modb.b], writes=[hT.b])
            if ti == SPn // TT - 1:
                S.op("pool", lambda e: e.dma_start(out=SC[pk]["hT"][:, s0:s0 + SPn].rearrange("(k p) t -> p k t", p=128),
                                                   in_=hT.t[:, :, 0:SPn]), reads=[hT.b], dma="hTsp%d" % (spans[(pk, s0)] % 2))

        tab_l = table_ops(st) if TAB else []
        per = (len(tab_l) + len(tiles) - 1) // len(tiles)
        for i in range(NX - 1):
            load(i)
        chain(0)
        for i in range(len(tiles)):
            load(i + NX - 1)
            chain(i + 1)
            transp(i)
            S.replay(tab_l[i * per:(i + 1) * per])
        S.replay(tab_l[len(tiles) * per:])
        S.barrier()
        st.close()

    def load_w_bf(wt, cols0, ncols, stage, wname):
        c = 0
        i = 0
        while c < ncols:
            n = min(512, ncols - c)
            sg_ = stage[i % 2]
            S.op("sp", lambda e, sg_=sg_, c=c, n=n: e.dma_start(out=sg_.t[:, :, 0:n], in_=w_in[:, :, cols0 + c:cols0 + c + n]),
                 writes=[sg_.b], dma="%s%d" % (wname, i % 2))
            eng = "act" if i % 2 == 0 else "dve"
            if eng == "act":
                S.op("act", lambda e, sg_=sg_, c=c, n=n: e.copy(out=wt.t[:, :, c:c + n], in_=sg_.t[:, :, 0:n]),
                     reads=[sg_.b], writes=[wt.b])
            else:
                S.op("dve", lambda e, sg_=sg_, c=c, n=n: e.tensor_copy(out=wt.t[:, :, c:c + n], in_=sg_.t[:, :, 0:n]),
                     reads=[sg_.b], writes=[wt.b])
            c += n
            i += 1
        return wt

    def phaseA():
        st = contextlib.ExitStack()
        wA = tile(st, "wA", [128, 8, 2056], BF)
        st2 = contextlib.ExitStack()
        stage = [tile(st2, "wstg%d" % i, [128, 8, 512]) for i in range(2)]
        load_w_bf(wA, 1024, 2056, stage, "wsA")
        S.barrier()
        st2.close()
        hTa = [tile(st, "a_hT%d" % i, [128, 8, 512], BF) for i in range(2)]
        NSET = 2
        WS = []
        for i in range(NSET):
            d = {}
            for nm in ("qf", "kf", "sq", "tmpn", "ko", "vo"):
                d[nm] = tile(st, "a_%s%d" % (nm, i), [128, 512])
            d["st8"] = tile(st, "a_st8%d" % i, [128, 4, 8])
            d["lft"] = tile(st, "a_lft%d" % i, [128, 8])
            d["lf"] = tile(st, "a_lf%d" % i, [128, 8])
            d["cum"] = tile(st, "a_cum%d" % i, [128, 8])
            d["q_aug"] = tile(st, "a_qaug%d" % i, [128, 8, 65], BF)
            d["k_aug"] = tile(st, "a_kaug%d" % i, [128, 8, 65], BF)
            d["v_aug"] = tile(st, "a_vaug%d" % i, [128, 8, 128], BF)
            d["i"] = i
            WS.append(d)
            S.op("dve", lambda e, d=d: e.memset(d["k_aug"].t[:, :, 64:65], 1.0), writes=[d["k_aug"].b])
            va = d["v_aug"]
            S.op("dve", lambda e, va=va: e.memset(va.t[:], 0.0), writes=[va.b])
            S.op("dve", lambda e, va=va: e.memset(va.t[:, 0::2, 64:65], 1.0), writes=[va.b])
            S.op("dve", lambda e, va=va: e.memset(va.t[:, 1::2, 0:1], 1.0), writes=[va.b])
        qst = [tile(st, "a_qst%d" % i, [65, 8, 512], BF) for i in range(2)]
        kst = [tile(st, "a_kst%d" % i, [65, 8, 512], BF) for i in range(2)]
        zsg = tile(st, "a_zsg", [128, 512])
        zaS = [tile(st, "a_zaS%d" % i, [128, 4, 512], BF) for i in range(2)]
        qps, kps, vps, fps, qTps, kTps = PS[0], PS[1], PS[2], PS[3], PS[4], PS[5]
        qTb = qTps.t[:, :].bitcast(BF)
        kTb = kTps.t[:, :].bitcast(BF)
        cnt = [0]
        state = dict(cum_prev=None, cum_TT=None)

        def h3(ap_):
            return ap_.rearrange("p (h d) -> p h d", d=64)

        def evac_tile(TT, W):
            S.op("act", lambda e: e.copy(out=W["qf"].t[0:TT, :], in_=qps.t[0:TT, :]), reads=[qps.b], writes=[W["qf"].b])
            S.op("act", lambda e: e.copy(out=W["kf"].t[0:TT, :], in_=kps.t[0:TT, :]), reads=[kps.b], writes=[W["kf"].b])
            S.op("act", lambda e: e.copy(out=W["vo"].t[0:TT, :], in_=vps.t[0:TT, :]), reads=[vps.b], writes=[W["vo"].b])
            S.op("dve", lambda e: e.tensor_tensor(out=W["lft"].t[0:TT, :], in0=fps.t[0:TT, 0:8], in1=smv("bf_bc")[0:TT, :], op=ALU.add),
                 reads=[fps.b, sm.b], writes=[W["lft"].b])

        def kv_tile(pk, blk, TT, tok0, col0, from_cache, kstg, qstg, W, part="abc"):
            P = PATHS[pk]
            c = W["i"]
            ko, vo, lf_, cm, va = W["ko"], W["vo"], W["lf"], W["cum"], W["v_aug"]
            q_aug, k_aug, sq, st8, tmpn, lft = W["q_aug"], W["k_aug"], W["sq"], W["st8"], W["tmpn"], W["lft"]
            if "a" not in part:
                pass
            elif from_cache:
                ck_, cv_ = W["kf"], W["vo"]
                S.op("sp", lambda e: e.dma_start(out=ck_.t[0:TT, :], in_=ckin[tok0:tok0 + TT, :]), writes=[ck_.b], dma="ck%d" % c)
                S.op("sp", lambda e: e.dma_start(out=cv_.t[0:TT, :], in_=cvin[tok0:tok0 + TT, :]), writes=[cv_.b], dma="cv%d" % c)
                S.op("sp", lambda e: e.dma_start(out=lf_.t[0:TT, :], in_=clfin[tok0:tok0 + TT, :]), writes=[lf_.b], dma="clf%d" % c)
                S.op("act", lambda e: e.copy(out=k_aug.t[0:TT, :, 0:64], in_=h3(ck_.t[0:TT, :])), reads=[ck_.b], writes=[k_aug.b])
            else:
                nq = tok0 - P["npast"]
                for which, src in ((0, W["qf"]), (1, W["kf"])):
                    S.op("act", lambda e, src=src: e.activation(out=sq.t[0:TT, :], in_=src.t[0:TT, :], func=AF.Square),
                         reads=[src.b], writes=[sq.b])
                    S.op("dve", lambda e, which=which: e.reduce_sum(out=st8.t[0:TT, 2 * which, :], in_=h3(sq.t[0:TT, :]), axis=AX.X),
                         reads=[sq.b], writes=[st8.b])
                    S.op("act", lambda e, which=which: e.activation(
                        out=st8.t[0:TT, 2 * which + 1, :], in_=st8.t[0:TT, 2 * which, :], func=AF.Ln, scale=1.0 / 64, bias=epsc.t[0:TT, 0:1]),
                        reads=[st8.b, epsc.b], writes=[st8.b])
                    S.op("act", lambda e, which=which: e.activation(
                        out=st8.t[0:TT, 2 * which + 1, :], in_=st8.t[0:TT, 2 * which + 1, :], func=AF.Exp, scale=-0.5),
                        reads=[st8.b], writes=[st8.b])
                    S.op("dve", lambda e, which=which, src=src: e.tensor_tensor(
                        out=h3(tmpn.t[0:TT, :]), in0=h3(src.t[0:TT, :]),
                        in1=st8.t[0:TT, 2 * which + 1, :].unsqueeze(2).to_broadcast([TT, 8, 64]), op=ALU.mult),
                        reads=[src.b, st8.b], writes=[tmpn.b])
                    if which == 0:
                        S.op("dve", lambda e: e.tensor_tensor(out=q_aug.t[0:TT, :, 0:64], in0=h3(tmpn.t[0:TT, :]),
                                                              in1=h3(qkg.t[0:TT, 0:512]), op=ALU.mult),
                             reads=[tmpn.b, qkg.b], writes=[q_aug.b])
                    else:
                        S.op("dve", lambda e: e.tensor_tensor(out=ko.t[0:TT, :], in0=tmpn.t[0:TT, :], in1=qkg.t[0:TT, 512:1024],
                                                              op=ALU.mult), reads=[tmpn.b, qkg.b], writes=[ko.b])
                        S.op("sp", lambda e: e.dma_start(out=O["k_" + pk][nq:nq + TT, :], in_=ko.t[0:TT, :]),
                             reads=[ko.b], dma="ko%d" % c)
                        S.op("act", lambda e: e.copy(out=k_aug.t[0:TT, :, 0:64], in_=h3(ko.t[0:TT, :])),
                             reads=[ko.b], writes=[k_aug.b])
                S.op("sp", lambda e: e.dma_start(out=O["v_" + pk][nq:nq + TT, :], in_=vo.t[0:TT, :]),
                     reads=[vo.b], dma="vo%d" % c)
                S.op("act", lambda e: e.activation(out=lft.t[0:TT, :], in_=lft.t[0:TT, :], func=AF.Exp, scale=-1.0),
                     reads=[lft.b], writes=[lft.b])
                S.op("act", lambda e: e.activation(out=lft.t[0:TT, :], in_=lft.t[0:TT, :], func=AF.Ln, bias=ones_f.t[0:TT, 0:1]),
                     reads=[lft.b, ones_f.b], writes=[lft.b])
                S.op("dve", lambda e: e.tensor_scalar(out=lf_.t[0:TT, :], in0=lft.t[0:TT, :], scalar1=-1.0, scalar2=None, op0=ALU.mult),
                     reads=[lft.b], writes=[lf_.b])
                S.op("sp", lambda e: e.dma_start(out=O["lf_" + pk][nq:nq + TT, :], in_=lf_.t[0:TT, :]),
                     reads=[lf_.b], dma="lfo%d" % c)
            v4 = h3(vo.t[0:TT, :])
            if "a" in part:
                S.op("dve", lambda e: e.tensor_copy(out=va.t[0:TT, 0::2, 0:64], in_=v4[:, 0::2, :]), reads=[vo.b], writes=[va.b])
                S.op("dve", lambda e: e.tensor_copy(out=va.t[0:TT, 1::2, 64:128], in_=v4[:, 1::2, :]), reads=[vo.b], writes=[va.b])
                S.op("pool", lambda e: e.dma_start(out=SC[pk]["va"][tok0:tok0 + TT, :, :], in_=va.t[0:TT, :, :]),
                     reads=[va.b], dma="vao%d" % c)
            if "b" in part:
                kv_tile_b(pk, blk, TT, from_cache, W)
            if "c" in part:
                kv_tile_c(TT, col0, from_cache, kstg, qstg, W)

        def kv_tile_b(pk, blk, TT, from_cache, W):
            lf_, cm, q_aug = W["lf"], W["cum"], W["q_aug"]
            cps = fps.t[0:TT, 16:24]
            prev = state["cum_prev"]
            S.op("pe", lambda e: e.matmul(cps, lhsT=tri_f[0:TT, 0:TT], rhs=lf_.t[0:TT, :], start=True, stop=(prev is None)),
                 reads=[lf_.b, cf.b], writes=[fps.b])
            if prev is not None:
                pTT = state["cum_TT"]
                sel = sel128 if pTT == 128 else sel64
                S.op("pe", lambda e: e.matmul(cps, lhsT=sel[0:pTT, 0:TT], rhs=prev.t[0:pTT, :], start=False, stop=True),
                     reads=[prev.b, cf.b], writes=[fps.b])
            S.op("act", lambda e: e.copy(out=cm.t[0:TT, :], in_=cps), reads=[fps.b], writes=[cm.b])
            state["cum_prev"], state["cum_TT"] = cm, TT
            S.op("dve", lambda e: e.scalar_tensor_tensor(out=biasK[pk].t[0:TT, blk, :], in0=cm.t[0:TT, :], scalar=-1.0,
                                                         in1=negb8.t[0:TT, :], op0=ALU.mult, op1=ALU.add),
                 reads=[cm.b, negb8.b], writes=[biasK[pk].b])
            if not from_cache:
                S.op("dve", lambda e: e.tensor_scalar(out=q_aug.t[0:TT, :, 64:65], in0=cm.t[0:TT, :].unsqueeze(2), scalar1=8.0,
                                                      scalar2=None, op0=ALU.mult), reads=[cm.b], writes=[q_aug.b])

        def kv_tile_c(TT, col0, from_cache, kstg, qstg, W):
            q_aug, k_aug = W["q_aug"], W["k_aug"]
            for h in range(8):
                S.op("pe", lambda e, h=h: e.transpose(out=kTb[0:65, h * 128:h * 128 + TT], in_=k_aug.t[0:TT, h, :],
                                                      identity=ident_bf[0:TT, 0:TT]), reads=[k_aug.b, cb.b], writes=[kTps.b])
            S.op("act", lambda e: e.copy(out=kstg.t[0:65, :, col0:col0 + TT],
                                         in_=kTb[0:65, :].rearrange("p (h t) -> p h t", t=128)[:, :, 0:TT]),
                 reads=[kTps.b], writes=[kstg.b])
            if not from_cache:
                for h in range(8):
                    S.op("pe", lambda e, h=h: e.transpose(out=qTb[0:65, h * 128:h * 128 + TT], in_=q_aug.t[0:TT, h, :],
                                                          identity=ident_bf[0:TT, 0:TT]), reads=[q_aug.b, cb.b], writes=[qTps.b])
                S.op("dve", lambda e: e.tensor_copy(out=qstg.t[0:65, :, col0:col0 + TT],
                                                    in_=qTb[0:65, :].rearrange("p (h t) -> p h t", t=128)[:, :, 0:TT]),
                     reads=[qTps.b], writes=[qstg.b])

        def load_hT(pk_, s0_, hT_):
            n_ = PATHS[pk_]["SP"]
            S.op("sp", lambda e: e.dma_start(out=hT_.t[:, :, 0:n_], in_=SC[pk_]["hT"][:, s0_:s0_ + n_].rearrange("(k p) t -> p k t", p=128)),
                 writes=[hT_.b], dma="ahT%d" % (0 if hT_ is hTa[0] else 1))

        spc = [0]
        LAG = 2
        pending = []
        pend_b = []

        def flush(n_keep):
            while len(pending) > n_keep:
                S.replay(pending.pop(0))

        for pk in ("s", "p"):
            P = PATHS[pk]
            TT, SPn, T, npast = P["TT"], P["SP"], P["T"], P["npast"]
            state["cum_prev"] = None
            for s0 in range(0, npast, 512):
                sc_ = spc[0]
                spc[0] += 1
                kstg = kst[sc_ % 2]
                for ti in range(4):
                    W = WS[cnt[0] % NSET]
                    cnt[0] += 1
                    kv_tile(pk, (s0 + ti * 128) // 128, 128, s0 + ti * 128, ti * 128, True, kstg, None, W)
                S.op("pool", lambda e, kstg=kstg, s0=s0: e.dma_start(
                    out=SC[pk]["kTa"][:, :, s0:s0 + 512].rearrange("h r t -> r h t"), in_=kstg.t[0:65, :, 0:512]),
                    reads=[kstg.b], dma="kst%d" % (sc_ % 2))
            span_list = list(range(0, T, SPn))
            load_hT(pk, span_list[0], hTa[spc[0] % 2])
            for sidx, s0 in enumerate(span_list):
                sc_ = spc[0]
                spc[0] += 1
                kstg, qstg, zas = kst[sc_ % 2], qst[sc_ % 2], zaS[sc_ % 2]
                hT = hTa[sc_ % 2]
                if sidx + 1 < len(span_list):
                    load_hT(pk, span_list[sidx + 1], hTa[(sc_ + 1) % 2])
                ntile = SPn // TT
                for ti in range(ntile):
                    cols = slice(ti * TT, (ti + 1) * TT)
                    W = WS[cnt[0] % NSET]
                    cnt[0] += 1
                    for (ps, c0, n) in ((qps, 0, 512), (kps, 512, 512), (vps, 1024, 512), (fps, 2048, 8)):
                        for k in range(8):
                            S.op("pe", lambda e, ps=ps, c0=c0, n=n, k=k, cols=cols: e.matmul(
                                ps.t[0:TT, 0:n], lhsT=hT.t[:, k, cols], rhs=wA.t[:, k, c0:c0 + n],
                                start=(k == 0), stop=(k == 7)), reads=[hT.b, wA.b], writes=[ps.b])
                    evac_tile(TT, W)
                    tok0 = npast + s0 + ti * TT
                    if pend_b:
                        S.replay(pend_b.pop(0))
                    kv_tile(pk, tok0 // 128, TT, tok0, ti * TT, False, kstg, qstg, W, part="a")
                    flush(0)
                    S.begin_capture()
                    kv_tile(pk, tok0 // 128, TT, tok0, ti * TT, False, kstg, qstg, W, part="b")
                    pend_b.append(S.end_capture())
                    S.begin_capture()
                    kv_tile(pk, tok0 // 128, TT, tok0, ti * TT, False, kstg, qstg, W, part="c")
                    lst = S.end_capture()
                    if ti == ntile - 1:
                        S.begin_capture()
                        k0 = npast + s0
                        S.op("pool", lambda e, kstg=kstg, k0=k0: e.dma_start(
                            out=SC[pk]["kTa"][:, :, k0:k0 + SPn].rearrange("h r t -> r h t"), in_=kstg.t[0:65, :, 0:SPn]),
                            reads=[kstg.b], dma="kst%d" % (sc_ % 2))
                        S.op("pool", lambda e, qstg=qstg, s0=s0: e.dma_start(
                            out=SC[pk]["qTa"][:, :, s0:s0 + SPn].rearrange("h r t -> r h t"), in_=qstg.t[0:65, :, 0:SPn]),
                            reads=[qstg.b], dma="qst%d" % (sc_ % 2))
                        lst = lst + S.end_capture()
                    pending.append(lst)
                for cch in range(4):
                    zps = PS[6]
                    for k in range(8):
                        S.op("pe", lambda e, cch=cch, k=k: e.matmul(
                            zps.t[:, 0:SPn], lhsT=wA.t[:, k, 1536 + cch * 128:1536 + (cch + 1) * 128], rhs=hT.t[:, k, 0:SPn],
                            start=(k == 0), stop=(k == 7)), reads=[hT.b, wA.b], writes=[zps.b])
                    S.op("act", lambda e, cch=cch, zas=zas: e.activation(out=zas.t[:, cch, 0:SPn], in_=zps.t[:, 0:SPn], func=AF.Silu),
                         reads=[zps.b], writes=[zas.b])
                S.op("pool", lambda e, zas=zas, s0=s0: e.dma_start(
                    out=SC[pk]["zaG"][:, s0:s0 + SPn].rearrange("(c p) t -> p c t", p=128), in_=zas.t[:, :, 0:SPn]),
                    reads=[zas.b], dma="zas%d" % (sc_ % 2))
            while pend_b:
                S.replay(pend_b.pop(0))
            flush(0)
        S.barrier()
        st.close()

    def phaseB():
        st = contextlib.ExitStack()
        wB = tile(st, "wB", [128, 8, 1024], BF)
        wg = tile(st, "wg", [128, 4, 512], BF)
        cosT, sinT, lastc = TAB["cosT"], TAB["sinT"], TAB["lastc"]
        st2 = contextlib.ExitStack()
        stage = [tile(st2, "wstgB%d" % i, [128, 8, 512]) for i in range(2)]
        wgs = tile(st2, "wgs", [128, 4, 512])
        load_w_bf(wB, 0, 1024, stage, "wsB")
        S.op("sp", lambda e: e.dma_start(out=wgs.t[:], in_=w_glu), writes=[wgs.b], dma="ld_wg")
        S.op("act", lambda e: e.copy(out=wg.t[:], in_=wgs.t[:]), reads=[wgs.b], writes=[wg.b])
        S.barrier()
        st2.close()
        hTb = [tile(st, "b_hT%d" % i, [128, 8, 512], BF) for i in range(2)]
        uTs = [tile(st, "uT%d" % i, [128, 4, 512], BF) for i in range(2)]
        zsGs = [tile(st, "zsG%d" % i, [128, 4, 512], BF) for i in range(2)]
        zsg = tile(st, "b_zsg", [128, 512])
        bX = [[{nm: tile(st, "b_%s_%d%d" % (nm, par, tl_), [128, 512], BF) for nm in ("brb", "bib")} for tl_ in range(2)] for par in range(2)]
        NSET = 2
        TS = []
        for i in range(NSET):
            d = {}
            for nm in ("brb", "bib", "t1", "t2", "t3", "t4", "wre", "wim", "zrb", "zib", "p1", "p2", "p3", "p4"):
                d[nm] = tile(st, "b_%s%d" % (nm, i), [128, 512], BF)
            for nm in ("zre", "zim"):
                d[nm] = tile(st, "b_%s%d" % (nm, i), [128, 512])
            d["ctmp"] = tile(st, "b_ctmp%d" % i, [128, 2])
            d["bre_ps"], d["bim_ps"] = PS[1 + 2 * i], PS[2 + 2 * i]
            TS.append(d)
        hst = tile(st, "b_hst", [128, 32])
        hb = [Buf() for _ in range(16)]
        yT = tile(st, "b_yT", [128, 512])
        x2 = tile(st, "b_x2", [128, 512])
        gTs = [tile(st, "b_gT%d" % i, [128, 4, 512], BF) for i in range(2)]
        carry_bg = []
        sgl = tile(st, "b_sgl", [128, 512])
        y2 = tile(st, "b_y2", [128, 512])
        yS = [tile(st, "b_yS%d" % i, [128, 4, 512], BF) for i in range(2)]
        ups, yps, gps = PS[0], PS[5], PS[6]
        spc = [0]

        def tile_ops(pk, L, j, cq, jj, d, uT):
            cs, sn = cosT.t[:, j, 0:L], sinT.t[:, j, 0:L]
            lc = lastc[pk]
            ops_ = []
            A = ops_.append
            A(lambda: S.op("pe", lambda e: e.matmul(d["bre_ps"].t[:, 0:L], lhsT=BT.t[:, 0, j * 128:(j + 1) * 128], rhs=uT.t[:, cq, 0:L],
                                                    start=True, stop=True), reads=[BT.b, uT.b], writes=[d["bre_ps"].b]))
            A(lambda: S.op("pe", lambda e: e.matmul(d["bim_ps"].t[:, 0:L], lhsT=BT.t[:, 1, j * 128:(j + 1) * 128], rhs=uT.t[:, cq, 0:L],
                                                    start=True, stop=True), reads=[BT.b, uT.b], writes=[d["bim_ps"].b]))
            A(lambda: S.op("act", lambda e: e.copy(out=d["brb"].t[:, 0:L], in_=d["bre_ps"].t[:, 0:L]), reads=[d["bre_ps"].b], writes=[d["brb"].b]))
            A(lambda: S.op("act", lambda e: e.copy(out=d["bib"].t[:, 0:L], in_=d["bim_ps"].t[:, 0:L]), reads=[d["bim_ps"].b], writes=[d["bib"].b]))

            def tt(o, a_, b_, op, tab=None, eng="dve"):
                rd = [d[a_].b] + ([d[b_].b] if isinstance(b_, str) else [tab.b])
                bb = d[b_].t[:, 0:L] if isinstance(b_, str) else b_
                return lambda: S.op(eng, lambda e: e.tensor_tensor(out=d[o].t[:, 0:L], in0=d[a_].t[:, 0:L], in1=bb, op=op),
                                    reads=rd, writes=[d[o].b])
            A(tt("t1", "brb", cs, ALU.mult, cosT))
            A(tt("t2", "bib", sn, ALU.mult, sinT))
            A(tt("wre", "t1", "t2", ALU.add))
            A(tt("t3", "bib", cs, ALU.mult, cosT))
            A(tt("t4", "brb", sn, ALU.mult, sinT))
            A(tt("wim", "t3", "t4", ALU.subtract))
            rb = rho.t[:, j:j + 1].to_broadcast([128, L])
            A(lambda: S.op("dve", lambda e: e.tensor_tensor_scan(out=d["zre"].t[:, 0:L], data0=rb, data1=d["wre"].t[:, 0:L],
                                                                 initial=hst.t[:, j:j + 1], op0=ALU.mult, op1=ALU.add),
                           reads=[d["wre"].b, hb[j], rho.b], writes=[d["zre"].b]))
            A(lambda: S.op("dve", lambda e: e.tensor_tensor_scan(out=d["zim"].t[:, 0:L], data0=rb, data1=d["wim"].t[:, 0:L],
                                                                 initial=hst.t[:, 16 + j:17 + j], op0=ALU.mult, op1=ALU.add),
                           reads=[d["wim"].b, hb[j], rho.b], writes=[d["zim"].b]))
            A(lambda: S.op("act", lambda e: e.copy(out=d["zrb"].t[:, 0:L], in_=d["zre"].t[:, 0:L]), reads=[d["zre"].b], writes=[d["zrb"].b]))
            A(lambda: S.op("act", lambda e: e.copy(out=d["zib"].t[:, 0:L], in_=d["zim"].t[:, 0:L]), reads=[d["zim"].b], writes=[d["zib"].b]))
            ct = d["ctmp"]
            A(lambda: S.op("act", lambda e: e.activation(out=ct.t[:, 0:1], in_=d["zim"].t[:, L - 1:L], func=AF.Identity,
                                                         scale=lc.t[:, 2, j:j + 1]), reads=[d["zim"].b, lc.b], writes=[ct.b]))
            A(lambda: S.op("act", lambda e: e.activation(out=ct.t[:, 1:2], in_=d["zim"].t[:, L - 1:L], func=AF.Identity,
                                                         scale=lc.t[:, 0, j:j + 1]), reads=[d["zim"].b, lc.b], writes=[ct.b]))
            A(lambda: S.op("act", lambda e: e.activation(out=hst.t[:, j:j + 1], in_=d["zre"].t[:, L - 1:L], func=AF.Identity,
                                                         scale=lc.t[:, 0, j:j + 1], bias=ct.t[:, 0:1]),
                           reads=[d["zre"].b, lc.b, ct.b], writes=[hb[j]]))
            A(lambda: S.op("act", lambda e: e.activation(out=hst.t[:, 16 + j:17 + j], in_=d["zre"].t[:, L - 1:L], func=AF.Identity,
                                                         scale=lc.t[:, 1, j:j + 1], bias=ct.t[:, 1:2]),
                           reads=[d["zre"].b, lc.b, ct.b], writes=[hb[j]]))
            A(tt("p1", "zrb", cs, ALU.mult, cosT))
            A(tt("p2", "zib", sn, ALU.mult, sinT))
            A(tt("p3", "zrb", sn, ALU.mult, sinT, eng="pool"))
            A(tt("p4", "zib", cs, ALU.mult, cosT, eng="pool"))
            for q_, (pl, nm) in enumerate(((0, "p1"), (2, "p2"), (1, "p3"), (1, "p4"))):
                A(lambda pl=pl, nm=nm, q_=q_: S.op("pe", lambda e: e.matmul(
                    yps.t[:, 0:L], lhsT=CT.t[:, pl, j * 128:(j + 1) * 128], rhs=d[nm].t[:, 0:L],
                    start=(jj == 0 and q_ == 0), stop=(jj == 3 and q_ == 3)), reads=[CT.b, d[nm].b], writes=[yps.b]))
            return ops_

        for pk in ("s", "p"):
            P = PATHS[pk]
            SPn, T = P["SP"], P["T"]
            L = SPn
            if pk == "s":
                S.op("dve", lambda e: e.tensor_copy(out=hst.t[:, 0:16], in_=smv("stT_re")), reads=[sm.b] + hb, writes=hb)
                S.op("dve", lambda e: e.tensor_copy(out=hst.t[:, 16:32], in_=smv("stT_im")), reads=[sm.b] + hb, writes=hb)
            else:
                S.op("dve", lambda e: e.memset(hst.t[:], 0.0), reads=hb, writes=hb)
            def prologue(s0, uT, zsG):
                hT = hTb[(s0 // SPn) % 2]
                S.begin_capture()
                S.op("sp", lambda e: e.dma_start(out=hT.t[:, :, 0:L], in_=SC[pk]["hT"][:, s0:s0 + L].rearrange("(k p) t -> p k t", p=128)),
                     writes=[hT.b], dma="bhT%d" % ((s0 // SPn) % 2))
                norm_l = S.end_capture()
                groups = []
                for sec in (0, 1):
                    for cch in range(4):
                        S.begin_capture()
                        for k in range(8):
                            S.op("pe", lambda e, sec=sec, cch=cch, k=k: e.matmul(
                                ups.t[:, 0:L], lhsT=wB.t[:, k, sec * 512 + cch * 128:sec * 512 + (cch + 1) * 128],
                                rhs=hT.t[:, k, 0:L], start=(k == 0), stop=(k == 7)), reads=[hT.b, wB.b], writes=[ups.b])
                        if sec == 0:
                            S.op("act", lambda e, cch=cch: e.copy(out=uT.t[:, cch, 0:L], in_=ups.t[:, 0:L]), reads=[ups.b], writes=[uT.b])
                        else:
                            S.op("act", lambda e, cch=cch: e.activation(out=zsG.t[:, cch, 0:L], in_=ups.t[:, 0:L], func=AF.Silu),
                                 reads=[ups.b], writes=[zsG.b])
                        groups.append(S.end_capture())
                return norm_l, groups

            span_list = list(range(0, T, SPn))
            pro_n, pro_g = prologue(span_list[0], uTs[spc[0] % 2], zsGs[spc[0] % 2])
            S.replay(pro_n)
            for g_ in pro_g:
                S.replay(g_)
            for sidx, s0 in enumerate(span_list):
                sc_ = spc[0]
                spc[0] += 1
                ys, uT, zsG = yS[sc_ % 2], uTs[sc_ % 2], zsGs[sc_ % 2]
                nxt, nxt_groups = [], []
                if sidx + 1 < len(span_list):
                    nxt, nxt_groups = prologue(span_list[sidx + 1], uTs[(sc_ + 1) % 2], zsGs[(sc_ + 1) % 2])
                pairs = [(cq, pr) for cq in range(4) for pr in range(2)]

                def pair_ops(n):
                    cq, pr = pairs[n]
                    outl = []
                    for tl_ in range(2):
                        dd = dict(TS[tl_])
                        dd.update(bX[n % 2][tl_])
                        outl.append(tile_ops(pk, L, cq * 4 + pr * 2 + tl_, cq, pr * 2 + tl_, dd, uT))
                    return outl
                cur = pair_ops(0)
                for i in range(4):
                    cur[0][i]()
                    cur[1][i]()
                gT = gTs[sc_ % 2]
                bg = list(carry_bg)
                bg_norm = list(nxt)
                del carry_bg[:]

                def bg_step():
                    if bg:
                        S.replay([bg.pop(0)])
                    if bg_norm:
                        S.replay([bg_norm.pop(0)])

                def gelu_ops(cq):
                    S.begin_capture()
                    dcol = smv("dT")[:, cq:cq + 1]
                    S.op("dve", lambda e: e.scalar_tensor_tensor(
                        out=yT.t[:, 0:L], in0=uT.t[:, cq, 0:L], scalar=dcol, in1=yps.t[:, 0:L], op0=ALU.mult, op1=ALU.add),
                        reads=[uT.b, yps.b, sm.b], writes=[yT.b])
                    S.op("act", lambda e: e.activation(out=x2.t[:, 0:L], in_=yT.t[:, 0:L], func=AF.Square), reads=[yT.b], writes=[x2.b])
                    S.op("pool", lambda e: e.tensor_scalar(out=x2.t[:, 0:L], in0=x2.t[:, 0:L], scalar1=0.044715, scalar2=1.0,
                                                          op0=ALU.mult, op1=ALU.add), reads=[x2.b], writes=[x2.b])
                    S.op("pool", lambda e: e.tensor_tensor(out=x2.t[:, 0:L], in0=x2.t[:, 0:L], in1=yT.t[:, 0:L], op=ALU.mult),
                         reads=[x2.b, yT.b], writes=[x2.b])
                    S.op("act", lambda e: e.activation(out=x2.t[:, 0:L], in_=x2.t[:, 0:L], func=AF.Sigmoid,
                                                       scale=2.0 * math.sqrt(2.0 / math.pi)), reads=[x2.b], writes=[x2.b])
                    S.op("pool", lambda e: e.tensor_tensor(out=gT.t[:, cq, 0:L], in0=x2.t[:, 0:L], in1=yT.t[:, 0:L], op=ALU.mult),
                         reads=[x2.b, yT.b], writes=[gT.b])
                    return S.end_capture()

                for n, (cq, pr) in enumerate(pairs):
                    la, lb = cur
                    if n + 1 < len(pairs):
                        cur = pair_ops(n + 1)
                        for i in range(4):
                            cur[0][i]()
                            cur[1][i]()
                    if n >= 4 and nxt_groups:
                        while bg_norm:
                            S.replay([bg_norm.pop(0)])
                        for g_ in nxt_groups[2 * (n - 4):2 * (n - 4) + 2]:
                            S.replay(g_)
                    na = len(la) - 4
                    for i in range(4, na):
                        la[i]()
                        lb[i]()
                        bg_step()
                    for i in range(na, len(la)):
                        la[i]()
                    for i in range(na, len(lb)):
                        lb[i]()
                    if pr == 1:
                        g_ = gelu_ops(cq)
                        if cq < 3:
                            merged = []
                            rest = bg[:]
                            del bg[:]
                            for op_ in g_:
                                merged.append(op_)
                                merged.extend(rest[:2])
                                rest = rest[2:]
                            bg.extend(merged + rest)
                        else:
                            last_gelu = g_
                while bg or bg_norm:
                    bg_step()
                S.begin_capture()
                for co in range(4):
                    for ci in range(4):
                        S.op("pe", lambda e, co=co, ci=ci: e.matmul(gps.t[:, 0:L], lhsT=wg.t[:, ci, co * 128:(co + 1) * 128],
                                                                    rhs=gT.t[:, ci, 0:L], start=(ci == 0), stop=(ci == 3)),
                             reads=[wg.b, gT.b], writes=[gps.b])
                    S.op("act", lambda e, co=co: e.activation(out=sgl.t[:, 0:L], in_=gps.t[:, 0:L], func=AF.Sigmoid,
                                                              bias=smv("b_gluT")[:, co:co + 1]), reads=[gps.b, sm.b], writes=[sgl.b])
                    S.op("pool", lambda e, co=co: e.tensor_tensor(out=y2.t[:, 0:L], in0=sgl.t[:, 0:L], in1=gT.t[:, co, 0:L], op=ALU.mult),
                         reads=[sgl.b, gT.b], writes=[y2.b])
                    S.op("pool", lambda e, co=co, ys=ys, zsG=zsG: e.tensor_tensor(out=ys.t[:, co, 0:L], in0=y2.t[:, 0:L], in1=zsG.t[:, co, 0:L],
                                                                               op=ALU.mult), reads=[y2.b, zsG.b], writes=[ys.b])
                S.op("pool", lambda e, ys=ys, s0=s0: e.dma_start(
                    out=SC[pk]["yssm"][:, s0:s0 + SPn].rearrange("(c p) t -> p c t", p=128), in_=ys.t[:, :, 0:SPn]),
                    reads=[ys.b], dma="ys%d" % (sc_ % 2))
                glu = S.end_capture()
                if sidx + 1 < len(span_list):
                    carry_bg.extend(last_gelu + glu)
                else:
                    S.replay(last_gelu + glu)
            S.op("pool", lambda e, pk=pk: e.dma_start(out=O["ssm_" + pk], in_=hst.t[:, :]), reads=hb, dma="hsto")
        S.barrier()
        st.close()

    def phase2C():
        st = contextlib.ExitStack()
        yatt = {"p": tile(st, "yatt_p", [128, 4, T_P], BF), "s": tile(st, "yatt_s", [128, 4, T_S], BF)}
        st2 = contextlib.ExitStack()
        kT = [tile(st2, "c_kT%d" % i, [65, T_P], BF) for i in range(2)]
        vA = [tile(st2, "c_vA%d" % i, [128, 64, 128], BF) for i in range(2)]
        qT = [tile(st2, "c_qT%d" % i, [65, 512], BF) for i in range(2)]
        zg = [tile(st2, "c_zg%d" % i, [128, 512], BF) for i in range(2)]
        NSB = 4
        kTb2 = [Buf(), Buf()]
        vAb2 = [Buf(), Buf()]
        pT = [tile(st2, "c_pT%d" % i, [128, 512], BF) for i in range(NSB)]
        rdens = [tile(st2, "c_rden%d" % i, [128, 512]) for i in range(2)]
        rdh = [tile(st2, "c_rdh%d" % i, [128, 512], BF) for i in range(2)]
        rdl = [tile(st2, "c_rdl%d" % i, [128, 512], BF) for i in range(2)]
        epi_q = []
        gbc = tile(st2, "c_gbc", [128, 512])
        sps = [PS[0], PS[1], PS[2], PS[6]]
        ops = [PS[3], PS[4]]
        bcps = PS[5]
        LA = 3
        heads = [(pk, h) for pk in ("s", "p") for h in range(8)]
        spans = []
        tasks = []
        for hi, (pk, h) in enumerate(heads):
            P = PATHS[pk]
            T, SPn, npast = P["T"], P["SP"], P["npast"]
            nk = npast + T
            for s0 in range(0, T, SPn):
                si = len(spans)
                qpos0 = npast + s0
                last_blk = (qpos0 + SPn - 1) // 128
                spans.append(dict(pk=pk, h=h, hi=hi, s0=s0, SPn=SPn, si=si, last_blk=last_blk, nk=nk))
                for j in range(last_blk + 1):
                    kk = min(128, nk - j * 128)
                    c0 = max(0, j * 128 - qpos0)
                    N = SPn - c0
                    diag = (j * 128 + kk - 1) > (qpos0 + c0)
                    tasks.append(dict(si=si, j=j, kk=kk, c0=c0, N=N, diag=diag, last=(j == last_blk)))
        ntasks_of = [sp["last_blk"] + 1 for sp in spans]
        last_span_of_head = {}
        for sp in spans:
            last_span_of_head[sp["hi"]] = sp["si"]

        def head_load(hi):
            if hi >= len(heads):
                return
            pk, h = heads[hi]
            hb = hi % 2
            kt, va = kT[hb], vA[hb]
            nk = PATHS[pk]["npast"] + PATHS[pk]["T"]
            nh = (nk // 256) * 128
            S.op("sp", lambda e: e.dma_start(out=kt.t[0:65, 0:nh], in_=SC[pk]["kTa"][h, :, 0:nh]), writes=[kt.b], dma="ckT%d" % hb)
            S.op("pool", lambda e: e.dma_start(out=kt.t[0:65, nh:nk], in_=SC[pk]["kTa"][h, :, nh:nk]), writes=[kTb2[hb]], dma="ckTb%d" % hb)
            nfull = nk // 128
            nfh = nfull // 2
            S.op("sp", lambda e: e.dma_start(
                out=va.t[:, 0:nfh, :], in_=SC[pk]["va"][0:nfh * 128, h, :].rearrange("(b p) c -> p b c", p=128)),
                writes=[va.b], dma="cvA%d" % hb)
            S.op("pool", lambda e: e.dma_start(
                out=va.t[:, nfh:nfull, :], in_=SC[pk]["va"][nfh * 128:nfull * 128, h, :].rearrange("(b p) c -> p b c", p=128)),
                writes=[vAb2[hb]], dma="cvAb%d" % hb)
            if nk % 128:
                r = nk % 128
                S.op("sp", lambda e: e.dma_start(out=va.t[0:r, nfull, :], in_=SC[pk]["va"][nfull * 128:nk, h, :]),
                     writes=[va.b], dma="cvA%d" % hb)

        def span_load(si):
            if si >= len(spans):
                return
            sp = spans[si]
            pk, h, s0, SPn = sp["pk"], sp["h"], sp["s0"], sp["SPn"]
            qb = si % 2
            prw = slice(64, 128) if h % 2 else slice(0, 64)
            S.op("sp", lambda e: e.dma_start(out=qT[qb].t[0:65, 0:SPn], in_=SC[pk]["qTa"][h, :, s0:s0 + SPn]),
                 writes=[qT[qb].b], dma="cqT%d" % qb)
            S.op("sp", lambda e: e.dma_start(out=zg[qb].t[prw, 0:SPn], in_=SC[pk]["zaG"][h * 64:(h + 1) * 64, s0:s0 + SPn]),
                 writes=[zg[qb].b], dma="czg%d" % qb)

        def emit_S(i):
            t = tasks[i]
            sp = spans[t["si"]]
            kt, qt = kT[sp["hi"] % 2], qT[sp["si"] % 2]
            sp_ = sps[i % NSB]
            j, kk, c0, N, diag = t["j"], t["kk"], t["c0"], t["N"], t["diag"]
            S.op("pe", lambda e: e.matmul(sp_.t[0:kk, 0:N], lhsT=kt.t[0:65, j * 128:j * 128 + kk], rhs=qt.t[0:65, c0:c0 + N],
                                          start=True, stop=(not diag)), reads=[kt.b, kTb2[sp["hi"] % 2], qt.b], writes=[sp_.b])
            if diag:
                nd = min(kk, N)
                S.op("pe", lambda e: e.matmul(sp_.t[0:kk, 0:nd], lhsT=ident_bf[0:kk, 0:kk], rhs=maskT[0:kk, 0:nd], start=False, stop=True),
                     reads=[cb.b], writes=[sp_.b])

        def emit_PV(i):
            t = tasks[i]
            sp = spans[t["si"]]
            pk, h, s0, SPn, si = sp["pk"], sp["h"], sp["s0"], sp["SPn"], sp["si"]
            va = vA[sp["hi"] % 2]
            sp_, ptl, opsb = sps[i % NSB], pT[i % NSB], ops[si % 2]
            j, kk, c0, N = t["j"], t["kk"], t["c0"], t["N"]
            odd = h % 2
            prw = slice(64, 128) if odd else slice(0, 64)
            drow = 0 if odd else 64
            M = 128 if odd else 65
            S.op("act", lambda e: e.activation(out=ptl.t[0:kk, 0:N], in_=sp_.t[0:kk, 0:N], func=AF.Exp, scale=0.125,
                                               bias=biasK[pk].t[0:kk, j, h:h + 1]), reads=[sp_.b, biasK[pk].b], writes=[ptl.b])
            S.op("pe", lambda e: e.matmul(opsb.t[0:M, c0:c0 + N], lhsT=va.t[0:kk, j, 0:M], rhs=ptl.t[0:kk, 0:N],
                                          start=(j == 0), stop=t["last"]), reads=[va.b, vAb2[sp["hi"] % 2], ptl.b], writes=[opsb.b])
            if t["last"]:
                zgt = zg[si % 2]
                rd = rdens[si % 2]
                rh, rl = rdh[si % 2], rdl[si % 2]
                S.op("dve", lambda e: e.reciprocal(out=rd.t[drow:drow + 1, 0:SPn], in_=opsb.t[drow:drow + 1, 0:SPn]),
                     reads=[opsb.b], writes=[rd.b])
                S.op("dve", lambda e: e.tensor_copy(out=rh.t[drow:drow + 1, 0:SPn], in_=rd.t[drow:drow + 1, 0:SPn]),
                     reads=[rd.b], writes=[rh.b])
                S.op("dve", lambda e: e.tensor_tensor(out=rl.t[drow:drow + 1, 0:SPn], in0=rd.t[drow:drow + 1, 0:SPn],
                                                      in1=rh.t[drow:drow + 1, 0:SPn], op=ALU.subtract),
                     reads=[rd.b, rh.b], writes=[rl.b])

                def epi():
                    S.op("pe", lambda e: e.matmul(bcps.t[:, 0:SPn], lhsT=ones_bf.t[drow:drow + 1, 0:128], rhs=rh.t[drow:drow + 1, 0:SPn],
                                                  start=True, stop=False), reads=[rh.b, ones_bf.b], writes=[bcps.b])
                    S.op("pe", lambda e: e.matmul(bcps.t[:, 0:SPn], lhsT=ones_bf.t[drow:drow + 1, 0:128], rhs=rl.t[drow:drow + 1, 0:SPn],
                                                  start=False, stop=True), reads=[rl.b, ones_bf.b], writes=[bcps.b])
                    S.op("dve", lambda e: e.tensor_tensor(out=gbc.t[prw, 0:SPn], in0=bcps.t[prw, 0:SPn], in1=zgt.t[prw, 0:SPn], op=ALU.mult),
                         reads=[bcps.b, zgt.b], writes=[gbc.b])
                    S.op("dve", lambda e: e.tensor_tensor(out=yatt[pk].t[prw, h // 2, s0:s0 + SPn], in0=opsb.t[prw, 0:SPn],
                                                          in1=gbc.t[prw, 0:SPn], op=ALU.mult), reads=[opsb.b, gbc.b], writes=[yatt[pk].b])
                    span_load(si + 2)
                    if last_span_of_head[sp["hi"]] == si:
                        head_load(sp["hi"] + 2)
                nxt_n = ntasks_of[si + 1] if si + 1 < len(spans) else 8
                epi_q.append([max(1, min(5, nxt_n - LA)), epi])

        head_load(0)
        head_load(1)
        span_load(0)
        span_load(1)
        for i in range(len(tasks) + LA):
            if i < len(tasks):
                emit_S(i)
            for it in epi_q:
                it[0] -= 1
            while epi_q and epi_q[0][0] <= 0:
                epi_q.pop(0)[1]()
            if i >= LA:
                emit_PV(i - LA)
        while epi_q:
            epi_q.pop(0)[1]()
        S.barrier()
        st2.close()
        wos = [tile(st, "c_wos%d" % i, [128, 1024]) for i in range(2)]
        wo = {"p": tile(st, "c_wo_p", [128, 8, 1024], BF), "s": tile(st, "c_wo_s", [128, 8, 1024], BF)}
        for k in range(8):
            wsl = wos[k % 2]
            S.op("sp", lambda e, wsl=wsl, k=k: e.dma_start(out=wsl.t[:], in_=w_out[:, k, :]), writes=[wsl.b], dma="cwo%d" % (k % 2))
            for pk in ("p", "s"):
                pi = PATHS[pk]["pi"]
                S.op("dve", lambda e, wsl=wsl, k=k, pk=pk, pi=pi: e.tensor_tensor(out=wo[pk].t[:, k, :], in0=wsl.t[:], in1=gate_bc.t[:, pi, :],
                                                                               op=ALU.mult), reads=[wsl.b, gate_bc.b], writes=[wo[pk].b])
        ysm = [tile(st, "c_ysm%d" % i, [128, 4, 512], BF) for i in range(2)]
        xr = [tile(st, "c_xr%d" % i, [128, D]) for i in range(2)]
        yo = [tile(st, "c_yo%d" % i, [128, D]) for i in range(2)]
        o_ps = [PS[6], PS[7]]
        sc2 = [0]
        tc2 = [0]
        for pk in ("s", "p"):
            P = PATHS[pk]
            T, SPn, TT = P["T"], P["SP"], P["TT"]
            for s0 in range(0, T, SPn):
                sb = sc2[0] % 2
                sc2[0] += 1
                ym = ysm[sb]
                S.op("sp", lambda e, ym=ym, s0=s0: e.dma_start(
                    out=ym.t[:, :, 0:SPn], in_=SC[pk]["yssm"][:, s0:s0 + SPn].rearrange("(c p) t -> p c t", p=128)),
                    writes=[ym.b], dma="cys%d" % sb)
                for ti in range(SPn // TT):
                    tb = tc2[0] % 2
                    tc2[0] += 1
                    t0 = s0 + ti * TT
                    xt, yt = xr[tb], yo[tb]
                    S.op("sp", lambda e, xt=xt, t0=t0: e.dma_start(out=xt.t[0:TT, :], in_=xin[pk][t0:t0 + TT, :]), writes=[xt.b], dma="cxr%d" % tb)
                    for half in range(2):
                        op_ = o_ps[half]
                        for c in range(8):
                            if c < 4:
                                lh, lb = ym.t[:, c, ti * TT:(ti + 1) * TT], ym.b
                            else:
                                lh, lb = yatt[pk].t[:, c - 4, t0:t0 + TT], yatt[pk].b
                            S.op("pe", lambda e, op_=op_, lh=lh, c=c, half=half: e.matmul(
                                op_.t[0:TT, :], lhsT=lh, rhs=wo[pk].t[:, c, half * 512:(half + 1) * 512], start=(c == 0), stop=(c == 7)),
                                reads=[lb, wo[pk].b], writes=[op_.b])
                        S.op("dve", lambda e, op_=op_, xt=xt, yt=yt, half=half: e.tensor_tensor(
                            out=yt.t[0:TT, half * 512:(half + 1) * 512], in0=op_.t[0:TT, :], in1=xt.t[0:TT, half * 512:(half + 1) * 512],
                            op=ALU.add), reads=[op_.b, xt.b], writes=[yt.b])
                    S.op("pool", lambda e, yt=yt, t0=t0: e.dma_start(out=O["y_" + pk][t0:t0 + TT, :], in_=yt.t[0:TT, :]),
                         reads=[yt.b], dma="cyo%d" % tb)
        S.barrier()
        st.close()

    if "0" in phases:
        phase0()
    stTab = contextlib.ExitStack()
    if "A" in phases:
        alloc_tables(stTab)
        phaseN()
        phaseA()
    if "B" in phases:
        phaseB()
    stTab.close()
    if "2" in phases:
        phase2C()
    S.barrier()
    S.emit(nc, root)
    root.close()
    return nc


def _common_inputs(inp):
    f = np.float32
    d = {}
    d["w_ada"] = np.ascontiguousarray(inp["w_ada"][0], f)
    d["w_in"] = np.ascontiguousarray(inp["w_in"][0], f)
    d["w_out"] = np.ascontiguousarray(inp["w_out"][0], f)
    d["w_glu"] = np.ascontiguousarray(inp["w_glu"][0], f)
    bc = np.zeros((128, 2048), f)
    bc[:, 0:1024] = inp["b_ada"][0][2048:3072][None, :]
    bc[:, 1024:1536] = np.tile(inp["q_norm_g"][0], 8)[None, :]
    bc[:, 1536:2048] = np.tile(inp["k_norm_g"][0], 8)[None, :]
    d["bc"] = bc
    rep3 = np.zeros((128, 3, 2048), f)
    rep3[:, 0, :] = inp["ssm_a_re"][0].reshape(-1)[None, :]
    rep3[:, 1, :] = inp["ssm_a_im"][0].reshape(-1)[None, :]
    rep3[:, 2, :] = np.repeat(inp["ssm_log_dt"][0], 64)[None, :]
    d["rep3"] = rep3
    BTl = np.zeros((128, 2, 16, 2, 64), f)
    CTl = np.zeros((128, 2, 16, 128), f)
    for pl, (bsrc, csrc) in enumerate(((inp["ssm_b_re"][0], inp["ssm_c_re"][0]), (inp["ssm_b_im"][0], inp["ssm_c_im"][0]))):
        for g in range(32):
            j, g2, gl = g // 2, g % 2, g % 8
            BTl[gl * 16:(gl + 1) * 16, pl, j, g2, :] = bsrc[g].T
            CTl[g2 * 64:(g2 + 1) * 64, pl, j, gl * 16:(gl + 1) * 16] = csrc[g].T
    d["BTlay"] = BTl.reshape(128, 2, 2048)
    d["CTlay"] = CTl.reshape(128, 2, 2048)
    cf = np.zeros((128, NCF), f)
    cf[:, CF_ID:CF_ID + 128] = np.eye(128, dtype=f)
    cf[:, CF_TRI:CF_TRI + 128] = np.triu(np.ones((128, 128), f))
    cf[127, CF_SEL128:CF_SEL128 + 128] = 1.0
    cf[63, CF_SEL64:CF_SEL64 + 128] = 1.0
    cf[:, CF_IOTA:CF_IOTA + 512] = np.arange(1, 513, dtype=f)[None, :]
    d["cf"] = cf
    cb = np.zeros((128, 256), f)
    cb[:, 0:128] = np.eye(128, dtype=f)
    cb[:, 128:256] = np.where(np.arange(128)[:, None] > np.arange(128)[None, :], -240000.0, 0.0)
    d["cb"] = cb.astype(ml_dtypes.bfloat16)
    return d


def _smalls(inp, b):
    f = np.float32
    sm = np.zeros((128, NS), f)

    def put(name, arr):
        a, e = SM[name]
        sm[:, a:e] = arr

    put("b_adaT", inp["b_ada"][0][0:2048].reshape(16, 128).T)
    put("norm_gT", inp["norm_g"][0].reshape(8, 128).T)
    put("bf_bc", np.broadcast_to(inp["b_f"][0][None, :], (128, 8)))
    put("dT", inp["ssm_d"][0].reshape(4, 128).T)
    put("b_gluT", inp["b_glu"][0].reshape(4, 128).T)
    put("logdtT", np.repeat(inp["ssm_log_dt"][0], 64).reshape(16, 128).T)
    put("a_reT", inp["ssm_a_re"][0].reshape(16, 128).T)
    put("a_imT", inp["ssm_a_im"][0].reshape(16, 128).T)
    c2 = np.stack([inp["c_prompt"][b].reshape(8, 128).T, inp["c_sample"][b].reshape(8, 128).T], axis=-1)
    put("c2T", c2.reshape(128, 16))
    put("stT_re", inp["state_ssm_re"][0, b].reshape(16, 128).T)
    put("stT_im", inp["state_ssm_im"][0, b].reshape(16, 128).T)
    return sm


_NC_CACHE = {}


def kernel(**inp):
    inp = {k: np.asarray(v) for k, v in inp.items()}
    if "nc" not in _NC_CACHE:
        _NC_CACHE["nc"] = build_program()
    nc = _NC_CACHE["nc"]
    common = _common_inputs(inp)
    in_maps = []
    f = np.float32
    for b in range(NCORES):
        m = dict(common)
        m["smalls"] = _smalls(inp, b)
        m["x_p"] = np.ascontiguousarray(inp["x_prompt"][b], f)
        m["x_s"] = np.ascontiguousarray(inp["x_sample"][b], f)
        m["ck"] = np.ascontiguousarray(inp["cache_k"][0, b].reshape(NPAST, 512), f)
        m["cv"] = np.ascontiguousarray(inp["cache_v"][0, b].reshape(NPAST, 512), f)
        m["clf"] = np.ascontiguousarray(inp["cache_logf"][0, b], f)
        in_maps.append(m)
    res = run_bass_kernel_spmd(nc, in_maps, core_ids=list(range(NCORES)))
    R = res.results

    def stack(name):
        return np.stack([np.asarray(R[b][name], f) for b in range(NCORES)], axis=0)

    def ssm(name, lo):
        a = stack(name)[:, :, lo:lo + 16]
        return np.ascontiguousarray(a.transpose(0, 2, 1).reshape(NCORES, 32, 64))[None]

    y_p = stack("y_p")
    y_s = stack("y_s")
    k_p = stack("k_p").reshape(1, NCORES, T_P, 8, 64)
    v_p = stack("v_p").reshape(1, NCORES, T_P, 8, 64)
    lf_p = stack("lf_p")[None]
    k_s = stack("k_s").reshape(1, NCORES, T_S, 8, 64)
    v_s = stack("v_s").reshape(1, NCORES, T_S, 8, 64)
    lf_s = stack("lf_s")[None]
    return (y_p, y_s, k_p, v_p, lf_p, ssm("ssm_p", 0), ssm("ssm_p", 16),
            k_s, v_s, lf_s, ssm("ssm_s", 0), ssm("ssm_s", 16))
```

```python
import contextlib
import math
import numpy as np
import ml_dtypes
import concourse.bass as bass
import concourse.mybir as mybir
from concourse.bass_utils import run_bass_kernel_spmd

F32 = mybir.dt.float32
BF = mybir.dt.bfloat16
ALU = mybir.AluOpType
AF = mybir.ActivationFunctionType
AX = mybir.AxisListType

SAME_ENG_SYNC = True
D = 1024
NCORES = 8
T_P, T_S, NPAST = 8192, 64, 2048
EPS = 1e-6
PATHS = {
    "p": dict(pi=0, T=T_P, TT=128, SP=512, npast=0),
    "s": dict(pi=1, T=T_S, TT=64, SP=64, npast=NPAST),
}
SM = {}
_o = 0
for _n, _w in (("b_adaT", 16), ("norm_gT", 8), ("bf_bc", 8), ("dT", 4), ("b_gluT", 4), ("logdtT", 16),
               ("a_reT", 16), ("a_imT", 16), ("c2T", 16), ("stT_re", 16), ("stT_im", 16)):
    SM[_n] = (_o, _o + _w)
    _o += _w
NS = _o
CF_ID, CF_TRI, CF_SEL128, CF_SEL64, CF_IOTA = 0, 128, 256, 384, 512
NCF = 1024


class Ev:
    __slots__ = ("kind", "key", "val")

    def __init__(s, kind, key, val):
        s.kind, s.key, s.val = kind, key, val


class Rec:
    __slots__ = ("call",)

    def __init__(s):
        s.call = None

    def __getattr__(s, name):
        def f(*a, **k):
            s.call = (name, a, k)
            return s
        return f


class Buf:
    __slots__ = ("w", "r")

    def __init__(s):
        s.w = None
        s.r = {}


class Sched:
    ENGS = ("pe", "act", "dve", "pool", "sp")

    def __init__(s):
        s.q = {e: [] for e in s.ENGS}
        s.waited = {e: {} for e in s.ENGS}
        s.sig = {e: set() for e in s.ENGS}
        s.dma_cnt = {}
        s.dma_last = {}
        s.capture = None

    def op(s, eng, fn, reads=(), writes=(), dma=None, after=()):
        rec = Rec()
        fn(rec)
        assert rec.call is not None
        if s.capture is not None:
            s.capture.append((eng, rec.call, tuple(reads), tuple(writes), dma, tuple(after)))
            return None
        return s._sched(eng, rec.call, reads, writes, dma, after)

    def begin_capture(s):
        s.capture = []

    def end_capture(s):
        items, s.capture = s.capture, None
        return items

    def replay(s, items):
        for it in items:
            s._sched(*it)

    def _sched(s, eng, call, reads=(), writes=(), dma=None, after=()):
        deps = list(after)
        for b in reads:
            if b.w is not None:
                deps.append(b.w)
        for b in writes:
            if b.w is not None:
                deps.append(b.w)
            deps.extend(b.r.values())
        waits = []
        wd = s.waited[eng]
        for ev in deps:
            if ev.kind == "E" and ev.key == eng and dma is None and (eng == "pe" or not SAME_ENG_SYNC):
                continue
            k = (ev.kind, ev.key)
            if wd.get(k, -1) >= ev.val:
                continue
            wd[k] = ev.val
            waits.append(ev)
            if ev.kind == "E":
                s.sig[ev.key].add(ev.val)
        idx = len(s.q[eng])
        if dma is not None:
            n = s.dma_cnt.get(dma, 0) + 1
            s.dma_cnt[dma] = n
            ev = Ev("D", dma, 16 * n)
            s.dma_last[dma] = ev
        else:
            ev = Ev("E", eng, idx)
        s.q[eng].append((call, waits, dma))
        k = (ev.kind, ev.key)
        for b in reads:
            b.r[k] = ev
        for b in writes:
            b.w = ev
            b.r = {}
        return ev

    def barrier(s, engs=None):
        evs = []
        for e in s.ENGS:
            if s.q[e]:
                for idx in range(len(s.q[e]) - 1, -1, -1):
                    fn, _, dma = s.q[e][idx]
                    if fn is not None and dma is None:
                        evs.append(Ev("E", e, idx))
                        break
        evs.extend(s.dma_last.values())
        for e in (engs or s.ENGS):
            waits = []
            wd = s.waited[e]
            for ev in evs:
                if ev.kind == "E" and ev.key == e:
                    continue
                k = (ev.kind, ev.key)
                if wd.get(k, -1) >= ev.val:
                    continue
                wd[k] = ev.val
                waits.append(ev)
                if ev.kind == "E":
                    s.sig[ev.key].add(ev.val)
            s.q[e].append((None, waits, None))

    def emit(s, nc, st):
        sems = {e: st.enter_context(nc.semaphore("c_" + e)) for e in s.ENGS}
        dsems = {k: st.enter_context(nc.semaphore("d_" + k)) for k in s.dma_cnt}
        rank = {}
        for e in s.ENGS:
            rank[e] = {idx: i + 1 for i, idx in enumerate(sorted(s.sig[e]))}

        def run(eo, e):
            rk = rank[e]
            for idx, (fn, waits, dma) in enumerate(s.q[e]):
                for ev in waits:
                    if ev.kind == "E":
                        eo.wait_ge(sems[ev.key], rank[ev.key][ev.val])
                    else:
                        eo.wait_ge(dsems[ev.key], ev.val)
                if fn is None:
                    continue
                ins = getattr(eo, fn[0])(*fn[1], **fn[2])
                if dma is not None:
                    ins.then_inc(dsems[dma], 16)
                elif idx in rk:
                    ins.then_inc(sems[e], 1)

        with nc.Block() as block:
            @block.tensor
            def _(eo):
                run(eo, "pe")

            @block.scalar
            def _(eo):
                run(eo, "act")

            @block.vector
            def _(eo):
                run(eo, "dve")

            @block.gpsimd
            def _(eo):
                run(eo, "pool")

            @block.sync
            def _(eo):
                run(eo, "sp")


class Tl:
    __slots__ = ("t", "b")

    def __init__(s, t):
        s.t = t
        s.b = Buf()


def build_program(phases="0AB2C"):
    nc = bass.Bass("TRN2", target_bir_lowering=False)
    S = Sched()
    root = contextlib.ExitStack()

    def din(name, shape, dt=F32):
        return nc.dram_tensor(name, list(shape), dt, kind="ExternalInput").ap()

    def dout(name, shape):
        return nc.dram_tensor(name, list(shape), F32, kind="ExternalOutput").ap()

    def dscr(name, shape, dt=BF):
        return nc.dram_tensor(name, list(shape), dt).ap()

    w_ada = din("w_ada", [D, 3 * D]).rearrange("(k p) n -> p k n", p=128)
    w_in = din("w_in", [D, 3080]).rearrange("(k p) n -> p k n", p=128)
    w_out = din("w_out", [D, D]).rearrange("(k p) n -> p k n", p=128)
    w_glu = din("w_glu", [512, 512]).rearrange("(k p) n -> p k n", p=128)
    bcin = din("bc", [128, 2048])
    rep3 = din("rep3", [128, 3, 2048])
    BTlay = din("BTlay", [128, 2, 2048])
    CTlay = din("CTlay", [128, 2, 2048])
    cfin = din("cf", [128, NCF])
    cbin = din("cb", [128, 256], BF)
    smin = din("smalls", [128, NS])
    xin = {"p": din("x_p", [T_P, D]), "s": din("x_s", [T_S, D])}
    ckin = din("ck", [NPAST, 512])
    cvin = din("cv", [NPAST, 512])
    clfin = din("clf", [NPAST, 8])
    O = {}
    for pk, P in PATHS.items():
        O["y_" + pk] = dout("y_" + pk, [P["T"], D])
        O["k_" + pk] = dout("k_" + pk, [P["T"], 512])
        O["v_" + pk] = dout("v_" + pk, [P["T"], 512])
        O["lf_" + pk] = dout("lf_" + pk, [P["T"], 8])
        O["ssm_" + pk] = dout("ssm_" + pk, [128, 32])
    SC = {}
    for pk, P in PATHS.items():
        nk = P["npast"] + P["T"]
        SC[pk] = dict(qTa=dscr("qTa_" + pk, [8, 65, P["T"]]), kTa=dscr("kTa_" + pk, [8, 65, nk]),
                      va=dscr("va_" + pk, [nk, 8, 128]), zaG=dscr("zaG_" + pk, [512, P["T"]]),
                      yssm=dscr("yssm_" + pk, [512, P["T"]]), hT=dscr("hTs_" + pk, [1024, P["T"]]))

    tcount = [0]

    def tile(st, name, shape, dt=F32):
        tcount[0] += 1
        return Tl(st.enter_context(nc.sbuf_tensor("sb%d_%s" % (tcount[0], name), list(shape), dt)))

    PS = [Tl(root.enter_context(nc.psum_tensor("ps%d" % i, [128, 512], F32))) for i in range(8)]

    sm = tile(root, "sm", [128, NS])
    cf = tile(root, "cf", [128, NCF])
    cb = tile(root, "cb", [128, 256], BF)
    qkg = tile(root, "qkg", [128, 1024])
    gate_bc = tile(root, "gate_bc", [128, 2, 1024])
    modb = tile(root, "modb", [128, 16, 2])
    gsT = tile(root, "gsT", [128, 8, 2])
    negb8 = tile(root, "negb8", [128, 8])
    theta = tile(root, "theta", [128, 16])
    rho = tile(root, "rho", [128, 16])
    BT = tile(root, "BT", [128, 2, 2048], BF)
    CT = tile(root, "CT", [128, 3, 2048], BF)
    biasK = {"p": tile(root, "biasK_p", [128, 64, 8]), "s": tile(root, "biasK_s", [128, 17, 8])}
    ones_bf = tile(root, "ones_bf", [128, 128], BF)
    ones_f = tile(root, "ones_f", [128, 128])
    epsc = tile(root, "epsc", [128, 1])

    def smv(name):
        a, b = SM[name]
        return sm.t[:, a:b]

    ident_bf = cb.t[:, 0:128]
    maskT = cb.t[:, 128:256]
    ident_f = cf.t[:, CF_ID:CF_ID + 128]
    tri_f = cf.t[:, CF_TRI:CF_TRI + 128]
    sel128 = cf.t[:, CF_SEL128:CF_SEL128 + 128]
    sel64 = cf.t[:, CF_SEL64:CF_SEL64 + 128]
    iota_t1 = cf.t[:, CF_IOTA:CF_IOTA + 512]

    dcount = [0]

    def dname(prefix):
        dcount[0] += 1
        return "%s%d" % (prefix, dcount[0])

    TWO_PI_HI = 6.28125
    TWO_PI_LO = 2.0 * math.pi - 6.28125

    def range_sin(dst, ang, shift, tf, ti, bd, ba, btf, bti):
        off = shift + 16.0 * math.pi
        S.op("dve", lambda e: e.tensor_scalar(out=tf, in0=ang, scalar1=off, scalar2=1.0 / (2.0 * math.pi), op0=ALU.add, op1=ALU.mult),
             reads=[ba], writes=[btf])
        S.op("dve", lambda e: e.tensor_copy(out=ti, in_=tf), reads=[btf], writes=[bti])
        S.op("dve", lambda e: e.tensor_copy(out=tf, in_=ti), reads=[bti], writes=[btf])
        S.op("dve", lambda e: e.scalar_tensor_tensor(out=dst, in0=tf, scalar=-TWO_PI_HI, in1=ang, op0=ALU.mult, op1=ALU.add),
             reads=[btf, ba], writes=[bd])
        S.op("dve", lambda e: e.scalar_tensor_tensor(out=dst, in0=tf, scalar=-TWO_PI_LO, in1=dst, op0=ALU.mult, op1=ALU.add),
             reads=[btf, bd], writes=[bd])
        S.op("dve", lambda e: e.tensor_scalar(out=dst, in0=dst, scalar1=off, scalar2=None, op0=ALU.add), reads=[bd], writes=[bd])
        S.op("dve", lambda e: e.tensor_scalar(out=tf, in0=dst, scalar1=math.pi, scalar2=-2.0 * math.pi, op0=ALU.is_gt, op1=ALU.mult),
             reads=[bd], writes=[btf])
        S.op("dve", lambda e: e.tensor_tensor(out=dst, in0=dst, in1=tf, op=ALU.add), reads=[bd, btf], writes=[bd])
        S.op("dve", lambda e: e.tensor_scalar(out=dst, in0=dst, scalar1=3.14159, scalar2=-3.14159, op0=ALU.min, op1=ALU.max),
             reads=[bd], writes=[bd])
        S.op("act", lambda e: e.activation(out=dst, in_=dst, func=AF.Sin), reads=[bd], writes=[bd])

    def phase0():
        st = contextlib.ExitStack()
        bc = tile(st, "bc", [128, 2048])
        g = "ld0"
        S.op("sp", lambda e: e.dma_start(out=sm.t[:], in_=smin), writes=[sm.b], dma=g)
        S.op("sp", lambda e: e.dma_start(out=cf.t[:], in_=cfin), writes=[cf.b], dma=g)
        S.op("sp", lambda e: e.dma_start(out=cb.t[:], in_=cbin), writes=[cb.b], dma=g)
        ev = S.op("sp", lambda e: e.dma_start(out=bc.t[:], in_=bcin), writes=[bc.b], dma=g)
        for tl in (sm, cf, cb, bc):
            tl.b.w = ev
        r3 = tile(st, "r3", [128, 3, 2048])
        S.op("sp", lambda e: e.dma_start(out=r3.t[:], in_=rep3), writes=[r3.b], dma="ld_r3")
        bl = tile(st, "bl", [128, 2, 2048])
        S.op("sp", lambda e: e.dma_start(out=bl.t[:], in_=BTlay), writes=[bl.b], dma="ld_bl")
        S.op("dve", lambda e: e.memset(ones_bf.t[:], 1.0), writes=[ones_bf.b])
        S.op("dve", lambda e: e.memset(ones_f.t[:], 1.0), writes=[ones_f.b])
        S.op("dve", lambda e: e.memset(epsc.t[:], EPS), writes=[epsc.b])
        S.op("dve", lambda e: e.tensor_copy(out=qkg.t[:], in_=bc.t[:, 1024:2048]), reads=[bc.b], writes=[qkg.b])
        sc = tile(st, "sc", [128, 8, 2])
        sg = tile(st, "sg", [128, 16])
        screp = tile(st, "screp", [128, 16, 128])
        c2 = smv("c2T")
        S.op("act", lambda e: e.activation(out=sg.t[:], in_=c2, func=AF.Sigmoid), reads=[sm.b], writes=[sg.b])
        scf = sc.t[:].rearrange("p k t -> p (k t)")
        S.op("dve", lambda e: e.tensor_tensor(out=scf, in0=c2, in1=sg.t[:], op=ALU.mult), reads=[sg.b, sm.b], writes=[sc.b])
        for i in range(16):
            S.op("dve", lambda e, i=i: e.tensor_scalar(out=screp.t[:, i, :], in0=ones_f.t[:], scalar1=scf[:, i:i + 1],
                                                     scalar2=None, op0=ALU.mult),
                 reads=[sc.b, ones_f.b], writes=[screp.b])
        wa = [tile(st, "wa%d" % i, [128, 8, 512]) for i in range(2)]
        wa_b2 = [Buf(), Buf()]
        modps = PS[0]
        for pc in range(6):
            w = wa[pc % 2]
            wb2 = wa_b2[pc % 2]
            S.op("sp", lambda e, w=w, pc=pc: e.dma_start(out=w.t[:, 0:4, :], in_=w_ada[:, 0:4, pc * 512:(pc + 1) * 512]),
                 writes=[w.b], dma="wa%d" % (pc % 2))
            S.op("pool", lambda e, w=w, pc=pc: e.dma_start(out=w.t[:, 4:8, :], in_=w_ada[:, 4:8, pc * 512:(pc + 1) * 512]),
                 writes=[wb2], dma="wab%d" % (pc % 2))
            if pc < 4:
                for jj in range(4):
                    j = pc * 4 + jj
                    for k in range(8):
                        S.op("pe", lambda e, w=w, jj=jj, j=j, k=k: e.matmul(
                            modps.t[:, 2 * j:2 * j + 2], lhsT=w.t[:, k, jj * 128:(jj + 1) * 128], rhs=sc.t[:, k, :],
                            start=(k == 0), stop=(k == 7)), reads=[w.b, wb2, sc.b], writes=[modps.b])
            else:
                half = pc - 4
                for pi in range(2):
                    ps = PS[1 + pi * 2 + half]
                    for k in range(8):
                        S.op("pe", lambda e, w=w, ps=ps, k=k, pi=pi: e.matmul(
                            ps.t[:, :], lhsT=screp.t[:, k * 2 + pi, :], rhs=w.t[:, k, :],
                            start=(k == 0), stop=(k == 7)), reads=[w.b, wb2, screp.b], writes=[ps.b])
                    S.op("dve", lambda e, ps=ps, pi=pi, half=half: e.tensor_tensor(
                        out=gate_bc.t[:, pi, half * 512:(half + 1) * 512], in0=ps.t[:, :],
                        in1=bc.t[:, half * 512:(half + 1) * 512], op=ALU.add), reads=[ps.b, bc.b], writes=[gate_bc.b])
        S.op("dve", lambda e: e.tensor_tensor(
            out=modb.t[:], in0=modps.t[:, 0:32].rearrange("p (j t) -> p j t", t=2),
            in1=smv("b_adaT").unsqueeze(2).to_broadcast([128, 16, 2]), op=ALU.add),
            reads=[modps.b, sm.b], writes=[modb.b])
        S.op("dve", lambda e: e.scalar_tensor_tensor(
            out=gsT.t[:], in0=modb.t[:, 8:16, :], scalar=1.0,
            in1=smv("norm_gT").unsqueeze(2).to_broadcast([128, 8, 2]), op0=ALU.add, op1=ALU.mult),
            reads=[modb.b, sm.b], writes=[gsT.b])
        g2 = tile(st, "g2", [128, 128])
        m2 = tile(st, "m2", [128, 2])
        S.op("dve", lambda e: e.tensor_tensor(out=g2.t[:, 0:64], in0=qkg.t[:, 0:64], in1=qkg.t[:, 0:64], op=ALU.mult),
             reads=[qkg.b], writes=[g2.b])
        S.op("dve", lambda e: e.tensor_tensor(out=g2.t[:, 64:128], in0=qkg.t[:, 512:576], in1=qkg.t[:, 512:576], op=ALU.mult),
             reads=[qkg.b], writes=[g2.b])
        S.op("dve", lambda e: e.reduce_max(out=m2.t[:, 0:1], in_=g2.t[:, 0:64], axis=AX.X), reads=[g2.b], writes=[m2.b])
        S.op("dve", lambda e: e.reduce_max(out=m2.t[:, 1:2], in_=g2.t[:, 64:128], axis=AX.X), reads=[g2.b], writes=[m2.b])
        S.op("dve", lambda e: e.tensor_tensor(out=m2.t[:, 0:1], in0=m2.t[:, 0:1], in1=m2.t[:, 1:2], op=ALU.mult),
             reads=[m2.b], writes=[m2.b])
        S.op("act", lambda e: e.activation(out=m2.t[:, 0:1], in_=m2.t[:, 0:1], func=AF.Ln), reads=[m2.b], writes=[m2.b])
        S.op("act", lambda e: e.activation(out=m2.t[:, 0:1], in_=m2.t[:, 0:1], func=AF.Exp, scale=0.5), reads=[m2.b], writes=[m2.b])
        S.op("dve", lambda e: e.tensor_scalar(out=m2.t[:, 0:1], in0=m2.t[:, 0:1], scalar1=-8.0, scalar2=None, op0=ALU.mult),
             reads=[m2.b], writes=[m2.b])
        S.op("dve", lambda e: e.tensor_copy(out=negb8.t[:], in_=m2.t[:, 0:1].to_broadcast([128, 8])),
             reads=[m2.b], writes=[negb8.b])
        dtT = tile(st, "dtT", [128, 16])
        S.op("act", lambda e: e.activation(out=dtT.t[:], in_=smv("logdtT"), func=AF.Exp), reads=[sm.b], writes=[dtT.b])
        S.op("dve", lambda e: e.tensor_tensor(out=theta.t[:], in0=smv("a_imT"), in1=dtT.t[:], op=ALU.mult),
             reads=[sm.b, dtT.b], writes=[theta.b])
        S.op("dve", lambda e: e.tensor_tensor(out=rho.t[:], in0=smv("a_reT"), in1=dtT.t[:], op=ALU.mult),
             reads=[sm.b, dtT.b], writes=[rho.b])
        S.op("act", lambda e: e.activation(out=rho.t[:], in_=rho.t[:], func=AF.Exp), reads=[rho.b], writes=[rho.b])
        tt = [tile(st, "tt%d" % i, [128, 2048]) for i in range(7)]
        ar, ai, ldt = r3.t[:, 0, :], r3.t[:, 1, :], r3.t[:, 2, :]
        dtr, mag, ang, c1, s1, den, t6 = tt
        TWO_PI = 2.0 * math.pi
        negpi = tile(st, "negpi", [128, 1])
        S.op("dve", lambda e: e.memset(negpi.t[:], -math.pi), writes=[negpi.b])
        S.op("act", lambda e: e.activation(out=dtr.t[:], in_=ldt, func=AF.Exp), reads=[r3.b], writes=[dtr.b])
        S.op("dve", lambda e: e.tensor_tensor(out=mag.t[:], in0=ar, in1=dtr.t[:], op=ALU.mult), reads=[r3.b, dtr.b], writes=[mag.b])
        S.op("act", lambda e: e.activation(out=mag.t[:], in_=mag.t[:], func=AF.Exp), reads=[mag.b], writes=[mag.b])
        S.op("dve", lambda e: e.tensor_tensor(out=ang.t[:], in0=ai, in1=dtr.t[:], op=ALU.mult), reads=[r3.b, dtr.b], writes=[ang.b])
        tiI = tile(st, "tiI", [128, 2048], mybir.dt.int32)
        range_sin(s1.t[:], ang.t[:], 0.0, den.t[:], tiI.t[:], s1.b, ang.b, den.b, tiI.b)
        range_sin(c1.t[:], ang.t[:], 0.5 * math.pi, den.t[:], tiI.t[:], c1.b, ang.b, den.b, tiI.b)
        S.op("dve", lambda e: e.tensor_tensor(out=c1.t[:], in0=c1.t[:], in1=mag.t[:], op=ALU.mult), reads=[c1.b, mag.b], writes=[c1.b])
        S.op("dve", lambda e: e.tensor_scalar(out=c1.t[:], in0=c1.t[:], scalar1=-1.0, scalar2=None, op0=ALU.add), reads=[c1.b], writes=[c1.b])
        S.op("dve", lambda e: e.tensor_tensor(out=s1.t[:], in0=s1.t[:], in1=mag.t[:], op=ALU.mult), reads=[s1.b, mag.b], writes=[s1.b])
        S.op("dve", lambda e: e.tensor_tensor(out=den.t[:], in0=ar, in1=ar, op=ALU.mult), reads=[r3.b], writes=[den.b])
        S.op("dve", lambda e: e.tensor_tensor(out=t6.t[:], in0=ai, in1=ai, op=ALU.mult), reads=[r3.b], writes=[t6.b])
        S.op("dve", lambda e: e.tensor_tensor(out=den.t[:], in0=den.t[:], in1=t6.t[:], op=ALU.add), reads=[den.b, t6.b], writes=[den.b])
        S.op("dve", lambda e: e.reciprocal(out=den.t[:], in_=den.t[:]), reads=[den.b], writes=[den.b])
        S.op("dve", lambda e: e.tensor_tensor(out=mag.t[:], in0=c1.t[:], in1=ar, op=ALU.mult), reads=[c1.b, r3.b], writes=[mag.b])
        S.op("dve", lambda e: e.tensor_tensor(out=t6.t[:], in0=s1.t[:], in1=ai, op=ALU.mult), reads=[s1.b, r3.b], writes=[t6.b])
        S.op("dve", lambda e: e.tensor_tensor(out=mag.t[:], in0=mag.t[:], in1=t6.t[:], op=ALU.add), reads=[mag.b, t6.b], writes=[mag.b])
        S.op("dve", lambda e: e.tensor_tensor(out=mag.t[:], in0=mag.t[:], in1=den.t[:], op=ALU.mult), reads=[mag.b, den.b], writes=[mag.b])
        S.op("dve", lambda e: e.tensor_tensor(out=ang.t[:], in0=s1.t[:], in1=ar, op=ALU.mult), reads=[s1.b, r3.b], writes=[ang.b])
        S.op("dve", lambda e: e.tensor_tensor(out=t6.t[:], in0=c1.t[:], in1=ai, op=ALU.mult), reads=[c1.b, r3.b], writes=[t6.b])
        S.op("dve", lambda e: e.tensor_tensor(out=ang.t[:], in0=ang.t[:], in1=t6.t[:], op=ALU.subtract), reads=[ang.b, t6.b], writes=[ang.b])
        S.op("dve", lambda e: e.tensor_tensor(out=ang.t[:], in0=ang.t[:], in1=den.t[:], op=ALU.mult), reads=[ang.b, den.b], writes=[ang.b])
        qre, qim = mag, ang
        S.op("dve", lambda e: e.tensor_tensor(out=c1.t[:], in0=qre.t[:], in1=bl.t[:, 0, :], op=ALU.mult), reads=[qre.b, bl.b], writes=[c1.b])
        S.op("dve", lambda e: e.tensor_tensor(out=s1.t[:], in0=qim.t[:], in1=bl.t[:, 1, :], op=ALU.mult), reads=[qim.b, bl.b], writes=[s1.b])
        S.op("dve", lambda e: e.tensor_tensor(out=BT.t[:, 0, :], in0=c1.t[:], in1=s1.t[:], op=ALU.subtract), reads=[c1.b, s1.b], writes=[BT.b])
        S.op("dve", lambda e: e.tensor_tensor(out=c1.t[:], in0=qre.t[:], in1=bl.t[:, 1, :], op=ALU.mult), reads=[qre.b, bl.b], writes=[c1.b])
        S.op("dve", lambda e: e.tensor_tensor(out=s1.t[:], in0=qim.t[:], in1=bl.t[:, 0, :], op=ALU.mult), reads=[qim.b, bl.b], writes=[s1.b])
        S.op("dve", lambda e: e.tensor_tensor(out=BT.t[:, 1, :], in0=c1.t[:], in1=s1.t[:], op=ALU.add), reads=[c1.b, s1.b], writes=[BT.b])
        S.op("sp", lambda e: e.dma_start(out=bl.t[:], in_=CTlay), writes=[bl.b], dma="ld_bl")
        S.op("act", lambda e: e.copy(out=CT.t[:, 0, :], in_=bl.t[:, 0, :]), reads=[bl.b], writes=[CT.b])
        S.op("act", lambda e: e.mul(out=CT.t[:, 1, :], in_=bl.t[:, 1, :], mul=-1.0), reads=[bl.b], writes=[CT.b])
        S.op("act", lambda e: e.mul(out=CT.t[:, 2, :], in_=bl.t[:, 0, :], mul=-1.0), reads=[bl.b], writes=[CT.b])
        S.barrier()
        st.close()

    TAB = {}

    def alloc_tables(stk):
        TAB["cosT"] = tile(stk, "cosT", [128, 16, 512], BF)
        TAB["sinT"] = tile(stk, "sinT", [128, 16, 512], BF)
        TAB["lastc"] = {"p": tile(stk, "lastc_p", [128, 3, 16]), "s": tile(stk, "lastc_s", [128, 3, 16])}

    def table_ops(st3):
        cosT, sinT, lastc = TAB["cosT"], TAB["sinT"], TAB["lastc"]
        angT = tile(st3, "angT", [128, 8, 512])
        tfT = tile(st3, "tfT", [128, 8, 512])
        tiT = tile(st3, "tiT", [128, 8, 512], mybir.dt.int32)
        d32 = tile(st3, "d32", [128, 8, 512])
        fl = lambda a_: a_.rearrange("p j t -> p (j t)")
        S.begin_capture()
        for hf in range(2):
            for jj in range(8):
                j = hf * 8 + jj
                S.op("dve", lambda e, j=j, jj=jj: e.tensor_scalar(out=angT.t[:, jj, :], in0=iota_t1, scalar1=theta.t[:, j:j + 1], scalar2=None,
                                                              op0=ALU.mult), reads=[cf.b, theta.b], writes=[angT.b])
            for which, tab, shift in ((1, sinT, 0.0), (0, cosT, 0.5 * math.pi)):
                range_sin(fl(d32.t[:]), fl(angT.t[:]), shift, fl(tfT.t[:]), fl(tiT.t[:]), d32.b, angT.b, tfT.b, tiT.b)
                S.op("act", lambda e, tab=tab, hf=hf: e.copy(out=tab.t[:, hf * 8:(hf + 1) * 8, :], in_=d32.t[:]), reads=[d32.b], writes=[tab.b])
                for pk, Lc in (("p", 512), ("s", 64)):
                    S.op("dve", lambda e, pk=pk, Lc=Lc, which=which, hf=hf: e.tensor_copy(
                        out=lastc[pk].t[:, which, hf * 8:(hf + 1) * 8], in_=d32.t[:, :, Lc - 1]), reads=[d32.b], writes=[lastc[pk].b])
        for pk in ("p", "s"):
            S.op("dve", lambda e, pk=pk: e.tensor_scalar(out=lastc[pk].t[:, 2, :], in0=lastc[pk].t[:, 1, :], scalar1=-1.0, scalar2=None,
                                                       op0=ALU.mult), reads=[lastc[pk].b], writes=[lastc[pk].b])
        return S.end_capture()

    def phaseN():
        st = contextlib.ExitStack()
        NX = 4
        xt = [tile(st, "nx%d" % i, [128, D]) for i in range(NX)]
        sqj = tile(st, "nsq", [128, D], BF)
        sst = [tile(st, "nss%d" % i, [128, 2]) for i in range(NX)]
        xst = [tile(st, "nxs%d" % i, [128, D], BF) for i in range(NX)]
        hTn = [tile(st, "n_hT%d" % i, [128, 8, 512], BF) for i in range(2)]
        tps = [PS[6], PS[7]]
        tiles = []
        for pk in ("s", "p"):
            P = PATHS[pk]
            for s0 in range(0, P["T"], P["SP"]):
                for ti in range(P["SP"] // P["TT"]):
                    tiles.append((pk, s0, ti))
        spans = {}
        for (pk, s0, ti) in tiles:
            spans.setdefault((pk, s0), len(spans))

        def load(i):
            if i >= len(tiles):
                return
            pk, s0, ti = tiles[i]
            TT = PATHS[pk]["TT"]
            t0 = s0 + ti * TT
            x = xt[i % NX]
            S.op("sp", lambda e: e.dma_start(out=x.t[0:TT, :], in_=xin[pk][t0:t0 + TT, :]), writes=[x.b], dma="nx%d" % (i % NX))

        def chain(i):
            if i >= len(tiles):
                return
            pk, s0, ti = tiles[i]
            TT = PATHS[pk]["TT"]
            x, xs, ss = xt[i % NX], xst[i % NX], sst[i % NX]
            S.op("act", lambda e: e.activation(out=sqj.t[0:TT, :], in_=x.t[0:TT, :], func=AF.Square, accum_out=ss.t[0:TT, 0:1]),
                 reads=[x.b], writes=[sqj.b, ss.b])
            S.op("act", lambda e: e.activation(out=ss.t[0:TT, 1:2], in_=ss.t[0:TT, 0:1], func=AF.Ln, scale=1.0 / D, bias=epsc.t[0:TT, 0:1]),
                 reads=[ss.b, epsc.b], writes=[ss.b])
            S.op("act", lambda e: e.activation(out=ss.t[0:TT, 1:2], in_=ss.t[0:TT, 1:2], func=AF.Exp, scale=-0.5), reads=[ss.b], writes=[ss.b])
            S.op("dve", lambda e: e.tensor_scalar(out=xs.t[0:TT, :], in0=x.t[0:TT, :], scalar1=ss.t[0:TT, 1:2], scalar2=None, op0=ALU.mult),
                 reads=[x.b, ss.b], writes=[xs.b])

        def transp(i):
            pk, s0, ti = tiles[i]
            P = PATHS[pk]
            TT, pi, SPn = P["TT"], P["pi"], P["SP"]
            xs = xst[i % NX]
            tp = tps[i % 2]
            tpb = tp.t[:, :].bitcast(BF)
            hT = hTn[spans[(pk, s0)] % 2]
            for k in range(8):
                S.op("pe", lambda e, k=k: e.transpose(out=tpb[:, k * 128:k * 128 + TT], in_=xs.t[0:TT, k * 128:(k + 1) * 128],
                                                      identity=ident_bf[0:TT, 0:TT]), reads=[xs.b, cb.b], writes=[tp.b])
            for k in range(8):
                if k % 2 == 0:
                    S.op("act", lambda e, k=k: e.activation(
                        out=hT.t[:, k, ti * TT:(ti + 1) * TT], in_=tpb[:, k * 128:k * 128 + TT], func=AF.Identity,
                        scale=gsT.t[:, k, pi:pi + 1], bias=modb.t[:, k, pi:pi + 1]), reads=[tp.b, gsT.b, modb.b], writes=[hT.b])
                else:
                    S.op("dve", lambda e, k=k: e.tensor_scalar(
                        out=hT.t[:, k, ti * TT:(ti + 1) * TT], in0=tpb[:, k * 128:k * 128 + TT],
                        scalar1=gsT.t[:, k, pi:pi + 1], scalar2=modb.t[:, k, pi:pi + 1], op0=ALU.mult, op1=ALU.add),
                        reads=[tp.b, gsT.b, modb.b], writes=[hT.b])
            if ti == SPn // TT - 1:
                S.op("pool", lambda e: e.dma_start(out=SC[pk]["hT"][:, s0:s0 + SPn].rearrange("(k p) t -> p k t", p=128),
                                                   in_=hT.t[:, :, 0:SPn]), reads=[hT.b], dma="hTsp%d" % (spans[(pk, s0)] % 2))

        tab_l = table_ops(st) if TAB else []
        per = (len(tab_l) + len(tiles) - 1) // len(tiles)
        for i in range(NX - 1):
            load(i)
        chain(0)
        for i in range(len(tiles)):
            load(i + NX - 1)
            chain(i + 1)
            transp(i)
            S.replay(tab_l[i * per:(i + 1) * per])
        S.replay(tab_l[len(tiles) * per:])
        S.barrier()
        st.close()

    def load_w_bf(wt, cols0, ncols, stage, wname):
        c = 0
        i = 0
        while c < ncols:
            n = min(512, ncols - c)
            sg_ = stage[i % 2]
            S.op("sp", lambda e, sg_=sg_, c=c, n=n: e.dma_start(out=sg_.t[:, :, 0:n], in_=w_in[:, :, cols0 + c:cols0 + c + n]),
                 writes=[sg_.b], dma="%s%d" % (wname, i % 2))
            eng = "act" if i % 2 == 0 else "dve"
            if eng == "act":
                S.op("act", lambda e, sg_=sg_, c=c, n=n: e.copy(out=wt.t[:, :, c:c + n], in_=sg_.t[:, :, 0:n]),
                     reads=[sg_.b], writes=[wt.b])
            else:
                S.op("dve", lambda e, sg_=sg_, c=c, n=n: e.tensor_copy(out=wt.t[:, :, c:c + n], in_=sg_.t[:, :, 0:n]),
                     reads=[sg_.b], writes=[wt.b])
            c += n
            i += 1
        return wt

    def phaseA():
        st = contextlib.ExitStack()
        wA = tile(st, "wA", [128, 8, 2056], BF)
        st2 = contextlib.ExitStack()
        stage = [tile(st2, "wstg%d" % i, [128, 8, 512]) for i in range(2)]
        load_w_bf(wA, 1024, 2056, stage, "wsA")
        S.barrier()
        st2.close()
        hTa = [tile(st, "a_hT%d" % i, [128, 8, 512], BF) for i in range(2)]
        NSET = 2
        WS = []
        for i in range(NSET):
            d = {}
            for nm in ("qf", "kf", "sq", "tmpn", "ko", "vo"):
                d[nm] = tile(st, "a_%s%d" % (nm, i), [128, 512])
            d["st8"] = tile(st, "a_st8%d" % i, [128, 4, 8])
            d["lft"] = tile(st, "a_lft%d" % i, [128, 8])
            d["lf"] = tile(st, "a_lf%d" % i, [128, 8])
            d["cum"] = tile(st, "a_cum%d" % i, [128, 8])
            d["q_aug"] = tile(st, "a_qaug%d" % i, [128, 8, 65], BF)
            d["k_aug"] = tile(st, "a_kaug%d" % i, [128, 8, 65], BF)
            d["v_aug"] = tile(st, "a_vaug%d" % i, [128, 8, 128], BF)
            d["i"] = i
            WS.append(d)
            S.op("dve", lambda e, d=d: e.memset(d["k_aug"].t[:, :, 64:65], 1.0), writes=[d["k_aug"].b])
            va = d["v_aug"]
            S.op("dve", lambda e, va=va: e.memset(va.t[:], 0.0), writes=[va.b])
            S.op("dve", lambda e, va=va: e.memset(va.t[:, 0::2, 64:65], 1.0), writes=[va.b])
            S.op("dve", lambda e, va=va: e.memset(va.t[:, 1::2, 0:1], 1.0), writes=[va.b])
        qst = [tile(st, "a_qst%d" % i, [65, 8, 512], BF) for i in range(2)]
        kst = [tile(st, "a_kst%d" % i, [65, 8, 512], BF) for i in range(2)]
        zsg = tile(st, "a_zsg", [128, 512])
        zaS = [tile(st, "a_zaS%d" % i, [128, 4, 512], BF) for i in range(2)]
        qps, kps, vps, fps, qTps, kTps = PS[0], PS[1], PS[2], PS[3], PS[4], PS[5]
        qTb = qTps.t[:, :].bitcast(BF)
        kTb = kTps.t[:, :].bitcast(BF)
        cnt = [0]
        state = dict(cum_prev=None, cum_TT=None)

        def h3(ap_):
            return ap_.rearrange("p (h d) -> p h d", d=64)

        def evac_tile(TT, W):
            S.op("act", lambda e: e.copy(out=W["qf"].t[0:TT, :], in_=qps.t[0:TT, :]), reads=[qps.b], writes=[W["qf"].b])
            S.op("act", lambda e: e.copy(out=W["kf"].t[0:TT, :], in_=kps.t[0:TT, :]), reads=[kps.b], writes=[W["kf"].b])
            S.op("act", lambda e: e.copy(out=W["vo"].t[0:TT, :], in_=vps.t[0:TT, :]), reads=[vps.b], writes=[W["vo"].b])
            S.op("dve", lambda e: e.tensor_tensor(out=W["lft"].t[0:TT, :], in0=fps.t[0:TT, 0:8], in1=smv("bf_bc")[0:TT, :], op=ALU.add),
                 reads=[fps.b, sm.b], writes=[W["lft"].b])

        def kv_tile(pk, blk, TT, tok0, col0, from_cache, kstg, qstg, W, part="abc"):
            P = PATHS[pk]
            c = W["i"]
            ko, vo, lf_, cm, va = W["ko"], W["vo"], W["lf"], W["cum"], W["v_aug"]
            q_aug, k_aug, sq, st8, tmpn, lft = W["q_aug"], W["k_aug"], W["sq"], W["st8"], W["tmpn"], W["lft"]
            if "a" not in part:
                pass
            elif from_cache:
                ck_, cv_ = W["kf"], W["vo"]
                S.op("sp", lambda e: e.dma_start(out=ck_.t[0:TT, :], in_=ckin[tok0:tok0 + TT, :]), writes=[ck_.b], dma="ck%d" % c)
                S.op("sp", lambda e: e.dma_start(out=cv_.t[0:TT, :], in_=cvin[tok0:tok0 + TT, :]), writes=[cv_.b], dma="cv%d" % c)
                S.op("sp", lambda e: e.dma_start(out=lf_.t[0:TT, :], in_=clfin[tok0:tok0 + TT, :]), writes=[lf_.b], dma="clf%d" % c)
                S.op("act", lambda e: e.copy(out=k_aug.t[0:TT, :, 0:64], in_=h3(ck_.t[0:TT, :])), reads=[ck_.b], writes=[k_aug.b])
            else:
                nq = tok0 - P["npast"]
                for which, src in ((0, W["qf"]), (1, W["kf"])):
                    S.op("act", lambda e, src=src: e.activation(out=sq.t[0:TT, :], in_=src.t[0:TT, :], func=AF.Square),
                         reads=[src.b], writes=[sq.b])
                    S.op("dve", lambda e, which=which: e.reduce_sum(out=st8.t[0:TT, 2 * which, :], in_=h3(sq.t[0:TT, :]), axis=AX.X),
                         reads=[sq.b], writes=[st8.b])
                    S.op("act", lambda e, which=which: e.activation(
                        out=st8.t[0:TT, 2 * which + 1, :], in_=st8.t[0:TT, 2 * which, :], func=AF.Ln, scale=1.0 / 64, bias=epsc.t[0:TT, 0:1]),
                        reads=[st8.b, epsc.b], writes=[st8.b])
                    S.op("act", lambda e, which=which: e.activation(
                        out=st8.t[0:TT, 2 * which + 1, :], in_=st8.t[0:TT, 2 * which + 1, :], func=AF.Exp, scale=-0.5),
                        reads=[st8.b], writes=[st8.b])
                    S.op("dve", lambda e, which=which, src=src: e.tensor_tensor(
                        out=h3(tmpn.t[0:TT, :]), in0=h3(src.t[0:TT, :]),
                        in1=st8.t[0:TT, 2 * which + 1, :].unsqueeze(2).to_broadcast([TT, 8, 64]), op=ALU.mult),
                        reads=[src.b, st8.b], writes=[tmpn.b])
                    if which == 0:
                        S.op("dve", lambda e: e.tensor_tensor(out=q_aug.t[0:TT, :, 0:64], in0=h3(tmpn.t[0:TT, :]),
                                                              in1=h3(qkg.t[0:TT, 0:512]), op=ALU.mult),
                             reads=[tmpn.b, qkg.b], writes=[q_aug.b])
                    else:
                        S.op("dve", lambda e: e.tensor_tensor(out=ko.t[0:TT, :], in0=tmpn.t[0:TT, :], in1=qkg.t[0:TT, 512:1024],
                                                              op=ALU.mult), reads=[tmpn.b, qkg.b], writes=[ko.b])
                        S.op("sp", lambda e: e.dma_start(out=O["k_" + pk][nq:nq + TT, :], in_=ko.t[0:TT, :]),
                             reads=[ko.b], dma="ko%d" % c)
                        S.op("act", lambda e: e.copy(out=k_aug.t[0:TT, :, 0:64], in_=h3(ko.t[0:TT, :])),
                             reads=[ko.b], writes=[k_aug.b])
                S.op("sp", lambda e: e.dma_start(out=O["v_" + pk][nq:nq + TT, :], in_=vo.t[0:TT, :]),
                     reads=[vo.b], dma="vo%d" % c)
                S.op("act", lambda e: e.activation(out=lft.t[0:TT, :], in_=lft.t[0:TT, :], func=AF.Exp, scale=-1.0),
                     reads=[lft.b], writes=[lft.b])
                S.op("act", lambda e: e.activation(out=lft.t[0:TT, :], in_=lft.t[0:TT, :], func=AF.Ln, bias=ones_f.t[0:TT, 0:1]),
                     reads=[lft.b, ones_f.b], writes=[lft.b])
                S.op("dve", lambda e: e.tensor_scalar(out=lf_.t[0:TT, :], in0=lft.t[0:TT, :], scalar1=-1.0, scalar2=None, op0=ALU.mult),
                     reads=[lft.b], writes=[lf_.b])
                S.op("sp", lambda e: e.dma_start(out=O["lf_" + pk][nq:nq + TT, :], in_=lf_.t[0:TT, :]),
                     reads=[lf_.b], dma="lfo%d" % c)
            v4 = h3(vo.t[0:TT, :])
            if "a" in part:
                S.op("dve", lambda e: e.tensor_copy(out=va.t[0:TT, 0::2, 0:64], in_=v4[:, 0::2, :]), reads=[vo.b], writes=[va.b])
                S.op("dve", lambda e: e.tensor_copy(out=va.t[0:TT, 1::2, 64:128], in_=v4[:, 1::2, :]), reads=[vo.b], writes=[va.b])
                S.op("pool", lambda e: e.dma_start(out=SC[pk]["va"][tok0:tok0 + TT, :, :], in_=va.t[0:TT, :, :]),
                     reads=[va.b], dma="vao%d" % c)
            if "b" in part:
                kv_tile_b(pk, blk, TT, from_cache, W)
            if "c" in part:
                kv_tile_c(TT, col0, from_cache, kstg, qstg, W)

        def kv_tile_b(pk, blk, TT, from_cache, W):
            lf_, cm, q_aug = W["lf"], W["cum"], W["q_aug"]
            cps = fps.t[0:TT, 16:24]
            prev = state["cum_prev"]
            S.op("pe", lambda e: e.matmul(cps, lhsT=tri_f[0:TT, 0:TT], rhs=lf_.t[0:TT, :], start=True, stop=(prev is None)),
                 reads=[lf_.b, cf.b], writes=[fps.b])
            if prev is not None:
                pTT = state["cum_TT"]
                sel = sel128 if pTT == 128 else sel64
                S.op("pe", lambda e: e.matmul(cps, lhsT=sel[0:pTT, 0:TT], rhs=prev.t[0:pTT, :], start=False, stop=True),
                     reads=[prev.b, cf.b], writes=[fps.b])
            S.op("act", lambda e: e.copy(out=cm.t[0:TT, :], in_=cps), reads=[fps.b], writes=[cm.b])
            state["cum_prev"], state["cum_TT"] = cm, TT
            S.op("dve", lambda e: e.scalar_tensor_tensor(out=biasK[pk].t[0:TT, blk, :], in0=cm.t[0:TT, :], scalar=-1.0,
                                                         in1=negb8.t[0:TT, :], op0=ALU.mult, op1=ALU.add),
                 reads=[cm.b, negb8.b], writes=[biasK[pk].b])
            if not from_cache:
                S.op("dve", lambda e: e.tensor_scalar(out=q_aug.t[0:TT, :, 64:65], in0=cm.t[0:TT, :].unsqueeze(2), scalar1=8.0,
                                                      scalar2=None, op0=ALU.mult), reads=[cm.b], writes=[q_aug.b])

        def kv_tile_c(TT, col0, from_cache, kstg, qstg, W):
            q_aug, k_aug = W["q_aug"], W["k_aug"]
            for h in range(8):
                S.op("pe", lambda e, h=h: e.transpose(out=kTb[0:65, h * 128:h * 128 + TT], in_=k_aug.t[0:TT, h, :],
                                                      identity=ident_bf[0:TT, 0:TT]), reads=[k_aug.b, cb.b], writes=[kTps.b])
            S.op("act", lambda e: e.copy(out=kstg.t[0:65, :, col0:col0 + TT],
                                         in_=kTb[0:65, :].rearrange("p (h t) -> p h t", t=128)[:, :, 0:TT]),
                 reads=[kTps.b], writes=[kstg.b])
            if not from_cache:
                for h in range(8):
                    S.op("pe", lambda e, h=h: e.transpose(out=qTb[0:65, h * 128:h * 128 + TT], in_=q_aug.t[0:TT, h, :],
                                                          identity=ident_bf[0:TT, 0:TT]), reads=[q_aug.b, cb.b], writes=[qTps.b])
                S.op("dve", lambda e: e.tensor_copy(out=qstg.t[0:65, :, col0:col0 + TT],
                                                    in_=qTb[0:65, :].rearrange("p (h t) -> p h t", t=128)[:, :, 0:TT]),
                     reads=[qTps.b], writes=[qstg.b])

        def load_hT(pk_, s0_, hT_):
            n_ = PATHS[pk_]["SP"]
            S.op("sp", lambda e: e.dma_start(out=hT_.t[:, :, 0:n_], in_=SC[pk_]["hT"][:, s0_:s0_ + n_].rearrange("(k p) t -> p k t", p=128)),
                 writes=[hT_.b], dma="ahT%d" % (0 if hT_ is hTa[0] else 1))

        spc = [0]
        LAG = 2
        pending = []
        pend_b = []

        def flush(n_keep):
            while len(pending) > n_keep:
                S.replay(pending.pop(0))

        for pk in ("s", "p"):
            P = PATHS[pk]
            TT, SPn, T, npast = P["TT"], P["SP"], P["T"], P["npast"]
            state["cum_prev"] = None
            for s0 in range(0, npast, 512):
                sc_ = spc[0]
                spc[0] += 1
                kstg = kst[sc_ % 2]
                for ti in range(4):
                    W = WS[cnt[0] % NSET]
                    cnt[0] += 1
                    kv_tile(pk, (s0 + ti * 128) // 128, 128, s0 + ti * 128, ti * 128, True, kstg, None, W)
                S.op("pool", lambda e, kstg=kstg, s0=s0: e.dma_start(
                    out=SC[pk]["kTa"][:, :, s0:s0 + 512].rearrange("h r t -> r h t"), in_=kstg.t[0:65, :, 0:512]),
                    reads=[kstg.b], dma="kst%d" % (sc_ % 2))
            span_list = list(range(0, T, SPn))
            load_hT(pk, span_list[0], hTa[spc[0] % 2])
            for sidx, s0 in enumerate(span_list):
                sc_ = spc[0]
                spc[0] += 1
                kstg, qstg, zas = kst[sc_ % 2], qst[sc_ % 2], zaS[sc_ % 2]
                hT = hTa[sc_ % 2]
                if sidx + 1 < len(span_list):
                    load_hT(pk, span_list[sidx + 1], hTa[(sc_ + 1) % 2])
                ntile = SPn // TT
                for ti in range(ntile):
                    cols = slice(ti * TT, (ti + 1) * TT)
                    W = WS[cnt[0] % NSET]
                    cnt[0] += 1
                    for (ps, c0, n) in ((qps, 0, 512), (kps, 512, 512), (vps, 1024, 512), (fps, 2048, 8)):
                        for k in range(8):
                            S.op("pe", lambda e, ps=ps, c0=c0, n=n, k=k, cols=cols: e.matmul(
                                ps.t[0:TT, 0:n], lhsT=hT.t[:, k, cols], rhs=wA.t[:, k, c0:c0 + n],
                                start=(k == 0), stop=(k == 7)), reads=[hT.b, wA.b], writes=[ps.b])
                    evac_tile(TT, W)
                    tok0 = npast + s0 + ti * TT
                    if pend_b:
                        S.replay(pend_b.pop(0))
                    kv_tile(pk, tok0 // 128, TT, tok0, ti * TT, False, kstg, qstg, W, part="a")
                    flush(0)
                    S.begin_capture()
                    kv_tile(pk, tok0 // 128, TT, tok0, ti * TT, False, kstg, qstg, W, part="b")
                    pend_b.append(S.end_capture())
                    S.begin_capture()
                    kv_tile(pk, tok0 // 128, TT, tok0, ti * TT, False, kstg, qstg, W, part="c")
                    lst = S.end_capture()
                    if ti == ntile - 1:
                        S.begin_capture()
                        k0 = npast + s0
                        S.op("pool", lambda e, kstg=kstg, k0=k0: e.dma_start(
                            out=SC[pk]["kTa"][:, :, k0:k0 + SPn].rearrange("h r t -> r h t"), in_=kstg.t[0:65, :, 0:SPn]),
                            reads=[kstg.b], dma="kst%d" % (sc_ % 2))
                        S.op("pool", lambda e, qstg=qstg, s0=s0: e.dma_start(
                            out=SC[pk]["qTa"][:, :, s0:s0 + SPn].rearrange("h r t -> r h t"), in_=qstg.t[0:65, :, 0:SPn]),
                            reads=[qstg.b], dma="qst%d" % (sc_ % 2))
                        lst = lst + S.end_capture()
                    pending.append(lst)
                for cch in range(4):
                    zps = PS[6]
                    for k in range(8):
                        S.op("pe", lambda e, cch=cch, k=k: e.matmul(
                            zps.t[:, 0:SPn], lhsT=wA.t[:, k, 1536 + cch * 128:1536 + (cch + 1) * 128], rhs=hT.t[:, k, 0:SPn],
                            start=(k == 0), stop=(k == 7)), reads=[hT.b, wA.b], writes=[zps.b])
                    S.op("act", lambda e, cch=cch, zas=zas: e.activation(out=zas.t[:, cch, 0:SPn], in_=zps.t[:, 0:SPn], func=AF.Silu),
                         reads=[zps.b], writes=[zas.b])
                S.op("pool", lambda e, zas=zas, s0=s0: e.dma_start(
                    out=SC[pk]["zaG"][:, s0:s0 + SPn].rearrange("(c p) t -> p c t", p=128), in_=zas.t[:, :, 0:SPn]),
                    reads=[zas.b], dma="zas%d" % (sc_ % 2))
            while pend_b:
                S.replay(pend_b.pop(0))
            flush(0)
        S.barrier()
        st.close()

    def phaseB():
        st = contextlib.ExitStack()
        wB = tile(st, "wB", [128, 8, 1024], BF)
        wg = tile(st, "wg", [128, 4, 512], BF)
        cosT, sinT, lastc = TAB["cosT"], TAB["sinT"], TAB["lastc"]
        st2 = contextlib.ExitStack()
        stage = [tile(st2, "wstgB%d" % i, [128, 8, 512]) for i in range(2)]
        wgs = tile(st2, "wgs", [128, 4, 512])
        load_w_bf(wB, 0, 1024, stage, "wsB")
        S.op("sp", lambda e: e.dma_start(out=wgs.t[:], in_=w_glu), writes=[wgs.b], dma="ld_wg")
        S.op("act", lambda e: e.copy(out=wg.t[:], in_=wgs.t[:]), reads=[wgs.b], writes=[wg.b])
        S.barrier()
        st2.close()
        hTb = [tile(st, "b_hT%d" % i, [128, 8, 512], BF) for i in range(2)]
        uTs = [tile(st, "uT%d" % i, [128, 4, 512], BF) for i in range(2)]
        zsGs = [tile(st, "zsG%d" % i, [128, 4, 512], BF) for i in range(2)]
        zsg = tile(st, "b_zsg", [128, 512])
        bX = [[{nm: tile(st, "b_%s_%d%d" % (nm, par, tl_), [128, 512], BF) for nm in ("brb", "bib")} for tl_ in range(2)] for par in range(2)]
        NSET = 2
        TS = []
        for i in range(NSET):
            d = {}
            for nm in ("brb", "bib", "t1", "t2", "t3", "t4", "wre", "wim", "zrb", "zib", "p1", "p2", "p3", "p4"):
                d[nm] = tile(st, "b_%s%d" % (nm, i), [128, 512], BF)
            for nm in ("zre", "zim"):
                d[nm] = tile(st, "b_%s%d" % (nm, i), [128, 512])
            d["ctmp"] = tile(st, "b_ctmp%d" % i, [128, 2])
            d["bre_ps"], d["bim_ps"] = PS[1 + 2 * i], PS[2 + 2 * i]
            TS.append(d)
        hst = tile(st, "b_hst", [128, 32])
        hb = [Buf() for _ in range(16)]
        yT = tile(st, "b_yT", [128, 512])
        x2 = tile(st, "b_x2", [128, 512])
        gTs = [tile(st, "b_gT%d" % i, [128, 4, 512], BF) for i in range(2)]
        carry_bg = []
        sgl = tile(st, "b_sgl", [128, 512])
        y2 = tile(st, "b_y2", [128, 512])
        yS = [tile(st, "b_yS%d" % i, [128, 4, 512], BF) for i in range(2)]
        ups, yps, gps = PS[0], PS[5], PS[6]
        spc = [0]

        def tile_ops(pk, L, j, cq, jj, d, uT):
            cs, sn = cosT.t[:, j, 0:L], sinT.t[:, j, 0:L]
            lc = lastc[pk]
            ops_ = []
            A = ops_.append
            A(lambda: S.op("pe", lambda e: e.matmul(d["bre_ps"].t[:, 0:L], lhsT=BT.t[:, 0, j * 128:(j + 1) * 128], rhs=uT.t[:, cq, 0:L],
                                                    start=True, stop=True), reads=[BT.b, uT.b], writes=[d["bre_ps"].b]))
            A(lambda: S.op("pe", lambda e: e.matmul(d["bim_ps"].t[:, 0:L], lhsT=BT.t[:, 1, j * 128:(j + 1) * 128], rhs=uT.t[:, cq, 0:L],
                                                    start=True, stop=True), reads=[BT.b, uT.b], writes=[d["bim_ps"].b]))
            A(lambda: S.op("act", lambda e: e.copy(out=d["brb"].t[:, 0:L], in_=d["bre_ps"].t[:, 0:L]), reads=[d["bre_ps"].b], writes=[d["brb"].b]))
            A(lambda: S.op("act", lambda e: e.copy(out=d["bib"].t[:, 0:L], in_=d["bim_ps"].t[:, 0:L]), reads=[d["bim_ps"].b], writes=[d["bib"].b]))

            def tt(o, a_, b_, op, tab=None):
                rd = [d[a_].b] + ([d[b_].b] if isinstance(b_, str) else [tab.b])
                bb = d[b_].t[:, 0:L] if isinstance(b_, str) else b_
                return lambda: S.op("dve", lambda e: e.tensor_tensor(out=d[o].t[:, 0:L], in0=d[a_].t[:, 0:L], in1=bb, op=op),
                                    reads=rd, writes=[d[o].b])
            A(tt("t1", "brb", cs, ALU.mult, cosT))
            A(tt("t2", "bib", sn, ALU.mult, sinT))
            A(tt("wre", "t1", "t2", ALU.add))
            A(tt("t3", "bib", cs, ALU.mult, cosT))
            A(tt("t4", "brb", sn, ALU.mult, sinT))
            A(tt("wim", "t3", "t4", ALU.subtract))
            rb = rho.t[:, j:j + 1].to_broadcast([128, L])
            A(lambda: S.op("dve", lambda e: e.tensor_tensor_scan(out=d["zre"].t[:, 0:L], data0=rb, data1=d["wre"].t[:, 0:L],
                                                                 initial=hst.t[:, j:j + 1], op0=ALU.mult, op1=ALU.add),
                           reads=[d["wre"].b, hb[j], rho.b], writes=[d["zre"].b]))
            A(lambda: S.op("dve", lambda e: e.tensor_tensor_scan(out=d["zim"].t[:, 0:L], data0=rb, data1=d["wim"].t[:, 0:L],
                                                                 initial=hst.t[:, 16 + j:17 + j], op0=ALU.mult, op1=ALU.add),
                           reads=[d["wim"].b, hb[j], rho.b], writes=[d["zim"].b]))
            A(lambda: S.op("act", lambda e: e.copy(out=d["zrb"].t[:, 0:L], in_=d["zre"].t[:, 0:L]), reads=[d["zre"].b], writes=[d["zrb"].b]))
            A(lambda: S.op("act", lambda e: e.copy(out=d["zib"].t[:, 0:L], in_=d["zim"].t[:, 0:L]), reads=[d["zim"].b], writes=[d["zib"].b]))
            ct = d["ctmp"]
            A(lambda: S.op("act", lambda e: e.activation(out=ct.t[:, 0:1], in_=d["zim"].t[:, L - 1:L], func=AF.Identity,
                                                         scale=lc.t[:, 2, j:j + 1]), reads=[d["zim"].b, lc.b], writes=[ct.b]))
            A(lambda: S.op("act", lambda e: e.activation(out=ct.t[:, 1:2], in_=d["zim"].t[:, L - 1:L], func=AF.Identity,
                                                         scale=lc.t[:, 0, j:j + 1]), reads=[d["zim"].b, lc.b], writes=[ct.b]))
            A(lambda: S.op("act", lambda e: e.activation(out=hst.t[:, j:j + 1], in_=d["zre"].t[:, L - 1:L], func=AF.Identity,
                                                         scale=lc.t[:, 0, j:j + 1], bias=ct.t[:, 0:1]),
                           reads=[d["zre"].b, lc.b, ct.b], writes=[hb[j]]))
            A(lambda: S.op("act", lambda e: e.activation(out=hst.t[:, 16 + j:17 + j], in_=d["zre"].t[:, L - 1:L], func=AF.Identity,
                                                         scale=lc.t[:, 1, j:j + 1], bias=ct.t[:, 1:2]),
                           reads=[d["zre"].b, lc.b, ct.b], writes=[hb[j]]))
            A(tt("p1", "zrb", cs, ALU.mult, cosT))
            A(tt("p2", "zib", sn, ALU.mult, sinT))
            A(tt("p3", "zrb", sn, ALU.mult, sinT))
            A(tt("p4", "zib", cs, ALU.mult, cosT))
            for q_, (pl, nm) in enumerate(((0, "p1"), (2, "p2"), (1, "p3"), (1, "p4"))):
                A(lambda pl=pl, nm=nm, q_=q_: S.op("pe", lambda e: e.matmul(
                    yps.t[:, 0:L], lhsT=CT.t[:, pl, j * 128:(j + 1) * 128], rhs=d[nm].t[:, 0:L],
                    start=(jj == 0 and q_ == 0), stop=(jj == 3 and q_ == 3)), reads=[CT.b, d[nm].b], writes=[yps.b]))
            return ops_

        for pk in ("s", "p"):
            P = PATHS[pk]
            SPn, T = P["SP"], P["T"]
            L = SPn
            if pk == "s":
                S.op("dve", lambda e: e.tensor_copy(out=hst.t[:, 0:16], in_=smv("stT_re")), reads=[sm.b] + hb, writes=hb)
                S.op("dve", lambda e: e.tensor_copy(out=hst.t[:, 16:32], in_=smv("stT_im")), reads=[sm.b] + hb, writes=hb)
            else:
                S.op("dve", lambda e: e.memset(hst.t[:], 0.0), reads=hb, writes=hb)
            def prologue(s0, uT, zsG):
                hT = hTb[(s0 // SPn) % 2]
                S.begin_capture()
                S.op("sp", lambda e: e.dma_start(out=hT.t[:, :, 0:L], in_=SC[pk]["hT"][:, s0:s0 + L].rearrange("(k p) t -> p k t", p=128)),
                     writes=[hT.b], dma="bhT%d" % ((s0 // SPn) % 2))
                norm_l = S.end_capture()
                groups = []
                for sec in (0, 1):
                    for cch in range(4):
                        S.begin_capture()
                        for k in range(8):
                            S.op("pe", lambda e, sec=sec, cch=cch, k=k: e.matmul(
                                ups.t[:, 0:L], lhsT=wB.t[:, k, sec * 512 + cch * 128:sec * 512 + (cch + 1) * 128],
                                rhs=hT.t[:, k, 0:L], start=(k == 0), stop=(k == 7)), reads=[hT.b, wB.b], writes=[ups.b])
                        if sec == 0:
                            S.op("act", lambda e, cch=cch: e.copy(out=uT.t[:, cch, 0:L], in_=ups.t[:, 0:L]), reads=[ups.b], writes=[uT.b])
                        else:
                            S.op("act", lambda e, cch=cch: e.activation(out=zsG.t[:, cch, 0:L], in_=ups.t[:, 0:L], func=AF.Silu),
                                 reads=[ups.b], writes=[zsG.b])
                        groups.append(S.end_capture())
                return norm_l, groups

            span_list = list(range(0, T, SPn))
            pro_n, pro_g = prologue(span_list[0], uTs[spc[0] % 2], zsGs[spc[0] % 2])
            S.replay(pro_n)
            for g_ in pro_g:
                S.replay(g_)
            for sidx, s0 in enumerate(span_list):
                sc_ = spc[0]
                spc[0] += 1
                ys, uT, zsG = yS[sc_ % 2], uTs[sc_ % 2], zsGs[sc_ % 2]
                nxt, nxt_groups = [], []
                if sidx + 1 < len(span_list):
                    nxt, nxt_groups = prologue(span_list[sidx + 1], uTs[(sc_ + 1) % 2], zsGs[(sc_ + 1) % 2])
                pairs = [(cq, pr) for cq in range(4) for pr in range(2)]

                def pair_ops(n):
                    cq, pr = pairs[n]
                    outl = []
                    for tl_ in range(2):
                        dd = dict(TS[tl_])
                        dd.update(bX[n % 2][tl_])
                        outl.append(tile_ops(pk, L, cq * 4 + pr * 2 + tl_, cq, pr * 2 + tl_, dd, uT))
                    return outl
                cur = pair_ops(0)
                for i in range(4):
                    cur[0][i]()
                    cur[1][i]()
                gT = gTs[sc_ % 2]
                bg = list(carry_bg)
                bg_norm = list(nxt)
                del carry_bg[:]

                def bg_step():
                    if bg:
                        S.replay([bg.pop(0)])
                    if bg_norm:
                        S.replay([bg_norm.pop(0)])

                def gelu_ops(cq):
                    S.begin_capture()
                    dcol = smv("dT")[:, cq:cq + 1]
                    S.op("dve", lambda e: e.scalar_tensor_tensor(
                        out=yT.t[:, 0:L], in0=uT.t[:, cq, 0:L], scalar=dcol, in1=yps.t[:, 0:L], op0=ALU.mult, op1=ALU.add),
                        reads=[uT.b, yps.b, sm.b], writes=[yT.b])
                    S.op("act", lambda e: e.activation(out=x2.t[:, 0:L], in_=yT.t[:, 0:L], func=AF.Square), reads=[yT.b], writes=[x2.b])
                    S.op("pool", lambda e: e.tensor_scalar(out=x2.t[:, 0:L], in0=x2.t[:, 0:L], scalar1=0.044715, scalar2=1.0,
                                                          op0=ALU.mult, op1=ALU.add), reads=[x2.b], writes=[x2.b])
                    S.op("pool", lambda e: e.tensor_tensor(out=x2.t[:, 0:L], in0=x2.t[:, 0:L], in1=yT.t[:, 0:L], op=ALU.mult),
                         reads=[x2.b, yT.b], writes=[x2.b])
                    S.op("act", lambda e: e.activation(out=x2.t[:, 0:L], in_=x2.t[:, 0:L], func=AF.Sigmoid,
                                                       scale=2.0 * math.sqrt(2.0 / math.pi)), reads=[x2.b], writes=[x2.b])
                    S.op("pool", lambda e: e.tensor_tensor(out=gT.t[:, cq, 0:L], in0=x2.t[:, 0:L], in1=yT.t[:, 0:L], op=ALU.mult),
                         reads=[x2.b, yT.b], writes=[gT.b])
                    return S.end_capture()

                for n, (cq, pr) in enumerate(pairs):
                    la, lb = cur
                    if n + 1 < len(pairs):
                        cur = pair_ops(n + 1)
                        for i in range(4):
                            cur[0][i]()
                            cur[1][i]()
                    if n >= 4 and nxt_groups:
                        while bg_norm:
                            S.replay([bg_norm.pop(0)])
                        for g_ in nxt_groups[2 * (n - 4):2 * (n - 4) + 2]:
                            S.replay(g_)
                    na = len(la) - 4
                    for i in range(4, na):
                        la[i]()
                        lb[i]()
                        bg_step()
                    for i in range(na, len(la)):
                        la[i]()
                    for i in range(na, len(lb)):
                        lb[i]()
                    if pr == 1:
                        g_ = gelu_ops(cq)
                        if cq < 3:
                            merged = []
                            rest = bg[:]
                            del bg[:]
                            for op_ in g_:
                                merged.append(op_)
                                merged.extend(rest[:2])
                                rest = rest[2:]
                            bg.extend(merged + rest)
                        else:
                            last_gelu = g_
                while bg or bg_norm:
                    bg_step()
                S.begin_capture()
                for co in range(4):
                    for ci in range(4):
                        S.op("pe", lambda e, co=co, ci=ci: e.matmul(gps.t[:, 0:L], lhsT=wg.t[:, ci, co * 128:(co + 1) * 128],
                                                                    rhs=gT.t[:, ci, 0:L], start=(ci == 0), stop=(ci == 3)),
                             reads=[wg.b, gT.b], writes=[gps.b])
                    S.op("act", lambda e, co=co: e.activation(out=sgl.t[:, 0:L], in_=gps.t[:, 0:L], func=AF.Sigmoid,
                                                              bias=smv("b_gluT")[:, co:co + 1]), reads=[gps.b, sm.b], writes=[sgl.b])
                    S.op("pool", lambda e, co=co: e.tensor_tensor(out=y2.t[:, 0:L], in0=sgl.t[:, 0:L], in1=gT.t[:, co, 0:L], op=ALU.mult),
                         reads=[sgl.b, gT.b], writes=[y2.b])
                    S.op("pool", lambda e, co=co, ys=ys, zsG=zsG: e.tensor_tensor(out=ys.t[:, co, 0:L], in0=y2.t[:, 0:L], in1=zsG.t[:, co, 0:L],
                                                                               op=ALU.mult), reads=[y2.b, zsG.b], writes=[ys.b])
                S.op("pool", lambda e, ys=ys, s0=s0: e.dma_start(
                    out=SC[pk]["yssm"][:, s0:s0 + SPn].rearrange("(c p) t -> p c t", p=128), in_=ys.t[:, :, 0:SPn]),
                    reads=[ys.b], dma="ys%d" % (sc_ % 2))
                glu = S.end_capture()
                if sidx + 1 < len(span_list):
                    carry_bg.extend(last_gelu + glu)
                else:
                    S.replay(last_gelu + glu)
            S.op("pool", lambda e, pk=pk: e.dma_start(out=O["ssm_" + pk], in_=hst.t[:, :]), reads=hb, dma="hsto")
        S.barrier()
        st.close()

    def phase2C():
        st = contextlib.ExitStack()
        yatt = {"p": tile(st, "yatt_p", [128, 4, T_P], BF), "s": tile(st, "yatt_s", [128, 4, T_S], BF)}
        st2 = contextlib.ExitStack()
        kT = [tile(st2, "c_kT%d" % i, [65, T_P], BF) for i in range(2)]
        vA = [tile(st2, "c_vA%d" % i, [128, 64, 128], BF) for i in range(2)]
        qT = [tile(st2, "c_qT%d" % i, [65, 512], BF) for i in range(2)]
        zg = [tile(st2, "c_zg%d" % i, [128, 512], BF) for i in range(2)]
        NSB = 4
        kTb2 = [Buf(), Buf()]
        vAb2 = [Buf(), Buf()]
        pT = [tile(st2, "c_pT%d" % i, [128, 512], BF) for i in range(NSB)]
        rdens = [tile(st2, "c_rden%d" % i, [128, 512]) for i in range(2)]
        rdh = [tile(st2, "c_rdh%d" % i, [128, 512], BF) for i in range(2)]
        rdl = [tile(st2, "c_rdl%d" % i, [128, 512], BF) for i in range(2)]
        epi_q = []
        gbc = tile(st2, "c_gbc", [128, 512])
        sps = [PS[0], PS[1], PS[2], PS[6]]
        ops = [PS[3], PS[4]]
        bcps = PS[5]
        LA = 3
        heads = [(pk, h) for pk in ("s", "p") for h in range(8)]
        spans = []
        tasks = []
        for hi, (pk, h) in enumerate(heads):
            P = PATHS[pk]
            T, SPn, npast = P["T"], P["SP"], P["npast"]
            nk = npast + T
            for s0 in range(0, T, SPn):
                si = len(spans)
                qpos0 = npast + s0
                last_blk = (qpos0 + SPn - 1) // 128
                spans.append(dict(pk=pk, h=h, hi=hi, s0=s0, SPn=SPn, si=si, last_blk=last_blk, nk=nk))
                for j in range(last_blk + 1):
                    kk = min(128, nk - j * 128)
                    c0 = max(0, j * 128 - qpos0)
                    N = SPn - c0
                    diag = (j * 128 + kk - 1) > (qpos0 + c0)
                    tasks.append(dict(si=si, j=j, kk=kk, c0=c0, N=N, diag=diag, last=(j == last_blk)))
        ntasks_of = [sp["last_blk"] + 1 for sp in spans]
        last_span_of_head = {}
        for sp in spans:
            last_span_of_head[sp["hi"]] = sp["si"]

        def head_load(hi):
            if hi >= len(heads):
                return
            pk, h = heads[hi]
            hb = hi % 2
            kt, va = kT[hb], vA[hb]
            nk = PATHS[pk]["npast"] + PATHS[pk]["T"]
            nh = (nk // 256) * 128
            S.op("sp", lambda e: e.dma_start(out=kt.t[0:65, 0:nh], in_=SC[pk]["kTa"][h, :, 0:nh]), writes=[kt.b], dma="ckT%d" % hb)
            S.op("pool", lambda e: e.dma_start(out=kt.t[0:65, nh:nk], in_=SC[pk]["kTa"][h, :, nh:nk]), writes=[kTb2[hb]], dma="ckTb%d" % hb)
            nfull = nk // 128
            nfh = nfull // 2
            S.op("sp", lambda e: e.dma_start(
                out=va.t[:, 0:nfh, :], in_=SC[pk]["va"][0:nfh * 128, h, :].rearrange("(b p) c -> p b c", p=128)),
                writes=[va.b], dma="cvA%d" % hb)
            S.op("pool", lambda e: e.dma_start(
                out=va.t[:, nfh:nfull, :], in_=SC[pk]["va"][nfh * 128:nfull * 128, h, :].rearrange("(b p) c -> p b c", p=128)),
                writes=[vAb2[hb]], dma="cvAb%d" % hb)
            if nk % 128:
                r = nk % 128
                S.op("sp", lambda e: e.dma_start(out=va.t[0:r, nfull, :], in_=SC[pk]["va"][nfull * 128:nk, h, :]),
                     writes=[va.b], dma="cvA%d" % hb)

        def span_load(si):
            if si >= len(spans):
                return
            sp = spans[si]
            pk, h, s0, SPn = sp["pk"], sp["h"], sp["s0"], sp["SPn"]
            qb = si % 2
            prw = slice(64, 128) if h % 2 else slice(0, 64)
            S.op("sp", lambda e: e.dma_start(out=qT[qb].t[0:65, 0:SPn], in_=SC[pk]["qTa"][h, :, s0:s0 + SPn]),
                 writes=[qT[qb].b], dma="cqT%d" % qb)
            S.op("sp", lambda e: e.dma_start(out=zg[qb].t[prw, 0:SPn], in_=SC[pk]["zaG"][h * 64:(h + 1) * 64, s0:s0 + SPn]),
                 writes=[zg[qb].b], dma="czg%d" % qb)

        def emit_S(i):
            t = tasks[i]
            sp = spans[t["si"]]
            kt, qt = kT[sp["hi"] % 2], qT[sp["si"] % 2]
            sp_ = sps[i % NSB]
            j, kk, c0, N, diag = t["j"], t["kk"], t["c0"], t["N"], t["diag"]
            S.op("pe", lambda e: e.matmul(sp_.t[0:kk, 0:N], lhsT=kt.t[0:65, j * 128:j * 128 + kk], rhs=qt.t[0:65, c0:c0 + N],
                                          start=True, stop=(not diag)), reads=[kt.b, kTb2[sp["hi"] % 2], qt.b], writes=[sp_.b])
            if diag:
                nd = min(kk, N)
                S.op("pe", lambda e: e.matmul(sp_.t[0:kk, 0:nd], lhsT=ident_bf[0:kk, 0:kk], rhs=maskT[0:kk, 0:nd], start=False, stop=True),
                     reads=[cb.b], writes=[sp_.b])

        def emit_PV(i):
            t = tasks[i]
            sp = spans[t["si"]]
            pk, h, s0, SPn, si = sp["pk"], sp["h"], sp["s0"], sp["SPn"], sp["si"]
            va = vA[sp["hi"] % 2]
            sp_, ptl, opsb = sps[i % NSB], pT[i % NSB], ops[si % 2]
            j, kk, c0, N = t["j"], t["kk"], t["c0"], t["N"]
            odd = h % 2
            prw = slice(64, 128) if odd else slice(0, 64)
            drow = 0 if odd else 64
            M = 128 if odd else 65
            S.op("act", lambda e: e.activation(out=ptl.t[0:kk, 0:N], in_=sp_.t[0:kk, 0:N], func=AF.Exp, scale=0.125,
                                               bias=biasK[pk].t[0:kk, j, h:h + 1]), reads=[sp_.b, biasK[pk].b], writes=[ptl.b])
            S.op("pe", lambda e: e.matmul(opsb.t[0:M, c0:c0 + N], lhsT=va.t[0:kk, j, 0:M], rhs=ptl.t[0:kk, 0:N],
                                          start=(j == 0), stop=t["last"]), reads=[va.b, vAb2[sp["hi"] % 2], ptl.b], writes=[opsb.b])
            if t["last"]:
                zgt = zg[si % 2]
                rd = rdens[si % 2]
                rh, rl = rdh[si % 2], rdl[si % 2]
                S.op("dve", lambda e: e.reciprocal(out=rd.t[drow:drow + 1, 0:SPn], in_=opsb.t[drow:drow + 1, 0:SPn]),
                     reads=[opsb.b], writes=[rd.b])
                S.op("dve", lambda e: e.tensor_copy(out=rh.t[drow:drow + 1, 0:SPn], in_=rd.t[drow:drow + 1, 0:SPn]),
                     reads=[rd.b], writes=[rh.b])
                S.op("dve", lambda e: e.tensor_tensor(out=rl.t[drow:drow + 1, 0:SPn], in0=rd.t[drow:drow + 1, 0:SPn],
                                                      in1=rh.t[drow:drow + 1, 0:SPn], op=ALU.subtract),
                     reads=[rd.b, rh.b], writes=[rl.b])

                def epi():
                    S.op("pe", lambda e: e.matmul(bcps.t[:, 0:SPn], lhsT=ones_bf.t[drow:drow + 1, 0:128], rhs=rh.t[drow:drow + 1, 0:SPn],
                                                  start=True, stop=False), reads=[rh.b, ones_bf.b], writes=[bcps.b])
                    S.op("pe", lambda e: e.matmul(bcps.t[:, 0:SPn], lhsT=ones_bf.t[drow:drow + 1, 0:128], rhs=rl.t[drow:drow + 1, 0:SPn],
                                                  start=False, stop=True), reads=[rl.b, ones_bf.b], writes=[bcps.b])
                    S.op("dve", lambda e: e.tensor_tensor(out=gbc.t[prw, 0:SPn], in0=bcps.t[prw, 0:SPn], in1=zgt.t[prw, 0:SPn], op=ALU.mult),
                         reads=[bcps.b, zgt.b], writes=[gbc.b])
                    S.op("dve", lambda e: e.tensor_tensor(out=yatt[pk].t[prw, h // 2, s0:s0 + SPn], in0=opsb.t[prw, 0:SPn],
                                                          in1=gbc.t[prw, 0:SPn], op=ALU.mult), reads=[opsb.b, gbc.b], writes=[yatt[pk].b])
                    span_load(si + 2)
                    if last_span_of_head[sp["hi"]] == si:
                        head_load(sp["hi"] + 2)
                nxt_n = ntasks_of[si + 1] if si + 1 < len(spans) else 8
                epi_q.append([max(1, min(5, nxt_n - LA)), epi])

        head_load(0)
        head_load(1)
        span_load(0)
        span_load(1)
        for i in range(len(tasks) + LA):
            if i < len(tasks):
                emit_S(i)
            for it in epi_q:
                it[0] -= 1
            while epi_q and epi_q[0][0] <= 0:
                epi_q.pop(0)[1]()
            if i >= LA:
                emit_PV(i - LA)
        while epi_q:
            epi_q.pop(0)[1]()
        S.barrier()
        st2.close()
        wos = [tile(st, "c_wos%d" % i, [128, 1024]) for i in range(2)]
        wo = {"p": tile(st, "c_wo_p", [128, 8, 1024], BF), "s": tile(st, "c_wo_s", [128, 8, 1024], BF)}
        for k in range(8):
            wsl = wos[k % 2]
            S.op("sp", lambda e, wsl=wsl, k=k: e.dma_start(out=wsl.t[:], in_=w_out[:, k, :]), writes=[wsl.b], dma="cwo%d" % (k % 2))
            for pk in ("p", "s"):
                pi = PATHS[pk]["pi"]
                S.op("dve", lambda e, wsl=wsl, k=k, pk=pk, pi=pi: e.tensor_tensor(out=wo[pk].t[:, k, :], in0=wsl.t[:], in1=gate_bc.t[:, pi, :],
                                                                               op=ALU.mult), reads=[wsl.b, gate_bc.b], writes=[wo[pk].b])
        ysm = [tile(st, "c_ysm%d" % i, [128, 4, 512], BF) for i in range(2)]
        xr = [tile(st, "c_xr%d" % i, [128, D]) for i in range(2)]
        yo = [tile(st, "c_yo%d" % i, [128, D]) for i in range(2)]
        o_ps = [PS[6], PS[7]]
        sc2 = [0]
        tc2 = [0]
        for pk in ("s", "p"):
            P = PATHS[pk]
            T, SPn, TT = P["T"], P["SP"], P["TT"]
            for s0 in range(0, T, SPn):
                sb = sc2[0] % 2
                sc2[0] += 1
                ym = ysm[sb]
                S.op("sp", lambda e, ym=ym, s0=s0: e.dma_start(
                    out=ym.t[:, :, 0:SPn], in_=SC[pk]["yssm"][:, s0:s0 + SPn].rearrange("(c p) t -> p c t", p=128)),
                    writes=[ym.b], dma="cys%d" % sb)
                for ti in range(SPn // TT):
                    tb = tc2[0] % 2
                    tc2[0] += 1
                    t0 = s0 + ti * TT
                    xt, yt = xr[tb], yo[tb]
                    S.op("sp", lambda e, xt=xt, t0=t0: e.dma_start(out=xt.t[0:TT, :], in_=xin[pk][t0:t0 + TT, :]), writes=[xt.b], dma="cxr%d" % tb)
                    for half in range(2):
                        op_ = o_ps[half]
                        for c in range(8):
                            if c < 4:
                                lh, lb = ym.t[:, c, ti * TT:(ti + 1) * TT], ym.b
                            else:
                                lh, lb = yatt[pk].t[:, c - 4, t0:t0 + TT], yatt[pk].b
                            S.op("pe", lambda e, op_=op_, lh=lh, c=c, half=half: e.matmul(
                                op_.t[0:TT, :], lhsT=lh, rhs=wo[pk].t[:, c, half * 512:(half + 1) * 512], start=(c == 0), stop=(c == 7)),
                                reads=[lb, wo[pk].b], writes=[op_.b])
                        S.op("dve", lambda e, op_=op_, xt=xt, yt=yt, half=half: e.tensor_tensor(
                            out=yt.t[0:TT, half * 512:(half + 1) * 512], in0=op_.t[0:TT, :], in1=xt.t[0:TT, half * 512:(half + 1) * 512],
                            op=ALU.add), reads=[op_.b, xt.b], writes=[yt.b])
                    S.op("pool", lambda e, yt=yt, t0=t0: e.dma_start(out=O["y_" + pk][t0:t0 + TT, :], in_=yt.t[0:TT, :]),
                         reads=[yt.b], dma="cyo%d" % tb)
        S.barrier()
        st.close()

    if "0" in phases:
        phase0()
    stTab = contextlib.ExitStack()
    if "A" in phases:
        alloc_tables(stTab)
        phaseN()
        phaseA()
    if "B" in phases:
        phaseB()
    stTab.close()
    if "2" in phases:
        phase2C()
    S.barrier()
    S.emit(nc, root)
    root.close()
    return nc


def _common_inputs(inp):
    f = np.float32
    d = {}
    d["w_ada"] = np.ascontiguousarray(inp["w_ada"][0], f)
    d["w_in"] = np.ascontiguousarray(inp["w_in"][0], f)
    d["w_out"] = np.ascontiguousarray(inp["w_out"][0], f)
    d["w_glu"] = np.ascontiguousarray(inp["w_glu"][0], f)
    bc = np.zeros((128, 2048), f)
    bc[:, 0:1024] = inp["b_ada"][0][2048:3072][None, :]
    bc[:, 1024:1536] = np.tile(inp["q_norm_g"][0], 8)[None, :]
    bc[:, 1536:2048] = np.tile(inp["k_norm_g"][0], 8)[None, :]
    d["bc"] = bc
    rep3 = np.zeros((128, 3, 2048), f)
    rep3[:, 0, :] = inp["ssm_a_re"][0].reshape(-1)[None, :]
    rep3[:, 1, :] = inp["ssm_a_im"][0].reshape(-1)[None, :]
    rep3[:, 2, :] = np.repeat(inp["ssm_log_dt"][0], 64)[None, :]
    d["rep3"] = rep3
    BTl = np.zeros((128, 2, 16, 2, 64), f)
    CTl = np.zeros((128, 2, 16, 128), f)
    for pl, (bsrc, csrc) in enumerate(((inp["ssm_b_re"][0], inp["ssm_c_re"][0]), (inp["ssm_b_im"][0], inp["ssm_c_im"][0]))):
        for g in range(32):
            j, g2, gl = g // 2, g % 2, g % 8
            BTl[gl * 16:(gl + 1) * 16, pl, j, g2, :] = bsrc[g].T
            CTl[g2 * 64:(g2 + 1) * 64, pl, j, gl * 16:(gl + 1) * 16] = csrc[g].T
    d["BTlay"] = BTl.reshape(128, 2, 2048)
    d["CTlay"] = CTl.reshape(128, 2, 2048)
    cf = np.zeros((128, NCF), f)
    cf[:, CF_ID:CF_ID + 128] = np.eye(128, dtype=f)
    cf[:, CF_TRI:CF_TRI + 128] = np.triu(np.ones((128, 128), f))
    cf[127, CF_SEL128:CF_SEL128 + 128] = 1.0
    cf[63, CF_SEL64:CF_SEL64 + 128] = 1.0
    cf[:, CF_IOTA:CF_IOTA + 512] = np.arange(1, 513, dtype=f)[None, :]
    d["cf"] = cf
    cb = np.zeros((128, 256), f)
    cb[:, 0:128] = np.eye(128, dtype=f)
    cb[:, 128:256] = np.where(np.arange(128)[:, None] > np.arange(128)[None, :], -240000.0, 0.0)
    d["cb"] = cb.astype(ml_dtypes.bfloat16)
    return d


def _smalls(inp, b):
    f = np.float32
    sm = np.zeros((128, NS), f)

    def put(name, arr):
        a, e = SM[name]
        sm[:, a:e] = arr

    put("b_adaT", inp["b_ada"][0][0:2048].reshape(16, 128).T)
    put("norm_gT", inp["norm_g"][0].reshape(8, 128).T)
    put("bf_bc", np.broadcast_to(inp["b_f"][0][None, :], (128, 8)))
    put("dT", inp["ssm_d"][0].reshape(4, 128).T)
    put("b_gluT", inp["b_glu"][0].reshape(4, 128).T)
    put("logdtT", np.repeat(inp["ssm_log_dt"][0], 64).reshape(16, 128).T)
    put("a_reT", inp["ssm_a_re"][0].reshape(16, 128).T)
    put("a_imT", inp["ssm_a_im"][0].reshape(16, 128).T)
    c2 = np.stack([inp["c_prompt"][b].reshape(8, 128).T, inp["c_sample"][b].reshape(8, 128).T], axis=-1)
    put("c2T", c2.reshape(128, 16))
    put("stT_re", inp["state_ssm_re"][0, b].reshape(16, 128).T)
    put("stT_im", inp["state_ssm_im"][0, b].reshape(16, 128).T)
    return sm


_NC_CACHE = {}


def kernel(**inp):
    inp = {k: np.asarray(v) for k, v in inp.items()}
    if "nc" not in _NC_CACHE:
        _NC_CACHE["nc"] = build_program()
    nc = _NC_CACHE["nc"]
    common = _common_inputs(inp)
    in_maps = []
    f = np.float32
    for b in range(NCORES):
        m = dict(common)
        m["smalls"] = _smalls(inp, b)
        m["x_p"] = np.ascontiguousarray(inp["x_prompt"][b], f)
        m["x_s"] = np.ascontiguousarray(inp["x_sample"][b], f)
        m["ck"] = np.ascontiguousarray(inp["cache_k"][0, b].reshape(NPAST, 512), f)
        m["cv"] = np.ascontiguousarray(inp["cache_v"][0, b].reshape(NPAST, 512), f)
        m["clf"] = np.ascontiguousarray(inp["cache_logf"][0, b], f)
        in_maps.append(m)
    res = run_bass_kernel_spmd(nc, in_maps, core_ids=list(range(NCORES)))
    R = res.results

    def stack(name):
        return np.stack([np.asarray(R[b][name], f) for b in range(NCORES)], axis=0)

    def ssm(name, lo):
        a = stack(name)[:, :, lo:lo + 16]
        return np.ascontiguousarray(a.transpose(0, 2, 1).reshape(NCORES, 32, 64))[None]

    y_p = stack("y_p")
    y_s = stack("y_s")
    k_p = stack("k_p").reshape(1, NCORES, T_P, 8, 64)
    v_p = stack("v_p").reshape(1, NCORES, T_P, 8, 64)
    lf_p = stack("lf_p")[None]
    k_s = stack("k_s").reshape(1, NCORES, T_S, 8, 64)
    v_s = stack("v_s").reshape(1, NCORES, T_S, 8, 64)
    lf_s = stack("lf_s")[None]
    return (y_p, y_s, k_p, v_p, lf_p, ssm("ssm_p", 0), ssm("ssm_p", 16),
            k_s, v_s, lf_s, ssm("ssm_s", 0), ssm("ssm_s", 16))
```

```python
import contextlib
import math
import numpy as np
import ml_dtypes
import concourse.bass as bass
import concourse.mybir as mybir
from concourse.bass_utils import run_bass_kernel_spmd

F32 = mybir.dt.float32
BF = mybir.dt.bfloat16
ALU = mybir.AluOpType
AF = mybir.ActivationFunctionType
AX = mybir.AxisListType

SAME_ENG_SYNC = True
D = 1024
NCORES = 8
T_P, T_S, NPAST = 8192, 64, 2048
EPS = 1e-6
PATHS = {
    "p": dict(pi=0, T=T_P, TT=128, SP=512, npast=0),
    "s": dict(pi=1, T=T_S, TT=64, SP=64, npast=NPAST),
}
SM = {}
_o = 0
for _n, _w in (("b_adaT", 16), ("norm_gT", 8), ("bf_bc", 8), ("dT", 4), ("b_gluT", 4), ("logdtT", 16),
               ("a_reT", 16), ("a_imT", 16), ("c2T", 16), ("stT_re", 16), ("stT_im", 16)):
    SM[_n] = (_o, _o + _w)
    _o += _w
NS = _o
CF_ID, CF_TRI, CF_SEL128, CF_SEL64, CF_IOTA = 0, 128, 256, 384, 512
NCF = 1024


class Ev:
    __slots__ = ("kind", "key", "val")

    def __init__(s, kind, key, val):
        s.kind, s.key, s.val = kind, key, val


class Rec:
    __slots__ = ("call",)

    def __init__(s):
        s.call = None

    def __getattr__(s, name):
        def f(*a, **k):
            s.call = (name, a, k)
            return s
        return f


class Buf:
    __slots__ = ("w", "r")

    def __init__(s):
        s.w = None
        s.r = {}


class Sched:
    ENGS = ("pe", "act", "dve", "pool", "sp")

    def __init__(s):
        s.q = {e: [] for e in s.ENGS}
        s.waited = {e: {} for e in s.ENGS}
        s.sig = {e: set() for e in s.ENGS}
        s.dma_cnt = {}
        s.dma_last = {}
        s.capture = None

    def op(s, eng, fn, reads=(), writes=(), dma=None, after=()):
        rec = Rec()
        fn(rec)
        assert rec.call is not None
        if s.capture is not None:
            s.capture.append((eng, rec.call, tuple(reads), tuple(writes), dma, tuple(after)))
            return None
        return s._sched(eng, rec.call, reads, writes, dma, after)

    def begin_capture(s):
        s.capture = []

    def end_capture(s):
        items, s.capture = s.capture, None
        return items

    def replay(s, items):
        for it in items:
            s._sched(*it)

    def _sched(s, eng, call, reads=(), writes=(), dma=None, after=()):
        deps = list(after)
        for b in reads:
            if b.w is not None:
                deps.append(b.w)
        for b in writes:
            if b.w is not None:
                deps.append(b.w)
            deps.extend(b.r.values())
        waits = []
        wd = s.waited[eng]
        for ev in deps:
            if ev.kind == "E" and ev.key == eng and dma is None and (eng == "pe" or not SAME_ENG_SYNC):
                continue
            k = (ev.kind, ev.key)
            if wd.get(k, -1) >= ev.val:
                continue
            wd[k] = ev.val
            waits.append(ev)
            if ev.kind == "E":
                s.sig[ev.key].add(ev.val)
        idx = len(s.q[eng])
        if dma is not None:
            n = s.dma_cnt.get(dma, 0) + 1
            s.dma_cnt[dma] = n
            ev = Ev("D", dma, 16 * n)
            s.dma_last[dma] = ev
        else:
            ev = Ev("E", eng, idx)
        s.q[eng].append((call, waits, dma))
        k = (ev.kind, ev.key)
        for b in reads:
            b.r[k] = ev
        for b in writes:
            b.w = ev
            b.r = {}
        return ev

    def barrier(s, engs=None):
        evs = []
        for e in s.ENGS:
            if s.q[e]:
                for idx in range(len(s.q[e]) - 1, -1, -1):
                    fn, _, dma = s.q[e][idx]
                    if fn is not None and dma is None:
                        evs.append(Ev("E", e, idx))
                        break
        evs.extend(s.dma_last.values())
        for e in (engs or s.ENGS):
            waits = []
            wd = s.waited[e]
            for ev in evs:
                if ev.kind == "E" and ev.key == e:
                    continue
                k = (ev.kind, ev.key)
                if wd.get(k, -1) >= ev.val:
                    continue
                wd[k] = ev.val
                waits.append(ev)
                if ev.kind == "E":
                    s.sig[ev.key].add(ev.val)
            s.q[e].append((None, waits, None))

    def emit(s, nc, st):
        sems = {e: st.enter_context(nc.semaphore("c_" + e)) for e in s.ENGS}
        dsems = {k: st.enter_context(nc.semaphore("d_" + k)) for k in s.dma_cnt}
        rank = {}
        for e in s.ENGS:
            rank[e] = {idx: i + 1 for i, idx in enumerate(sorted(s.sig[e]))}

        def run(eo, e):
            rk = rank[e]
            for idx, (fn, waits, dma) in enumerate(s.q[e]):
                for ev in waits:
                    if ev.kind == "E":
                        eo.wait_ge(sems[ev.key], rank[ev.key][ev.val])
                    else:
                        eo.wait_ge(dsems[ev.key], ev.val)
                if fn is None:
                    continue
                ins = getattr(eo, fn[0])(*fn[1], **fn[2])
                if dma is not None:
                    ins.then_inc(dsems[dma], 16)
                elif idx in rk:
                    ins.then_inc(sems[e], 1)

        with nc.Block() as block:
            @block.tensor
            def _(eo):
                run(eo, "pe")

            @block.scalar
            def _(eo):
                run(eo, "act")

            @block.vector
            def _(eo):
                run(eo, "dve")

            @block.gpsimd
            def _(eo):
                run(eo, "pool")

            @block.sync
            def _(eo):
                run(eo, "sp")


class Tl:
    __slots__ = ("t", "b")

    def __init__(s, t):
        s.t = t
        s.b = Buf()


def build_program(phases="0AB2C"):
    nc = bass.Bass("TRN2", target_bir_lowering=False)
    S = Sched()
    root = contextlib.ExitStack()

    def din(name, shape, dt=F32):
        return nc.dram_tensor(name, list(shape), dt, kind="ExternalInput").ap()

    def dout(name, shape):
        return nc.dram_tensor(name, list(shape), F32, kind="ExternalOutput").ap()

    def dscr(name, shape, dt=BF):
        return nc.dram_tensor(name, list(shape), dt).ap()

    w_ada = din("w_ada", [D, 3 * D]).rearrange("(k p) n -> p k n", p=128)
    w_in = din("w_in", [D, 3080]).rearrange("(k p) n -> p k n", p=128)
    w_out = din("w_out", [D, D]).rearrange("(k p) n -> p k n", p=128)
    w_glu = din("w_glu", [512, 512]).rearrange("(k p) n -> p k n", p=128)
    bcin = din("bc", [128, 2048])
    rep3 = din("rep3", [128, 3, 2048])
    BTlay = din("BTlay", [128, 2, 2048])
    CTlay = din("CTlay", [128, 2, 2048])
    cfin = din("cf", [128, NCF])
    cbin = din("cb", [128, 256], BF)
    smin = din("smalls", [128, NS])
    xin = {"p": din("x_p", [T_P, D]), "s": din("x_s", [T_S, D])}
    ckin = din("ck", [NPAST, 512])
    cvin = din("cv", [NPAST, 512])
    clfin = din("clf", [NPAST, 8])
    O = {}
    for pk, P in PATHS.items():
        O["y_" + pk] = dout("y_" + pk, [P["T"], D])
        O["k_" + pk] = dout("k_" + pk, [P["T"], 512])
        O["v_" + pk] = dout("v_" + pk, [P["T"], 512])
        O["lf_" + pk] = dout("lf_" + pk, [P["T"], 8])
        O["ssm_" + pk] = dout("ssm_" + pk, [128, 32])
    SC = {}
    for pk, P in PATHS.items():
        nk = P["npast"] + P["T"]
        SC[pk] = dict(qTa=dscr("qTa_" + pk, [8, 65, P["T"]]), kTa=dscr("kTa_" + pk, [8, 65, nk]),
                      va=dscr("va_" + pk, [nk, 8, 128]), zaG=dscr("zaG_" + pk, [512, P["T"]]),
                      yssm=dscr("yssm_" + pk, [512, P["T"]]), hT=dscr("hTs_" + pk, [1024, P["T"]]))

    tcount = [0]

    def tile(st, name, shape, dt=F32):
        tcount[0] += 1
        return Tl(st.enter_context(nc.sbuf_tensor("sb%d_%s" % (tcount[0], name), list(shape), dt)))

    PS = [Tl(root.enter_context(nc.psum_tensor("ps%d" % i, [128, 512], F32))) for i in range(8)]

    sm = tile(root, "sm", [128, NS])
    cf = tile(root, "cf", [128, NCF])
    cb = tile(root, "cb", [128, 256], BF)
    qkg = tile(root, "qkg", [128, 1024])
    gate_bc = tile(root, "gate_bc", [128, 2, 1024])
    modb = tile(root, "modb", [128, 16, 2])
    gsT = tile(root, "gsT", [128, 8, 2])
    negb8 = tile(root, "negb8", [128, 8])
    theta = tile(root, "theta", [128, 16])
    rho = tile(root, "rho", [128, 16])
    BT = tile(root, "BT", [128, 2, 2048], BF)
    CT = tile(root, "CT", [128, 3, 2048], BF)
    biasK = {"p": tile(root, "biasK_p", [128, 64, 8]), "s": tile(root, "biasK_s", [128, 17, 8])}
    ones_bf = tile(root, "ones_bf", [128, 128], BF)
    ones_f = tile(root, "ones_f", [128, 128])
    epsc = tile(root, "epsc", [128, 1])

    def smv(name):
        a, b = SM[name]
        return sm.t[:, a:b]

    ident_bf = cb.t[:, 0:128]
    maskT = cb.t[:, 128:256]
    ident_f = cf.t[:, CF_ID:CF_ID + 128]
    tri_f = cf.t[:, CF_TRI:CF_TRI + 128]
    sel128 = cf.t[:, CF_SEL128:CF_SEL128 + 128]
    sel64 = cf.t[:, CF_SEL64:CF_SEL64 + 128]
    iota_t1 = cf.t[:, CF_IOTA:CF_IOTA + 512]

    dcount = [0]

    def dname(prefix):
        dcount[0] += 1
        return "%s%d" % (prefix, dcount[0])

    TWO_PI_HI = 6.28125
    TWO_PI_LO = 2.0 * math.pi - 6.28125

    def range_sin(dst, ang, shift, tf, ti, bd, ba, btf, bti):
        off = shift + 16.0 * math.pi
        S.op("dve", lambda e: e.tensor_scalar(out=tf, in0=ang, scalar1=off, scalar2=1.0 / (2.0 * math.pi), op0=ALU.add, op1=ALU.mult),
             reads=[ba], writes=[btf])
        S.op("dve", lambda e: e.tensor_copy(out=ti, in_=tf), reads=[btf], writes=[bti])
        S.op("dve", lambda e: e.tensor_copy(out=tf, in_=ti), reads=[bti], writes=[btf])
        S.op("dve", lambda e: e.scalar_tensor_tensor(out=dst, in0=tf, scalar=-TWO_PI_HI, in1=ang, op0=ALU.mult, op1=ALU.add),
             reads=[btf, ba], writes=[bd])
        S.op("dve", lambda e: e.scalar_tensor_tensor(out=dst, in0=tf, scalar=-TWO_PI_LO, in1=dst, op0=ALU.mult, op1=ALU.add),
             reads=[btf, bd], writes=[bd])
        S.op("dve", lambda e: e.tensor_scalar(out=dst, in0=dst, scalar1=off, scalar2=None, op0=ALU.add), reads=[bd], writes=[bd])
        S.op("dve", lambda e: e.tensor_scalar(out=tf, in0=dst, scalar1=math.pi, scalar2=-2.0 * math.pi, op0=ALU.is_gt, op1=ALU.mult),
             reads=[bd], writes=[btf])
        S.op("dve", lambda e: e.tensor_tensor(out=dst, in0=dst, in1=tf, op=ALU.add), reads=[bd, btf], writes=[bd])
        S.op("dve", lambda e: e.tensor_scalar(out=dst, in0=dst, scalar1=3.14159, scalar2=-3.14159, op0=ALU.min, op1=ALU.max),
             reads=[bd], writes=[bd])
        S.op("act", lambda e: e.activation(out=dst, in_=dst, func=AF.Sin), reads=[bd], writes=[bd])

    def phase0():
        st = contextlib.ExitStack()
        bc = tile(st, "bc", [128, 2048])
        g = "ld0"
        S.op("sp", lambda e: e.dma_start(out=sm.t[:], in_=smin), writes=[sm.b], dma=g)
        S.op("sp", lambda e: e.dma_start(out=cf.t[:], in_=cfin), writes=[cf.b], dma=g)
        S.op("sp", lambda e: e.dma_start(out=cb.t[:], in_=cbin), writes=[cb.b], dma=g)
        ev = S.op("sp", lambda e: e.dma_start(out=bc.t[:], in_=bcin), writes=[bc.b], dma=g)
        for tl in (sm, cf, cb, bc):
            tl.b.w = ev
        r3 = tile(st, "r3", [128, 3, 2048])
        S.op("sp", lambda e: e.dma_start(out=r3.t[:], in_=rep3), writes=[r3.b], dma="ld_r3")
        bl = tile(st, "bl", [128, 2, 2048])
        S.op("sp", lambda e: e.dma_start(out=bl.t[:], in_=BTlay), writes=[bl.b], dma="ld_bl")
        S.op("dve", lambda e: e.memset(ones_bf.t[:], 1.0), writes=[ones_bf.b])
        S.op("dve", lambda e: e.memset(ones_f.t[:], 1.0), writes=[ones_f.b])
        S.op("dve", lambda e: e.memset(epsc.t[:], EPS), writes=[epsc.b])
        S.op("dve", lambda e: e.tensor_copy(out=qkg.t[:], in_=bc.t[:, 1024:2048]), reads=[bc.b], writes=[qkg.b])
        sc = tile(st, "sc", [128, 8, 2])
        sg = tile(st, "sg", [128, 16])
        screp = tile(st, "screp", [128, 16, 128])
        c2 = smv("c2T")
        S.op("act", lambda e: e.activation(out=sg.t[:], in_=c2, func=AF.Sigmoid), reads=[sm.b], writes=[sg.b])
        scf = sc.t[:].rearrange("p k t -> p (k t)")
        S.op("dve", lambda e: e.tensor_tensor(out=scf, in0=c2, in1=sg.t[:], op=ALU.mult), reads=[sg.b, sm.b], writes=[sc.b])
        for i in range(16):
            S.op("dve", lambda e, i=i: e.tensor_scalar(out=screp.t[:, i, :], in0=ones_f.t[:], scalar1=scf[:, i:i + 1],
                                                     scalar2=None, op0=ALU.mult),
                 reads=[sc.b, ones_f.b], writes=[screp.b])
        wa = [tile(st, "wa%d" % i, [128, 8, 512]) for i in range(2)]
        modps = PS[0]
        for pc in range(6):
            w = wa[pc % 2]
            S.op("sp", lambda e, w=w, pc=pc: e.dma_start(out=w.t[:], in_=w_ada[:, :, pc * 512:(pc + 1) * 512]),
                 writes=[w.b], dma="wa%d" % (pc % 2))
            if pc < 4:
                for jj in range(4):
                    j = pc * 4 + jj
                    for k in range(8):
                        S.op("pe", lambda e, w=w, jj=jj, j=j, k=k: e.matmul(
                            modps.t[:, 2 * j:2 * j + 2], lhsT=w.t[:, k, jj * 128:(jj + 1) * 128], rhs=sc.t[:, k, :],
                            start=(k == 0), stop=(k == 7)), reads=[w.b, sc.b], writes=[modps.b])
            else:
                half = pc - 4
                for pi in range(2):
                    ps = PS[1 + pi * 2 + half]
                    for k in range(8):
                        S.op("pe", lambda e, w=w, ps=ps, k=k, pi=pi: e.matmul(
                            ps.t[:, :], lhsT=screp.t[:, k * 2 + pi, :], rhs=w.t[:, k, :],
                            start=(k == 0), stop=(k == 7)), reads=[w.b, screp.b], writes=[ps.b])
                    S.op("dve", lambda e, ps=ps, pi=pi, half=half: e.tensor_tensor(
                        out=gate_bc.t[:, pi, half * 512:(half + 1) * 512], in0=ps.t[:, :],
                        in1=bc.t[:, half * 512:(half + 1) * 512], op=ALU.add), reads=[ps.b, bc.b], writes=[gate_bc.b])
        S.op("dve", lambda e: e.tensor_tensor(
            out=modb.t[:], in0=modps.t[:, 0:32].rearrange("p (j t) -> p j t", t=2),
            in1=smv("b_adaT").unsqueeze(2).to_broadcast([128, 16, 2]), op=ALU.add),
            reads=[modps.b, sm.b], writes=[modb.b])
        S.op("dve", lambda e: e.scalar_tensor_tensor(
            out=gsT.t[:], in0=modb.t[:, 8:16, :], scalar=1.0,
            in1=smv("norm_gT").unsqueeze(2).to_broadcast([128, 8, 2]), op0=ALU.add, op1=ALU.mult),
            reads=[modb.b, sm.b], writes=[gsT.b])
        g2 = tile(st, "g2", [128, 128])
        m2 = tile(st, "m2", [128, 2])
        S.op("dve", lambda e: e.tensor_tensor(out=g2.t[:, 0:64], in0=qkg.t[:, 0:64], in1=qkg.t[:, 0:64], op=ALU.mult),
             reads=[qkg.b], writes=[g2.b])
        S.op("dve", lambda e: e.tensor_tensor(out=g2.t[:, 64:128], in0=qkg.t[:, 512:576], in1=qkg.t[:, 512:576], op=ALU.mult),
             reads=[qkg.b], writes=[g2.b])
        S.op("dve", lambda e: e.reduce_max(out=m2.t[:, 0:1], in_=g2.t[:, 0:64], axis=AX.X), reads=[g2.b], writes=[m2.b])
        S.op("dve", lambda e: e.reduce_max(out=m2.t[:, 1:2], in_=g2.t[:, 64:128], axis=AX.X), reads=[g2.b], writes=[m2.b])
        S.op("dve", lambda e: e.tensor_tensor(out=m2.t[:, 0:1], in0=m2.t[:, 0:1], in1=m2.t[:, 1:2], op=ALU.mult),
             reads=[m2.b], writes=[m2.b])
        S.op("act", lambda e: e.activation(out=m2.t[:, 0:1], in_=m2.t[:, 0:1], func=AF.Ln), reads=[m2.b], writes=[m2.b])
        S.op("act", lambda e: e.activation(out=m2.t[:, 0:1], in_=m2.t[:, 0:1], func=AF.Exp, scale=0.5), reads=[m2.b], writes=[m2.b])
        S.op("dve", lambda e: e.tensor_scalar(out=m2.t[:, 0:1], in0=m2.t[:, 0:1], scalar1=-8.0, scalar2=None, op0=ALU.mult),
             reads=[m2.b], writes=[m2.b])
        S.op("dve", lambda e: e.tensor_copy(out=negb8.t[:], in_=m2.t[:, 0:1].to_broadcast([128, 8])),
             reads=[m2.b], writes=[negb8.b])
        dtT = tile(st, "dtT", [128, 16])
        S.op("act", lambda e: e.activation(out=dtT.t[:], in_=smv("logdtT"), func=AF.Exp), reads=[sm.b], writes=[dtT.b])
        S.op("dve", lambda e: e.tensor_tensor(out=theta.t[:], in0=smv("a_imT"), in1=dtT.t[:], op=ALU.mult),
             reads=[sm.b, dtT.b], writes=[theta.b])
        S.op("dve", lambda e: e.tensor_tensor(out=rho.t[:], in0=smv("a_reT"), in1=dtT.t[:], op=ALU.mult),
             reads=[sm.b, dtT.b], writes=[rho.b])
        S.op("act", lambda e: e.activation(out=rho.t[:], in_=rho.t[:], func=AF.Exp), reads=[rho.b], writes=[rho.b])
        tt = [tile(st, "tt%d" % i, [128, 2048]) for i in range(7)]
        ar, ai, ldt = r3.t[:, 0, :], r3.t[:, 1, :], r3.t[:, 2, :]
        dtr, mag, ang, c1, s1, den, t6 = tt
        TWO_PI = 2.0 * math.pi
        negpi = tile(st, "negpi", [128, 1])
        S.op("dve", lambda e: e.memset(negpi.t[:], -math.pi), writes=[negpi.b])
        S.op("act", lambda e: e.activation(out=dtr.t[:], in_=ldt, func=AF.Exp), reads=[r3.b], writes=[dtr.b])
        S.op("dve", lambda e: e.tensor_tensor(out=mag.t[:], in0=ar, in1=dtr.t[:], op=ALU.mult), reads=[r3.b, dtr.b], writes=[mag.b])
        S.op("act", lambda e: e.activation(out=mag.t[:], in_=mag.t[:], func=AF.Exp), reads=[mag.b], writes=[mag.b])
        S.op("dve", lambda e: e.tensor_tensor(out=ang.t[:], in0=ai, in1=dtr.t[:], op=ALU.mult), reads=[r3.b, dtr.b], writes=[ang.b])
        tiI = tile(st, "tiI", [128, 2048], mybir.dt.int32)
        range_sin(s1.t[:], ang.t[:], 0.0, den.t[:], tiI.t[:], s1.b, ang.b, den.b, tiI.b)
        range_sin(c1.t[:], ang.t[:], 0.5 * math.pi, den.t[:], tiI.t[:], c1.b, ang.b, den.b, tiI.b)
        S.op("dve", lambda e: e.tensor_tensor(out=c1.t[:], in0=c1.t[:], in1=mag.t[:], op=ALU.mult), reads=[c1.b, mag.b], writes=[c1.b])
        S.op("dve", lambda e: e.tensor_scalar(out=c1.t[:], in0=c1.t[:], scalar1=-1.0, scalar2=None, op0=ALU.add), reads=[c1.b], writes=[c1.b])
        S.op("dve", lambda e: e.tensor_tensor(out=s1.t[:], in0=s1.t[:], in1=mag.t[:], op=ALU.mult), reads=[s1.b, mag.b], writes=[s1.b])
        S.op("dve", lambda e: e.tensor_tensor(out=den.t[:], in0=ar, in1=ar, op=ALU.mult), reads=[r3.b], writes=[den.b])
        S.op("dve", lambda e: e.tensor_tensor(out=t6.t[:], in0=ai, in1=ai, op=ALU.mult), reads=[r3.b], writes=[t6.b])
        S.op("dve", lambda e: e.tensor_tensor(out=den.t[:], in0=den.t[:], in1=t6.t[:], op=ALU.add), reads=[den.b, t6.b], writes=[den.b])
        S.op("dve", lambda e: e.reciprocal(out=den.t[:], in_=den.t[:]), reads=[den.b], writes=[den.b])
        S.op("dve", lambda e: e.tensor_tensor(out=mag.t[:], in0=c1.t[:], in1=ar, op=ALU.mult), reads=[c1.b, r3.b], writes=[mag.b])
        S.op("dve", lambda e: e.tensor_tensor(out=t6.t[:], in0=s1.t[:], in1=ai, op=ALU.mult), reads=[s1.b, r3.b], writes=[t6.b])
        S.op("dve", lambda e: e.tensor_tensor(out=mag.t[:], in0=mag.t[:], in1=t6.t[:], op=ALU.add), reads=[mag.b, t6.b], writes=[mag.b])
        S.op("dve", lambda e: e.tensor_tensor(out=mag.t[:], in0=mag.t[:], in1=den.t[:], op=ALU.mult), reads=[mag.b, den.b], writes=[mag.b])
        S.op("dve", lambda e: e.tensor_tensor(out=ang.t[:], in0=s1.t[:], in1=ar, op=ALU.mult), reads=[s1.b, r3.b], writes=[ang.b])
        S.op("dve", lambda e: e.tensor_tensor(out=t6.t[:], in0=c1.t[:], in1=ai, op=ALU.mult), reads=[c1.b, r3.b], writes=[t6.b])
        S.op("dve", lambda e: e.tensor_tensor(out=ang.t[:], in0=ang.t[:], in1=t6.t[:], op=ALU.subtract), reads=[ang.b, t6.b], writes=[ang.b])
        S.op("dve", lambda e: e.tensor_tensor(out=ang.t[:], in0=ang.t[:], in1=den.t[:], op=ALU.mult), reads=[ang.b, den.b], writes=[ang.b])
        qre, qim = mag, ang
        S.op("dve", lambda e: e.tensor_tensor(out=c1.t[:], in0=qre.t[:], in1=bl.t[:, 0, :], op=ALU.mult), reads=[qre.b, bl.b], writes=[c1.b])
        S.op("dve", lambda e: e.tensor_tensor(out=s1.t[:], in0=qim.t[:], in1=bl.t[:, 1, :], op=ALU.mult), reads=[qim.b, bl.b], writes=[s1.b])
        S.op("dve", lambda e: e.tensor_tensor(out=BT.t[:, 0, :], in0=c1.t[:], in1=s1.t[:], op=ALU.subtract), reads=[c1.b, s1.b], writes=[BT.b])
        S.op("dve", lambda e: e.tensor_tensor(out=c1.t[:], in0=qre.t[:], in1=bl.t[:, 1, :], op=ALU.mult), reads=[qre.b, bl.b], writes=[c1.b])
        S.op("dve", lambda e: e.tensor_tensor(out=s1.t[:], in0=qim.t[:], in1=bl.t[:, 0, :], op=ALU.mult), reads=[qim.b, bl.b], writes=[s1.b])
        S.op("dve", lambda e: e.tensor_tensor(out=BT.t[:, 1, :], in0=c1.t[:], in1=s1.t[:], op=ALU.add), reads=[c1.b, s1.b], writes=[BT.b])
        S.op("sp", lambda e: e.dma_start(out=bl.t[:], in_=CTlay), writes=[bl.b], dma="ld_bl")
        S.op("act", lambda e: e.copy(out=CT.t[:, 0, :], in_=bl.t[:, 0, :]), reads=[bl.b], writes=[CT.b])
        S.op("act", lambda e: e.mul(out=CT.t[:, 1, :], in_=bl.t[:, 1, :], mul=-1.0), reads=[bl.b], writes=[CT.b])
        S.op("act", lambda e: e.mul(out=CT.t[:, 2, :], in_=bl.t[:, 0, :], mul=-1.0), reads=[bl.b], writes=[CT.b])
        S.barrier()
        st.close()

    TAB = {}

    def alloc_tables(stk):
        TAB["cosT"] = tile(stk, "cosT", [128, 16, 512], BF)
        TAB["sinT"] = tile(stk, "sinT", [128, 16, 512], BF)
        TAB["lastc"] = {"p": tile(stk, "lastc_p", [128, 3, 16]), "s": tile(stk, "lastc_s", [128, 3, 16])}

    def table_ops(st3):
        cosT, sinT, lastc = TAB["cosT"], TAB["sinT"], TAB["lastc"]
        angT = tile(st3, "angT", [128, 8, 512])
        tfT = tile(st3, "tfT", [128, 8, 512])
        tiT = tile(st3, "tiT", [128, 8, 512], mybir.dt.int32)
        d32 = tile(st3, "d32", [128, 8, 512])
        fl = lambda a_: a_.rearrange("p j t -> p (j t)")
        S.begin_capture()
        for hf in range(2):
            for jj in range(8):
                j = hf * 8 + jj
                S.op("dve", lambda e, j=j, jj=jj: e.tensor_scalar(out=angT.t[:, jj, :], in0=iota_t1, scalar1=theta.t[:, j:j + 1], scalar2=None,
                                                              op0=ALU.mult), reads=[cf.b, theta.b], writes=[angT.b])
            for which, tab, shift in ((1, sinT, 0.0), (0, cosT, 0.5 * math.pi)):
                range_sin(fl(d32.t[:]), fl(angT.t[:]), shift, fl(tfT.t[:]), fl(tiT.t[:]), d32.b, angT.b, tfT.b, tiT.b)
                S.op("act", lambda e, tab=tab, hf=hf: e.copy(out=tab.t[:, hf * 8:(hf + 1) * 8, :], in_=d32.t[:]), reads=[d32.b], writes=[tab.b])
                for pk, Lc in (("p", 512), ("s", 64)):
                    S.op("dve", lambda e, pk=pk, Lc=Lc, which=which, hf=hf: e.tensor_copy(
                        out=lastc[pk].t[:, which, hf * 8:(hf + 1) * 8], in_=d32.t[:, :, Lc - 1]), reads=[d32.b], writes=[lastc[pk].b])
        for pk in ("p", "s"):
            S.op("dve", lambda e, pk=pk: e.tensor_scalar(out=lastc[pk].t[:, 2, :], in0=lastc[pk].t[:, 1, :], scalar1=-1.0, scalar2=None,
                                                       op0=ALU.mult), reads=[lastc[pk].b], writes=[lastc[pk].b])
        return S.end_capture()

    def phaseN():
        st = contextlib.ExitStack()
        NX = 4
        xt = [tile(st, "nx%d" % i, [128, D]) for i in range(NX)]
        sqj = tile(st, "nsq", [128, D], BF)
        sst = [tile(st, "nss%d" % i, [128, 2]) for i in range(NX)]
        xst = [tile(st, "nxs%d" % i, [128, D], BF) for i in range(NX)]
        hTn = [tile(st, "n_hT%d" % i, [128, 8, 512], BF) for i in range(2)]
        tps = [PS[6], PS[7]]
        tiles = []
        for pk in ("s", "p"):
            P = PATHS[pk]
            for s0 in range(0, P["T"], P["SP"]):
                for ti in range(P["SP"] // P["TT"]):
                    tiles.append((pk, s0, ti))
        spans = {}
        for (pk, s0, ti) in tiles:
            spans.setdefault((pk, s0), len(spans))

        def load(i):
            if i >= len(tiles):
                return
            pk, s0, ti = tiles[i]
            TT = PATHS[pk]["TT"]
            t0 = s0 + ti * TT
            x = xt[i % NX]
            S.op("sp", lambda e: e.dma_start(out=x.t[0:TT, :], in_=xin[pk][t0:t0 + TT, :]), writes=[x.b], dma="nx%d" % (i % NX))

        def chain(i):
            if i >= len(tiles):
                return
            pk, s0, ti = tiles[i]
            TT = PATHS[pk]["TT"]
            x, xs, ss = xt[i % NX], xst[i % NX], sst[i % NX]
            S.op("act", lambda e: e.activation(out=sqj.t[0:TT, :], in_=x.t[0:TT, :], func=AF.Square, accum_out=ss.t[0:TT, 0:1]),
                 reads=[x.b], writes=[sqj.b, ss.b])
            S.op("act", lambda e: e.activation(out=ss.t[0:TT, 1:2], in_=ss.t[0:TT, 0:1], func=AF.Ln, scale=1.0 / D, bias=epsc.t[0:TT, 0:1]),
                 reads=[ss.b, epsc.b], writes=[ss.b])
            S.op("act", lambda e: e.activation(out=ss.t[0:TT, 1:2], in_=ss.t[0:TT, 1:2], func=AF.Exp, scale=-0.5), reads=[ss.b], writes=[ss.b])
            S.op("dve", lambda e: e.tensor_scalar(out=xs.t[0:TT, :], in0=x.t[0:TT, :], scalar1=ss.t[0:TT, 1:2], scalar2=None, op0=ALU.mult),
                 reads=[x.b, ss.b], writes=[xs.b])

        def transp(i):
            pk, s0, ti = tiles[i]
            P = PATHS[pk]
            TT, pi, SPn = P["TT"], P["pi"], P["SP"]
            xs = xst[i % NX]
            tp = tps[i % 2]
            tpb = tp.t[:, :].bitcast(BF)
            hT = hTn[spans[(pk, s0)] % 2]
            for k in range(8):
                S.op("pe", lambda e, k=k: e.transpose(out=tpb[:, k * 128:k * 128 + TT], in_=xs.t[0:TT, k * 128:(k + 1) * 128],
                                                      identity=ident_bf[0:TT, 0:TT]), reads=[xs.b, cb.b], writes=[tp.b])
            for k in range(8):
                if k % 2 == 0:
                    S.op("act", lambda e, k=k: e.activation(
                        out=hT.t[:, k, ti * TT:(ti + 1) * TT], in_=tpb[:, k * 128:k * 128 + TT], func=AF.Identity,
                        scale=gsT.t[:, k, pi:pi + 1], bias=modb.t[:, k, pi:pi + 1]), reads=[tp.b, gsT.b, modb.b], writes=[hT.b])
                else:
                    S.op("dve", lambda e, k=k: e.tensor_scalar(
                        out=hT.t[:, k, ti * TT:(ti + 1) * TT], in0=tpb[:, k * 128:k * 128 + TT],
                        scalar1=gsT.t[:, k, pi:pi + 1], scalar2=modb.t[:, k, pi:pi + 1], op0=ALU.mult, op1=ALU.add),
                        reads=[tp.b, gsT.b, modb.b], writes=[hT.b])
            if ti == SPn // TT - 1:
                S.op("pool", lambda e: e.dma_start(out=SC[pk]["hT"][:, s0:s0 + SPn].rearrange("(k p) t -> p k t", p=128),
                                                   in_=hT.t[:, :, 0:SPn]), reads=[hT.b], dma="hTsp%d" % (spans[(pk, s0)] % 2))

        tab_l = table_ops(st) if TAB else []
        per = (len(tab_l) + len(tiles) - 1) // len(tiles)
        for i in range(NX - 1):
            load(i)
        chain(0)
        for i in range(len(tiles)):
            load(i + NX - 1)
            chain(i + 1)
            transp(i)
            S.replay(tab_l[i * per:(i + 1) * per])
        S.replay(tab_l[len(tiles) * per:])
        S.barrier()
        st.close()

    def load_w_bf(wt, cols0, ncols, stage, wname):
        c = 0
        i = 0
        while c < ncols:
            n = min(512, ncols - c)
            sg_ = stage[i % 2]
            S.op("sp", lambda e, sg_=sg_, c=c, n=n: e.dma_start(out=sg_.t[:, :, 0:n], in_=w_in[:, :, cols0 + c:cols0 + c + n]),
                 writes=[sg_.b], dma="%s%d" % (wname, i % 2))
            eng = "act" if i % 2 == 0 else "dve"
            if eng == "act":
                S.op("act", lambda e, sg_=sg_, c=c, n=n: e.copy(out=wt.t[:, :, c:c + n], in_=sg_.t[:, :, 0:n]),
                     reads=[sg_.b], writes=[wt.b])
            else:
                S.op("dve", lambda e, sg_=sg_, c=c, n=n: e.tensor_copy(out=wt.t[:, :, c:c + n], in_=sg_.t[:, :, 0:n]),
                     reads=[sg_.b], writes=[wt.b])
            c += n
            i += 1
        return wt

    def phaseA():
        st = contextlib.ExitStack()
        wA = tile(st, "wA", [128, 8, 2056], BF)
        st2 = contextlib.ExitStack()
        stage = [tile(st2, "wstg%d" % i, [128, 8, 512]) for i in range(2)]
        load_w_bf(wA, 1024, 2056, stage, "wsA")
        S.barrier()
        st2.close()
        hTa = [tile(st, "a_hT%d" % i, [128, 8, 512], BF) for i in range(2)]
        NSET = 2
        WS = []
        for i in range(NSET):
            d = {}
            for nm in ("qf", "kf", "sq", "tmpn", "ko", "vo"):
                d[nm] = tile(st, "a_%s%d" % (nm, i), [128, 512])
            d["st8"] = tile(st, "a_st8%d" % i, [128, 4, 8])
            d["lft"] = tile(st, "a_lft%d" % i, [128, 8])
            d["lf"] = tile(st, "a_lf%d" % i, [128, 8])
            d["cum"] = tile(st, "a_cum%d" % i, [128, 8])
            d["q_aug"] = tile(st, "a_qaug%d" % i, [128, 8, 65], BF)
            d["k_aug"] = tile(st, "a_kaug%d" % i, [128, 8, 65], BF)
            d["v_aug"] = tile(st, "a_vaug%d" % i, [128, 8, 128], BF)
            d["i"] = i
            WS.append(d)
            S.op("dve", lambda e, d=d: e.memset(d["k_aug"].t[:, :, 64:65], 1.0), writes=[d["k_aug"].b])
            va = d["v_aug"]
            S.op("dve", lambda e, va=va: e.memset(va.t[:], 0.0), writes=[va.b])
            S.op("dve", lambda e, va=va: e.memset(va.t[:, 0::2, 64:65], 1.0), writes=[va.b])
            S.op("dve", lambda e, va=va: e.memset(va.t[:, 1::2, 0:1], 1.0), writes=[va.b])
        qst = [tile(st, "a_qst%d" % i, [65, 8, 512], BF) for i in range(2)]
        kst = [tile(st, "a_kst%d" % i, [65, 8, 512], BF) for i in range(2)]
        zsg = tile(st, "a_zsg", [128, 512])
        zaS = [tile(st, "a_zaS%d" % i, [128, 4, 512], BF) for i in range(2)]
        qps, kps, vps, fps, qTps, kTps = PS[0], PS[1], PS[2], PS[3], PS[4], PS[5]
        qTb = qTps.t[:, :].bitcast(BF)
        kTb = kTps.t[:, :].bitcast(BF)
        cnt = [0]
        state = dict(cum_prev=None, cum_TT=None)

        def h3(ap_):
            return ap_.rearrange("p (h d) -> p h d", d=64)

        def evac_tile(TT, W):
            S.op("act", lambda e: e.copy(out=W["qf"].t[0:TT, :], in_=qps.t[0:TT, :]), reads=[qps.b], writes=[W["qf"].b])
            S.op("act", lambda e: e.copy(out=W["kf"].t[0:TT, :], in_=kps.t[0:TT, :]), reads=[kps.b], writes=[W["kf"].b])
            S.op("act", lambda e: e.copy(out=W["vo"].t[0:TT, :], in_=vps.t[0:TT, :]), reads=[vps.b], writes=[W["vo"].b])
            S.op("dve", lambda e: e.tensor_tensor(out=W["lft"].t[0:TT, :], in0=fps.t[0:TT, 0:8], in1=smv("bf_bc")[0:TT, :], op=ALU.add),
                 reads=[fps.b, sm.b], writes=[W["lft"].b])

        def kv_tile(pk, blk, TT, tok0, col0, from_cache, kstg, qstg, W, part="abc"):
            P = PATHS[pk]
            c = W["i"]
            ko, vo, lf_, cm, va = W["ko"], W["vo"], W["lf"], W["cum"], W["v_aug"]
            q_aug, k_aug, sq, st8, tmpn, lft = W["q_aug"], W["k_aug"], W["sq"], W["st8"], W["tmpn"], W["lft"]
            if "a" not in part:
                pass
            elif from_cache:
                ck_, cv_ = W["kf"], W["vo"]
                S.op("sp", lambda e: e.dma_start(out=ck_.t[0:TT, :], in_=ckin[tok0:tok0 + TT, :]), writes=[ck_.b], dma="ck%d" % c)
                S.op("sp", lambda e: e.dma_start(out=cv_.t[0:TT, :], in_=cvin[tok0:tok0 + TT, :]), writes=[cv_.b], dma="cv%d" % c)
                S.op("sp", lambda e: e.dma_start(out=lf_.t[0:TT, :], in_=clfin[tok0:tok0 + TT, :]), writes=[lf_.b], dma="clf%d" % c)
                S.op("act", lambda e: e.copy(out=k_aug.t[0:TT, :, 0:64], in_=h3(ck_.t[0:TT, :])), reads=[ck_.b], writes=[k_aug.b])
            else:
                nq = tok0 - P["npast"]
                for which, src in ((0, W["qf"]), (1, W["kf"])):
                    S.op("act", lambda e, src=src: e.activation(out=sq.t[0:TT, :], in_=src.t[0:TT, :], func=AF.Square),
                         reads=[src.b], writes=[sq.b])
                    S.op("dve", lambda e, which=which: e.reduce_sum(out=st8.t[0:TT, 2 * which, :], in_=h3(sq.t[0:TT, :]), axis=AX.X),
                         reads=[sq.b], writes=[st8.b])
                    S.op("act", lambda e, which=which: e.activation(
                        out=st8.t[0:TT, 2 * which + 1, :], in_=st8.t[0:TT, 2 * which, :], func=AF.Ln, scale=1.0 / 64, bias=epsc.t[0:TT, 0:1]),
                        reads=[st8.b, epsc.b], writes=[st8.b])
                    S.op("act", lambda e, which=which: e.activation(
                        out=st8.t[0:TT, 2 * which + 1, :], in_=st8.t[0:TT, 2 * which + 1, :], func=AF.Exp, scale=-0.5),
                        reads=[st8.b], writes=[st8.b])
                    S.op("dve", lambda e, which=which, src=src: e.tensor_tensor(
                        out=h3(tmpn.t[0:TT, :]), in0=h3(src.t[0:TT, :]),
                        in1=st8.t[0:TT, 2 * which + 1, :].unsqueeze(2).to_broadcast([TT, 8, 64]), op=ALU.mult),
                        reads=[src.b, st8.b], writes=[tmpn.b])
                    if which == 0:
                        S.op("dve", lambda e: e.tensor_tensor(out=q_aug.t[0:TT, :, 0:64], in0=h3(tmpn.t[0:TT, :]),
                                                              in1=h3(qkg.t[0:TT, 0:512]), op=ALU.mult),
                             reads=[tmpn.b, qkg.b], writes=[q_aug.b])
                    else:
                        S.op("dve", lambda e: e.tensor_tensor(out=ko.t[0:TT, :], in0=tmpn.t[0:TT, :], in1=qkg.t[0:TT, 512:1024],
                                                              op=ALU.mult), reads=[tmpn.b, qkg.b], writes=[ko.b])
                        S.op("sp", lambda e: e.dma_start(out=O["k_" + pk][nq:nq + TT, :], in_=ko.t[0:TT, :]),
                             reads=[ko.b], dma="ko%d" % c)
                        S.op("act", lambda e: e.copy(out=k_aug.t[0:TT, :, 0:64], in_=h3(ko.t[0:TT, :])),
                             reads=[ko.b], writes=[k_aug.b])
                S.op("sp", lambda e: e.dma_start(out=O["v_" + pk][nq:nq + TT, :], in_=vo.t[0:TT, :]),
                     reads=[vo.b], dma="vo%d" % c)
                S.op("act", lambda e: e.activation(out=lft.t[0:TT, :], in_=lft.t[0:TT, :], func=AF.Exp, scale=-1.0),
                     reads=[lft.b], writes=[lft.b])
                S.op("act", lambda e: e.activation(out=lft.t[0:TT, :], in_=lft.t[0:TT, :], func=AF.Ln, bias=ones_f.t[0:TT, 0:1]),
                     reads=[lft.b, ones_f.b], writes=[lft.b])
                S.op("dve", lambda e: e.tensor_scalar(out=lf_.t[0:TT, :], in0=lft.t[0:TT, :], scalar1=-1.0, scalar2=None, op0=ALU.mult),
                     reads=[lft.b], writes=[lf_.b])
                S.op("sp", lambda e: e.dma_start(out=O["lf_" + pk][nq:nq + TT, :], in_=lf_.t[0:TT, :]),
                     reads=[lf_.b], dma="lfo%d" % c)
            v4 = h3(vo.t[0:TT, :])
            if "a" in part:
                S.op("dve", lambda e: e.tensor_copy(out=va.t[0:TT, 0::2, 0:64], in_=v4[:, 0::2, :]), reads=[vo.b], writes=[va.b])
                S.op("dve", lambda e: e.tensor_copy(out=va.t[0:TT, 1::2, 64:128], in_=v4[:, 1::2, :]), reads=[vo.b], writes=[va.b])
                S.op("pool", lambda e: e.dma_start(out=SC[pk]["va"][tok0:tok0 + TT, :, :], in_=va.t[0:TT, :, :]),
                     reads=[va.b], dma="vao%d" % c)
            if "b" in part:
                kv_tile_b(pk, blk, TT, from_cache, W)
            if "c" in part:
                kv_tile_c(TT, col0, from_cache, kstg, qstg, W)

        def kv_tile_b(pk, blk, TT, from_cache, W):
            lf_, cm, q_aug = W["lf"], W["cum"], W["q_aug"]
            cps = fps.t[0:TT, 16:24]
            prev = state["cum_prev"]
            S.op("pe", lambda e: e.matmul(cps, lhsT=tri_f[0:TT, 0:TT], rhs=lf_.t[0:TT, :], start=True, stop=(prev is None)),
                 reads=[lf_.b, cf.b], writes=[fps.b])
            if prev is not None:
                pTT = state["cum_TT"]
                sel = sel128 if pTT == 128 else sel64
                S.op("pe", lambda e: e.matmul(cps, lhsT=sel[0:pTT, 0:TT], rhs=prev.t[0:pTT, :], start=False, stop=True),
                     reads=[prev.b, cf.b], writes=[fps.b])
            S.op("act", lambda e: e.copy(out=cm.t[0:TT, :], in_=cps), reads=[fps.b], writes=[cm.b])
            state["cum_prev"], state["cum_TT"] = cm, TT
            S.op("dve", lambda e: e.scalar_tensor_tensor(out=biasK[pk].t[0:TT, blk, :], in0=cm.t[0:TT, :], scalar=-1.0,
                                                         in1=negb8.t[0:TT, :], op0=ALU.mult, op1=ALU.add),
                 reads=[cm.b, negb8.b], writes=[biasK[pk].b])
            if not from_cache:
                S.op("dve", lambda e: e.tensor_scalar(out=q_aug.t[0:TT, :, 64:65], in0=cm.t[0:TT, :].unsqueeze(2), scalar1=8.0,
                                                      scalar2=None, op0=ALU.mult), reads=[cm.b], writes=[q_aug.b])

        def kv_tile_c(TT, col0, from_cache, kstg, qstg, W):
            q_aug, k_aug = W["q_aug"], W["k_aug"]
            for h in range(8):
                S.op("pe", lambda e, h=h: e.transpose(out=kTb[0:65, h * 128:h * 128 + TT], in_=k_aug.t[0:TT, h, :],
                                                      identity=ident_bf[0:TT, 0:TT]), reads=[k_aug.b, cb.b], writes=[kTps.b])
            S.op("act", lambda e: e.copy(out=kstg.t[0:65, :, col0:col0 + TT],
                                         in_=kTb[0:65, :].rearrange("p (h t) -> p h t", t=128)[:, :, 0:TT]),
                 reads=[kTps.b], writes=[kstg.b])
            if not from_cache:
                for h in range(8):
                    S.op("pe", lambda e, h=h: e.transpose(out=qTb[0:65, h * 128:h * 128 + TT], in_=q_aug.t[0:TT, h, :],
                                                          identity=ident_bf[0:TT, 0:TT]), reads=[q_aug.b, cb.b], writes=[qTps.b])
                S.op("dve", lambda e: e.tensor_copy(out=qstg.t[0:65, :, col0:col0 + TT],
                                                    in_=qTb[0:65, :].rearrange("p (h t) -> p h t", t=128)[:, :, 0:TT]),
                     reads=[qTps.b], writes=[qstg.b])

        def load_hT(pk_, s0_, hT_):
            n_ = PATHS[pk_]["SP"]
            S.op("sp", lambda e: e.dma_start(out=hT_.t[:, :, 0:n_], in_=SC[pk_]["hT"][:, s0_:s0_ + n_].rearrange("(k p) t -> p k t", p=128)),
                 writes=[hT_.b], dma="ahT%d" % (0 if hT_ is hTa[0] else 1))

        spc = [0]
        LAG = 2
        pending = []
        pend_b = []

        def flush(n_keep):
            while len(pending) > n_keep:
                S.replay(pending.pop(0))

        for pk in ("s", "p"):
            P = PATHS[pk]
            TT, SPn, T, npast = P["TT"], P["SP"], P["T"], P["npast"]
            state["cum_prev"] = None
            for s0 in range(0, npast, 512):
                sc_ = spc[0]
                spc[0] += 1
                kstg = kst[sc_ % 2]
                for ti in range(4):
                    W = WS[cnt[0] % NSET]
                    cnt[0] += 1
                    kv_tile(pk, (s0 + ti * 128) // 128, 128, s0 + ti * 128, ti * 128, True, kstg, None, W)
                S.op("pool", lambda e, kstg=kstg, s0=s0: e.dma_start(
                    out=SC[pk]["kTa"][:, :, s0:s0 + 512].rearrange("h r t -> r h t"), in_=kstg.t[0:65, :, 0:512]),
                    reads=[kstg.b], dma="kst%d" % (sc_ % 2))
            span_list = list(range(0, T, SPn))
            load_hT(pk, span_list[0], hTa[spc[0] % 2])
            for sidx, s0 in enumerate(span_list):
                sc_ = spc[0]
                spc[0] += 1
                kstg, qstg, zas = kst[sc_ % 2], qst[sc_ % 2], zaS[sc_ % 2]
                hT = hTa[sc_ % 2]
                if sidx + 1 < len(span_list):
                    load_hT(pk, span_list[sidx + 1], hTa[(sc_ + 1) % 2])
                ntile = SPn // TT
                for ti in range(ntile):
                    cols = slice(ti * TT, (ti + 1) * TT)
                    W = WS[cnt[0] % NSET]
                    cnt[0] += 1
                    for (ps, c0, n) in ((qps, 0, 512), (kps, 512, 512), (vps, 1024, 512), (fps, 2048, 8)):
                        for k in range(8):
                            S.op("pe", lambda e, ps=ps, c0=c0, n=n, k=k, cols=cols: e.matmul(
                                ps.t[0:TT, 0:n], lhsT=hT.t[:, k, cols], rhs=wA.t[:, k, c0:c0 + n],
                                start=(k == 0), stop=(k == 7)), reads=[hT.b, wA.b], writes=[ps.b])
                    evac_tile(TT, W)
                    tok0 = npast + s0 + ti * TT
                    if pend_b:
                        S.replay(pend_b.pop(0))
                    kv_tile(pk, tok0 // 128, TT, tok0, ti * TT, False, kstg, qstg, W, part="a")
                    flush(0)
                    S.begin_capture()
                    kv_tile(pk, tok0 // 128, TT, tok0, ti * TT, False, kstg, qstg, W, part="b")
                    pend_b.append(S.end_capture())
                    S.begin_capture()
                    kv_tile(pk, tok0 // 128, TT, tok0, ti * TT, False, kstg, qstg, W, part="c")
                    lst = S.end_capture()
                    if ti == ntile - 1:
                        S.begin_capture()
                        k0 = npast + s0
                        S.op("pool", lambda e, kstg=kstg, k0=k0: e.dma_start(
                            out=SC[pk]["kTa"][:, :, k0:k0 + SPn].rearrange("h r t -> r h t"), in_=kstg.t[0:65, :, 0:SPn]),
                            reads=[kstg.b], dma="kst%d" % (sc_ % 2))
                        S.op("pool", lambda e, qstg=qstg, s0=s0: e.dma_start(
                            out=SC[pk]["qTa"][:, :, s0:s0 + SPn].rearrange("h r t -> r h t"), in_=qstg.t[0:65, :, 0:SPn]),
                            reads=[qstg.b], dma="qst%d" % (sc_ % 2))
                        lst = lst + S.end_capture()
                    pending.append(lst)
                for cch in range(4):
                    zps = PS[6]
                    for k in range(8):
                        S.op("pe", lambda e, cch=cch, k=k: e.matmul(
                            zps.t[:, 0:SPn], lhsT=wA.t[:, k, 1536 + cch * 128:1536 + (cch + 1) * 128], rhs=hT.t[:, k, 0:SPn],
                            start=(k == 0), stop=(k == 7)), reads=[hT.b, wA.b], writes=[zps.b])
                    S.op("act", lambda e, cch=cch, zas=zas: e.activation(out=zas.t[:, cch, 0:SPn], in_=zps.t[:, 0:SPn], func=AF.Silu),
                         reads=[zps.b], writes=[zas.b])
                S.op("pool", lambda e, zas=zas, s0=s0: e.dma_start(
                    out=SC[pk]["zaG"][:, s0:s0 + SPn].rearrange("(c p) t -> p c t", p=128), in_=zas.t[:, :, 0:SPn]),
                    reads=[zas.b], dma="zas%d" % (sc_ % 2))
            while pend_b:
                S.replay(pend_b.pop(0))
            flush(0)
        S.barrier()
        st.close()

    def phaseB():
        st = contextlib.ExitStack()
        wB = tile(st, "wB", [128, 8, 1024], BF)
        wg = tile(st, "wg", [128, 4, 512], BF)
        cosT, sinT, lastc = TAB["cosT"], TAB["sinT"], TAB["lastc"]
        st2 = contextlib.ExitStack()
        stage = [tile(st2, "wstgB%d" % i, [128, 8, 512]) for i in range(2)]
        wgs = tile(st2, "wgs", [128, 4, 512])
        load_w_bf(wB, 0, 1024, stage, "wsB")
        S.op("sp", lambda e: e.dma_start(out=wgs.t[:], in_=w_glu), writes=[wgs.b], dma="ld_wg")
        S.op("act", lambda e: e.copy(out=wg.t[:], in_=wgs.t[:]), reads=[wgs.b], writes=[wg.b])
        S.barrier()
        st2.close()
        hTb = [tile(st, "b_hT%d" % i, [128, 8, 512], BF) for i in range(2)]
        uTs = [tile(st, "uT%d" % i, [128, 4, 512], BF) for i in range(2)]
        zsGs = [tile(st, "zsG%d" % i, [128, 4, 512], BF) for i in range(2)]
        zsg = tile(st, "b_zsg", [128, 512])
        bX = [[{nm: tile(st, "b_%s_%d%d" % (nm, par, tl_), [128, 512], BF) for nm in ("brb", "bib")} for tl_ in range(2)] for par in range(2)]
        NSET = 2
        TS = []
        for i in range(NSET):
            d = {}
            for nm in ("brb", "bib", "t1", "t2", "t3", "t4", "wre", "wim", "zrb", "zib", "p1", "p2", "p3", "p4"):
                d[nm] = tile(st, "b_%s%d" % (nm, i), [128, 512], BF)
            for nm in ("zre", "zim"):
                d[nm] = tile(st, "b_%s%d" % (nm, i), [128, 512])
            d["ctmp"] = tile(st, "b_ctmp%d" % i, [128, 2])
            d["bre_ps"], d["bim_ps"] = PS[1 + 2 * i], PS[2 + 2 * i]
            TS.append(d)
        hst = tile(st, "b_hst", [128, 32])
        hb = [Buf() for _ in range(16)]
        yT = tile(st, "b_yT", [128, 512])
        x2 = tile(st, "b_x2", [128, 512])
        gTs = [tile(st, "b_gT%d" % i, [128, 4, 512], BF) for i in range(2)]
        carry_bg = []
        sgl = tile(st, "b_sgl", [128, 512])
        y2 = tile(st, "b_y2", [128, 512])
        yS = [tile(st, "b_yS%d" % i, [128, 4, 512], BF) for i in range(2)]
        ups, yps, gps = PS[0], PS[5], PS[6]
        spc = [0]

        def tile_ops(pk, L, j, cq, jj, d, uT):
            cs, sn = cosT.t[:, j, 0:L], sinT.t[:, j, 0:L]
            lc = lastc[pk]
            ops_ = []
            A = ops_.append
            A(lambda: S.op("pe", lambda e: e.matmul(d["bre_ps"].t[:, 0:L], lhsT=BT.t[:, 0, j * 128:(j + 1) * 128], rhs=uT.t[:, cq, 0:L],
                                                    start=True, stop=True), reads=[BT.b, uT.b], writes=[d["bre_ps"].b]))
            A(lambda: S.op("pe", lambda e: e.matmul(d["bim_ps"].t[:, 0:L], lhsT=BT.t[:, 1, j * 128:(j + 1) * 128], rhs=uT.t[:, cq, 0:L],
                                                    start=True, stop=True), reads=[BT.b, uT.b], writes=[d["bim_ps"].b]))
            A(lambda: S.op("act", lambda e: e.copy(out=d["brb"].t[:, 0:L], in_=d["bre_ps"].t[:, 0:L]), reads=[d["bre_ps"].b], writes=[d["brb"].b]))
            A(lambda: S.op("act", lambda e: e.copy(out=d["bib"].t[:, 0:L], in_=d["bim_ps"].t[:, 0:L]), reads=[d["bim_ps"].b], writes=[d["bib"].b]))

            def tt(o, a_, b_, op, tab=None):
                rd = [d[a_].b] + ([d[b_].b] if isinstance(b_, str) else [tab.b])
                bb = d[b_].t[:, 0:L] if isinstance(b_, str) else b_
                return lambda: S.op("dve", lambda e: e.tensor_tensor(out=d[o].t[:, 0:L], in0=d[a_].t[:, 0:L], in1=bb, op=op),
                                    reads=rd, writes=[d[o].b])
            A(tt("t1", "brb", cs, ALU.mult, cosT))
            A(tt("t2", "bib", sn, ALU.mult, sinT))
            A(tt("wre", "t1", "t2", ALU.add))
            A(tt("t3", "bib", cs, ALU.mult, cosT))
            A(tt("t4", "brb", sn, ALU.mult, sinT))
            A(tt("wim", "t3", "t4", ALU.subtract))
            rb = rho.t[:, j:j + 1].to_broadcast([128, L])
            A(lambda: S.op("dve", lambda e: e.tensor_tensor_scan(out=d["zre"].t[:, 0:L], data0=rb, data1=d["wre"].t[:, 0:L],
                                                                 initial=hst.t[:, j:j + 1], op0=ALU.mult, op1=ALU.add),
                           reads=[d["wre"].b, hb[j], rho.b], writes=[d["zre"].b]))
            A(lambda: S.op("dve", lambda e: e.tensor_tensor_scan(out=d["zim"].t[:, 0:L], data0=rb, data1=d["wim"].t[:, 0:L],
                                                                 initial=hst.t[:, 16 + j:17 + j], op0=ALU.mult, op1=ALU.add),
                           reads=[d["wim"].b, hb[j], rho.b], writes=[d["zim"].b]))
            A(lambda: S.op("act", lambda e: e.copy(out=d["zrb"].t[:, 0:L], in_=d["zre"].t[:, 0:L]), reads=[d["zre"].b], writes=[d["zrb"].b]))
            A(lambda: S.op("act", lambda e: e.copy(out=d["zib"].t[:, 0:L], in_=d["zim"].t[:, 0:L]), reads=[d["zim"].b], writes=[d["zib"].b]))
            ct = d["ctmp"]
            A(lambda: S.op("act", lambda e: e.activation(out=ct.t[:, 0:1], in_=d["zim"].t[:, L - 1:L], func=AF.Identity,
                                                         scale=lc.t[:, 2, j:j + 1]), reads=[d["zim"].b, lc.b], writes=[ct.b]))
            A(lambda: S.op("act", lambda e: e.activation(out=ct.t[:, 1:2], in_=d["zim"].t[:, L - 1:L], func=AF.Identity,
                                                         scale=lc.t[:, 0, j:j + 1]), reads=[d["zim"].b, lc.b], writes=[ct.b]))
            A(lambda: S.op("act", lambda e: e.activation(out=hst.t[:, j:j + 1], in_=d["zre"].t[:, L - 1:L], func=AF.Identity,
                                                         scale=lc.t[:, 0, j:j + 1], bias=ct.t[:, 0:1]),
                           reads=[d["zre"].b, lc.b, ct.b], writes=[hb[j]]))
            A(lambda: S.op("act", lambda e: e.activation(out=hst.t[:, 16 + j:17 + j], in_=d["zre"].t[:, L - 1:L], func=AF.Identity,
                                                         scale=lc.t[:, 1, j:j + 1], bias=ct.t[:, 1:2]),
                           reads=[d["zre"].b, lc.b, ct.b], writes=[hb[j]]))
            A(tt("p1", "zrb", cs, ALU.mult, cosT))
            A(tt("p2", "zib", sn, ALU.mult, sinT))
            A(tt("p3", "zrb", sn, ALU.mult, sinT))
            A(tt("p4", "zib", cs, ALU.mult, cosT))
            for q_, (pl, nm) in enumerate(((0, "p1"), (2, "p2"), (1, "p3"), (1, "p4"))):
                A(lambda pl=pl, nm=nm, q_=q_: S.op("pe", lambda e: e.matmul(
                    yps.t[:, 0:L], lhsT=CT.t[:, pl, j * 128:(j + 1) * 128], rhs=d[nm].t[:, 0:L],
                    start=(jj == 0 and q_ == 0), stop=(jj == 3 and q_ == 3)), reads=[CT.b, d[nm].b], writes=[yps.b]))
            return ops_

        for pk in ("s", "p"):
            P = PATHS[pk]
            SPn, T = P["SP"], P["T"]
            L = SPn
            if pk == "s":
                S.op("dve", lambda e: e.tensor_copy(out=hst.t[:, 0:16], in_=smv("stT_re")), reads=[sm.b] + hb, writes=hb)
                S.op("dve", lambda e: e.tensor_copy(out=hst.t[:, 16:32], in_=smv("stT_im")), reads=[sm.b] + hb, writes=hb)
            else:
                S.op("dve", lambda e: e.memset(hst.t[:], 0.0), reads=hb, writes=hb)
            def prologue(s0, uT, zsG):
                hT = hTb[(s0 // SPn) % 2]
                S.begin_capture()
                S.op("sp", lambda e: e.dma_start(out=hT.t[:, :, 0:L], in_=SC[pk]["hT"][:, s0:s0 + L].rearrange("(k p) t -> p k t", p=128)),
                     writes=[hT.b], dma="bhT%d" % ((s0 // SPn) % 2))
                norm_l = S.end_capture()
                groups = []
                for sec in (0, 1):
                    for cch in range(4):
                        S.begin_capture()
                        for k in range(8):
                            S.op("pe", lambda e, sec=sec, cch=cch, k=k: e.matmul(
                                ups.t[:, 0:L], lhsT=wB.t[:, k, sec * 512 + cch * 128:sec * 512 + (cch + 1) * 128],
                                rhs=hT.t[:, k, 0:L], start=(k == 0), stop=(k == 7)), reads=[hT.b, wB.b], writes=[ups.b])
                        if sec == 0:
                            S.op("act", lambda e, cch=cch: e.copy(out=uT.t[:, cch, 0:L], in_=ups.t[:, 0:L]), reads=[ups.b], writes=[uT.b])
                        else:
                            S.op("act", lambda e, cch=cch: e.activation(out=zsG.t[:, cch, 0:L], in_=ups.t[:, 0:L], func=AF.Silu),
                                 reads=[ups.b], writes=[zsG.b])
                        groups.append(S.end_capture())
                return norm_l, groups

            span_list = list(range(0, T, SPn))
            pro_n, pro_g = prologue(span_list[0], uTs[spc[0] % 2], zsGs[spc[0] % 2])
            S.replay(pro_n)
            for g_ in pro_g:
                S.replay(g_)
            for sidx, s0 in enumerate(span_list):
                sc_ = spc[0]
                spc[0] += 1
                ys, uT, zsG = yS[sc_ % 2], uTs[sc_ % 2], zsGs[sc_ % 2]
                nxt, nxt_groups = [], []
                if sidx + 1 < len(span_list):
                    nxt, nxt_groups = prologue(span_list[sidx + 1], uTs[(sc_ + 1) % 2], zsGs[(sc_ + 1) % 2])
                pairs = [(cq, pr) for cq in range(4) for pr in range(2)]

                def pair_ops(n):
                    cq, pr = pairs[n]
                    outl = []
                    for tl_ in range(2):
                        dd = dict(TS[tl_])
                        dd.update(bX[n % 2][tl_])
                        outl.append(tile_ops(pk, L, cq * 4 + pr * 2 + tl_, cq, pr * 2 + tl_, dd, uT))
                    return outl
                cur = pair_ops(0)
                for i in range(4):
                    cur[0][i]()
                    cur[1][i]()
                gT = gTs[sc_ % 2]
                bg = list(carry_bg)
                bg_norm = list(nxt)
                del carry_bg[:]

                def bg_step():
                    if bg:
                        S.replay([bg.pop(0)])
                    if bg_norm:
                        S.replay([bg_norm.pop(0)])

                def gelu_ops(cq):
                    S.begin_capture()
                    dcol = smv("dT")[:, cq:cq + 1]
                    S.op("dve", lambda e: e.scalar_tensor_tensor(
                        out=yT.t[:, 0:L], in0=uT.t[:, cq, 0:L], scalar=dcol, in1=yps.t[:, 0:L], op0=ALU.mult, op1=ALU.add),
                        reads=[uT.b, yps.b, sm.b], writes=[yT.b])
                    S.op("act", lambda e: e.activation(out=x2.t[:, 0:L], in_=yT.t[:, 0:L], func=AF.Square), reads=[yT.b], writes=[x2.b])
                    S.op("pool", lambda e: e.tensor_scalar(out=x2.t[:, 0:L], in0=x2.t[:, 0:L], scalar1=0.044715, scalar2=1.0,
                                                          op0=ALU.mult, op1=ALU.add), reads=[x2.b], writes=[x2.b])
                    S.op("pool", lambda e: e.tensor_tensor(out=x2.t[:, 0:L], in0=x2.t[:, 0:L], in1=yT.t[:, 0:L], op=ALU.mult),
                         reads=[x2.b, yT.b], writes=[x2.b])
                    S.op("act", lambda e: e.activation(out=x2.t[:, 0:L], in_=x2.t[:, 0:L], func=AF.Sigmoid,
                                                       scale=2.0 * math.sqrt(2.0 / math.pi)), reads=[x2.b], writes=[x2.b])
                    S.op("pool", lambda e: e.tensor_tensor(out=gT.t[:, cq, 0:L], in0=x2.t[:, 0:L], in1=yT.t[:, 0:L], op=ALU.mult),
                         reads=[x2.b, yT.b], writes=[gT.b])
                    return S.end_capture()

                for n, (cq, pr) in enumerate(pairs):
                    la, lb = cur
                    if n + 1 < len(pairs):
                        cur = pair_ops(n + 1)
                        for i in range(4):
                            cur[0][i]()
                            cur[1][i]()
                    if n >= 4 and nxt_groups:
                        while bg_norm:
                            S.replay([bg_norm.pop(0)])
                        for g_ in nxt_groups[2 * (n - 4):2 * (n - 4) + 2]:
                            S.replay(g_)
                    na = len(la) - 4
                    for i in range(4, na):
                        la[i]()
                        lb[i]()
                        bg_step()
                    for i in range(na, len(la)):
                        la[i]()
                    for i in range(na, len(lb)):
                        lb[i]()
                    if pr == 1:
                        g_ = gelu_ops(cq)
                        if cq < 3:
                            merged = []
                            rest = bg[:]
                            del bg[:]
                            for op_ in g_:
                                merged.append(op_)
                                merged.extend(rest[:2])
                                rest = rest[2:]
                            bg.extend(merged + rest)
                        else:
                            last_gelu = g_
                while bg or bg_norm:
                    bg_step()
                S.begin_capture()
                for co in range(4):
                    for ci in range(4):
                        S.op("pe", lambda e, co=co, ci=ci: e.matmul(gps.t[:, 0:L], lhsT=wg.t[:, ci, co * 128:(co + 1) * 128],
                                                                    rhs=gT.t[:, ci, 0:L], start=(ci == 0), stop=(ci == 3)),
                             reads=[wg.b, gT.b], writes=[gps.b])
                    S.op("act", lambda e, co=co: e.activation(out=sgl.t[:, 0:L], in_=gps.t[:, 0:L], func=AF.Sigmoid,
                                                              bias=smv("b_gluT")[:, co:co + 1]), reads=[gps.b, sm.b], writes=[sgl.b])
                    S.op("pool", lambda e, co=co: e.tensor_tensor(out=y2.t[:, 0:L], in0=sgl.t[:, 0:L], in1=gT.t[:, co, 0:L], op=ALU.mult),
                         reads=[sgl.b, gT.b], writes=[y2.b])
                    S.op("pool", lambda e, co=co, ys=ys, zsG=zsG: e.tensor_tensor(out=ys.t[:, co, 0:L], in0=y2.t[:, 0:L], in1=zsG.t[:, co, 0:L],
                                                                               op=ALU.mult), reads=[y2.b, zsG.b], writes=[ys.b])
                S.op("pool", lambda e, ys=ys, s0=s0: e.dma_start(
                    out=SC[pk]["yssm"][:, s0:s0 + SPn].rearrange("(c p) t -> p c t", p=128), in_=ys.t[:, :, 0:SPn]),
                    reads=[ys.b], dma="ys%d" % (sc_ % 2))
                glu = S.end_capture()
                if sidx + 1 < len(span_list):
                    carry_bg.extend(last_gelu + glu)
                else:
                    S.replay(last_gelu + glu)
            S.op("pool", lambda e, pk=pk: e.dma_start(out=O["ssm_" + pk], in_=hst.t[:, :]), reads=hb, dma="hsto")
        S.barrier()
        st.close()

    def phase2C():
        st = contextlib.ExitStack()
        yatt = {"p": tile(st, "yatt_p", [128, 4, T_P], BF), "s": tile(st, "yatt_s", [128, 4, T_S], BF)}
        st2 = contextlib.ExitStack()
        kT = [tile(st2, "c_kT%d" % i, [65, T_P], BF) for i in range(2)]
        vA = [tile(st2, "c_vA%d" % i, [128, 64, 128], BF) for i in range(2)]
        qT = [tile(st2, "c_qT%d" % i, [65, 512], BF) for i in range(2)]
        zg = [tile(st2, "c_zg%d" % i, [128, 512], BF) for i in range(2)]
        NSB = 4
        kTb2 = [Buf(), Buf()]
        vAb2 = [Buf(), Buf()]
        pT = [tile(st2, "c_pT%d" % i, [128, 512], BF) for i in range(NSB)]
        rdens = [tile(st2, "c_rden%d" % i, [128, 512]) for i in range(2)]
        rdh = [tile(st2, "c_rdh%d" % i, [128, 512], BF) for i in range(2)]
        rdl = [tile(st2, "c_rdl%d" % i, [128, 512], BF) for i in range(2)]
        epi_q = []
        gbc = tile(st2, "c_gbc", [128, 512])
        sps = [PS[0], PS[1], PS[2], PS[6]]
        ops = [PS[3], PS[4]]
        bcps = PS[5]
        LA = 3
        heads = [(pk, h) for pk in ("s", "p") for h in range(8)]
        spans = []
        tasks = []
        for hi, (pk, h) in enumerate(heads):
            P = PATHS[pk]
            T, SPn, npast = P["T"], P["SP"], P["npast"]
            nk = npast + T
            for s0 in range(0, T, SPn):
                si = len(spans)
                qpos0 = npast + s0
                last_blk = (qpos0 + SPn - 1) // 128
                spans.append(dict(pk=pk, h=h, hi=hi, s0=s0, SPn=SPn, si=si, last_blk=last_blk, nk=nk))
                for j in range(last_blk + 1):
                    kk = min(128, nk - j * 128)
                    c0 = max(0, j * 128 - qpos0)
                    N = SPn - c0
                    diag = (j * 128 + kk - 1) > (qpos0 + c0)
                    tasks.append(dict(si=si, j=j, kk=kk, c0=c0, N=N, diag=diag, last=(j == last_blk)))
        ntasks_of = [sp["last_blk"] + 1 for sp in spans]
        last_span_of_head = {}
        for sp in spans:
            last_span_of_head[sp["hi"]] = sp["si"]

        def head_load(hi):
            if hi >= len(heads):
                return
            pk, h = heads[hi]
            hb = hi % 2
            kt, va = kT[hb], vA[hb]
            nk = PATHS[pk]["npast"] + PATHS[pk]["T"]
            nh = (nk // 256) * 128
            S.op("sp", lambda e: e.dma_start(out=kt.t[0:65, 0:nh], in_=SC[pk]["kTa"][h, :, 0:nh]), writes=[kt.b], dma="ckT%d" % hb)
            S.op("pool", lambda e: e.dma_start(out=kt.t[0:65, nh:nk], in_=SC[pk]["kTa"][h, :, nh:nk]), writes=[kTb2[hb]], dma="ckTb%d" % hb)
            nfull = nk // 128
            nfh = nfull // 2
            S.op("sp", lambda e: e.dma_start(
                out=va.t[:, 0:nfh, :], in_=SC[pk]["va"][0:nfh * 128, h, :].rearrange("(b p) c -> p b c", p=128)),
                writes=[va.b], dma="cvA%d" % hb)
            S.op("pool", lambda e: e.dma_start(
                out=va.t[:, nfh:nfull, :], in_=SC[pk]["va"][nfh * 128:nfull * 128, h, :].rearrange("(b p) c -> p b c", p=128)),
                writes=[vAb2[hb]], dma="cvAb%d" % hb)
            if nk % 128:
                r = nk % 128
                S.op("sp", lambda e: e.dma_start(out=va.t[0:r, nfull, :], in_=SC[pk]["va"][nfull * 128:nk, h, :]),
                     writes=[va.b], dma="cvA%d" % hb)

        def span_load(si):
            if si >= len(spans):
                return
            sp = spans[si]
            pk, h, s0, SPn = sp["pk"], sp["h"], sp["s0"], sp["SPn"]
            qb = si % 2
            prw = slice(64, 128) if h % 2 else slice(0, 64)
            S.op("sp", lambda e: e.dma_start(out=qT[qb].t[0:65, 0:SPn], in_=SC[pk]["qTa"][h, :, s0:s0 + SPn]),
                 writes=[qT[qb].b], dma="cqT%d" % qb)
            S.op("sp", lambda e: e.dma_start(out=zg[qb].t[prw, 0:SPn], in_=SC[pk]["zaG"][h * 64:(h + 1) * 64, s0:s0 + SPn]),
                 writes=[zg[qb].b], dma="czg%d" % qb)

        def emit_S(i):
            t = tasks[i]
            sp = spans[t["si"]]
            kt, qt = kT[sp["hi"] % 2], qT[sp["si"] % 2]
            sp_ = sps[i % NSB]
            j, kk, c0, N, diag = t["j"], t["kk"], t["c0"], t["N"], t["diag"]
            S.op("pe", lambda e: e.matmul(sp_.t[0:kk, 0:N], lhsT=kt.t[0:65, j * 128:j * 128 + kk], rhs=qt.t[0:65, c0:c0 + N],
                                          start=True, stop=(not diag)), reads=[kt.b, kTb2[sp["hi"] % 2], qt.b], writes=[sp_.b])
            if diag:
                nd = min(kk, N)
                S.op("pe", lambda e: e.matmul(sp_.t[0:kk, 0:nd], lhsT=ident_bf[0:kk, 0:kk], rhs=maskT[0:kk, 0:nd], start=False, stop=True),
                     reads=[cb.b], writes=[sp_.b])

        def emit_PV(i):
            t = tasks[i]
            sp = spans[t["si"]]
            pk, h, s0, SPn, si = sp["pk"], sp["h"], sp["s0"], sp["SPn"], sp["si"]
            va = vA[sp["hi"] % 2]
            sp_, ptl, opsb = sps[i % NSB], pT[i % NSB], ops[si % 2]
            j, kk, c0, N = t["j"], t["kk"], t["c0"], t["N"]
            odd = h % 2
            prw = slice(64, 128) if odd else slice(0, 64)
            drow = 0 if odd else 64
            M = 128 if odd else 65
            S.op("act", lambda e: e.activation(out=ptl.t[0:kk, 0:N], in_=sp_.t[0:kk, 0:N], func=AF.Exp, scale=0.125,
                                               bias=biasK[pk].t[0:kk, j, h:h + 1]), reads=[sp_.b, biasK[pk].b], writes=[ptl.b])
            S.op("pe", lambda e: e.matmul(opsb.t[0:M, c0:c0 + N], lhsT=va.t[0:kk, j, 0:M], rhs=ptl.t[0:kk, 0:N],
                                          start=(j == 0), stop=t["last"]), reads=[va.b, vAb2[sp["hi"] % 2], ptl.b], writes=[opsb.b])
            if t["last"]:
                zgt = zg[si % 2]
                rd = rdens[si % 2]
                rh, rl = rdh[si % 2], rdl[si % 2]
                S.op("dve", lambda e: e.reciprocal(out=rd.t[drow:drow + 1, 0:SPn], in_=opsb.t[drow:drow + 1, 0:SPn]),
                     reads=[opsb.b], writes=[rd.b])
                S.op("dve", lambda e: e.tensor_copy(out=rh.t[drow:drow + 1, 0:SPn], in_=rd.t[drow:drow + 1, 0:SPn]),
                     reads=[rd.b], writes=[rh.b])
                S.op("dve", lambda e: e.tensor_tensor(out=rl.t[drow:drow + 1, 0:SPn], in0=rd.t[drow:drow + 1, 0:SPn],
                                                      in1=rh.t[drow:drow + 1, 0:SPn], op=ALU.subtract),
                     reads=[rd.b, rh.b], writes=[rl.b])

                def epi():
                    S.op("pe", lambda e: e.matmul(bcps.t[:, 0:SPn], lhsT=ones_bf.t[drow:drow + 1, 0:128], rhs=rh.t[drow:drow + 1, 0:SPn],
                                                  start=True, stop=False), reads=[rh.b, ones_bf.b], writes=[bcps.b])
                    S.op("pe", lambda e: e.matmul(bcps.t[:, 0:SPn], lhsT=ones_bf.t[drow:drow + 1, 0:128], rhs=rl.t[drow:drow + 1, 0:SPn],
                                                  start=False, stop=True), reads=[rl.b, ones_bf.b], writes=[bcps.b])
                    S.op("dve", lambda e: e.tensor_tensor(out=gbc.t[prw, 0:SPn], in0=bcps.t[prw, 0:SPn], in1=zgt.t[prw, 0:SPn], op=ALU.mult),
                         reads=[bcps.b, zgt.b], writes=[gbc.b])
                    S.op("dve", lambda e: e.tensor_tensor(out=yatt[pk].t[prw, h // 2, s0:s0 + SPn], in0=opsb.t[prw, 0:SPn],
                                                          in1=gbc.t[prw, 0:SPn], op=ALU.mult), reads=[opsb.b, gbc.b], writes=[yatt[pk].b])
                    span_load(si + 2)
                    if last_span_of_head[sp["hi"]] == si:
                        head_load(sp["hi"] + 2)
                nxt_n = ntasks_of[si + 1] if si + 1 < len(spans) else 8
                epi_q.append([max(1, min(5, nxt_n - LA)), epi])

        head_load(0)
        head_load(1)
        span_load(0)
        span_load(1)
        for i in range(len(tasks) + LA):
            if i < len(tasks):
                emit_S(i)
            for it in epi_q:
                it[0] -= 1
            while epi_q and epi_q[0][0] <= 0:
                epi_q.pop(0)[1]()
            if i >= LA:
                emit_PV(i - LA)
        while epi_q:
            epi_q.pop(0)[1]()
        S.barrier()
        st2.close()
        wos = [tile(st, "c_wos%d" % i, [128, 1024]) for i in range(2)]
        wo = {"p": tile(st, "c_wo_p", [128, 8, 1024], BF), "s": tile(st, "c_wo_s", [128, 8, 1024], BF)}
        for k in range(8):
            wsl = wos[k % 2]
            S.op("sp", lambda e, wsl=wsl, k=k: e.dma_start(out=wsl.t[:], in_=w_out[:, k, :]), writes=[wsl.b], dma="cwo%d" % (k % 2))
            for pk in ("p", "s"):
                pi = PATHS[pk]["pi"]
                S.op("dve", lambda e, wsl=wsl, k=k, pk=pk, pi=pi: e.tensor_tensor(out=wo[pk].t[:, k, :], in0=wsl.t[:], in1=gate_bc.t[:, pi, :],
                                                                               op=ALU.mult), reads=[wsl.b, gate_bc.b], writes=[wo[pk].b])
        ysm = [tile(st, "c_ysm%d" % i, [128, 4, 512], BF) for i in range(2)]
        xr = [tile(st, "c_xr%d" % i, [128, D]) for i in range(2)]
        yo = [tile(st, "c_yo%d" % i, [128, D]) for i in range(2)]
        o_ps = [PS[0], PS[1], PS[2], PS[3]]
        sc2 = [0]
        tc2 = [0]
        for pk in ("s", "p"):
            P = PATHS[pk]
            T, SPn, TT = P["T"], P["SP"], P["TT"]
            for s0 in range(0, T, SPn):
                sb = sc2[0] % 2
                sc2[0] += 1
                ym = ysm[sb]
                S.op("sp", lambda e, ym=ym, s0=s0: e.dma_start(
                    out=ym.t[:, :, 0:SPn], in_=SC[pk]["yssm"][:, s0:s0 + SPn].rearrange("(c p) t -> p c t", p=128)),
                    writes=[ym.b], dma="cys%d" % sb)
                for ti in range(SPn // TT):
                    tb = tc2[0] % 2
                    tc2[0] += 1
                    t0 = s0 + ti * TT
                    xt, yt = xr[tb], yo[tb]
                    S.op("sp", lambda e, xt=xt, t0=t0: e.dma_start(out=xt.t[0:TT, :], in_=xin[pk][t0:t0 + TT, :]), writes=[xt.b], dma="cxr%d" % tb)
                    for half in range(2):
                        op_ = o_ps[tb * 2 + half]
                        for c in range(8):
                            if c < 4:
                                lh, lb = ym.t[:, c, ti * TT:(ti + 1) * TT], ym.b
                            else:
                                lh, lb = yatt[pk].t[:, c - 4, t0:t0 + TT], yatt[pk].b
                            S.op("pe", lambda e, op_=op_, lh=lh, c=c, half=half: e.matmul(
                                op_.t[0:TT, :], lhsT=lh, rhs=wo[pk].t[:, c, half * 512:(half + 1) * 512], start=(c == 0), stop=(c == 7)),
                                reads=[lb, wo[pk].b], writes=[op_.b])
                        S.op("dve", lambda e, op_=op_, xt=xt, yt=yt, half=half: e.tensor_tensor(
                            out=yt.t[0:TT, half * 512:(half + 1) * 512], in0=op_.t[0:TT, :], in1=xt.t[0:TT, half * 512:(half + 1) * 512],
                            op=ALU.add), reads=[op_.b, xt.b], writes=[yt.b])
                    S.op("pool", lambda e, yt=yt, t0=t0: e.dma_start(out=O["y_" + pk][t0:t0 + TT, :], in_=yt.t[0:TT, :]),
                         reads=[yt.b], dma="cyo%d" % tb)
        S.barrier()
        st.close()

    if "0" in phases:
        phase0()
    stTab = contextlib.ExitStack()
    if "A" in phases:
        alloc_tables(stTab)
        phaseN()
        phaseA()
    if "B" in phases:
        phaseB()
    stTab.close()
    if "2" in phases:
        phase2C()
    S.barrier()
    S.emit(nc, root)
    root.close()
    return nc


def _common_inputs(inp):
    f = np.float32
    d = {}
    d["w_ada"] = np.ascontiguousarray(inp["w_ada"][0], f)
    d["w_in"] = np.ascontiguousarray(inp["w_in"][0], f)
    d["w_out"] = np.ascontiguousarray(inp["w_out"][0], f)
    d["w_glu"] = np.ascontiguousarray(inp["w_glu"][0], f)
    bc = np.zeros((128, 2048), f)
    bc[:, 0:1024] = inp["b_ada"][0][2048:3072][None, :]
    bc[:, 1024:1536] = np.tile(inp["q_norm_g"][0], 8)[None, :]
    bc[:, 1536:2048] = np.tile(inp["k_norm_g"][0], 8)[None, :]
    d["bc"] = bc
    rep3 = np.zeros((128, 3, 2048), f)
    rep3[:, 0, :] = inp["ssm_a_re"][0].reshape(-1)[None, :]
    rep3[:, 1, :] = inp["ssm_a_im"][0].reshape(-1)[None, :]
    rep3[:, 2, :] = np.repeat(inp["ssm_log_dt"][0], 64)[None, :]
    d["rep3"] = rep3
    BTl = np.zeros((128, 2, 16, 2, 64), f)
    CTl = np.zeros((128, 2, 16, 128), f)
    for pl, (bsrc, csrc) in enumerate(((inp["ssm_b_re"][0], inp["ssm_c_re"][0]), (inp["ssm_b_im"][0], inp["ssm_c_im"][0]))):
        for g in range(32):
            j, g2, gl = g // 2, g % 2, g % 8
            BTl[gl * 16:(gl + 1) * 16, pl, j, g2, :] = bsrc[g].T
            CTl[g2 * 64:(g2 + 1) * 64, pl, j, gl * 16:(gl + 1) * 16] = csrc[g].T
    d["BTlay"] = BTl.reshape(128, 2, 2048)
    d["CTlay"] = CTl.reshape(128, 2, 2048)
    cf = np.zeros((128, NCF), f)
    cf[:, CF_ID:CF_ID + 128] = np.eye(128, dtype=f)
    cf[:, CF_TRI:CF_TRI + 128] = np.triu(np.ones((128, 128), f))
    cf[127, CF_SEL128:CF_SEL128 + 128] = 1.0
    cf[63, CF_SEL64:CF_SEL64 + 128] = 1.0
    cf[:, CF_IOTA:CF_IOTA + 512] = np.arange(1, 513, dtype=f)[None, :]
    d["cf"] = cf
    cb = np.zeros((128, 256), f)
    cb[:, 0:128] = np.eye(128, dtype=f)
    cb[:, 128:256] = np.where(np.arange(128)[:, None] > np.arange(128)[None, :], -240000.0, 0.0)
    d["cb"] = cb.astype(ml_dtypes.bfloat16)
    return d


def _smalls(inp, b):
    f = np.float32
    sm = np.zeros((128, NS), f)

    def put(name, arr):
        a, e = SM[name]
        sm[:, a:e] = arr

    put("b_adaT", inp["b_ada"][0][0:2048].reshape(16, 128).T)
    put("norm_gT", inp["norm_g"][0].reshape(8, 128).T)
    put("bf_bc", np.broadcast_to(inp["b_f"][0][None, :], (128, 8)))
    put("dT", inp["ssm_d"][0].reshape(4, 128).T)
    put("b_gluT", inp["b_glu"][0].reshape(4, 128).T)
    put("logdtT", np.repeat(inp["ssm_log_dt"][0], 64).reshape(16, 128).T)
    put("a_reT", inp["ssm_a_re"][0].reshape(16, 128).T)
    put("a_imT", inp["ssm_a_im"][0].reshape(16, 128).T)
    c2 = np.stack([inp["c_prompt"][b].reshape(8, 128).T, inp["c_sample"][b].reshape(8, 128).T], axis=-1)
    put("c2T", c2.reshape(128, 16))
    put("stT_re", inp["state_ssm_re"][0, b].reshape(16, 128).T)
    put("stT_im", inp["state_ssm_im"][0, b].reshape(16, 128).T)
    return sm


_NC_CACHE = {}


def kernel(**inp):
    inp = {k: np.asarray(v) for k, v in inp.items()}
    if "nc" not in _NC_CACHE:
        _NC_CACHE["nc"] = build_program()
    nc = _NC_CACHE["nc"]
    common = _common_inputs(inp)
    in_maps = []
    f = np.float32
    for b in range(NCORES):
        m = dict(common)
        m["smalls"] = _smalls(inp, b)
        m["x_p"] = np.ascontiguousarray(inp["x_prompt"][b], f)
        m["x_s"] = np.ascontiguousarray(inp["x_sample"][b], f)
        m["ck"] = np.ascontiguousarray(inp["cache_k"][0, b].reshape(NPAST, 512), f)
        m["cv"] = np.ascontiguousarray(inp["cache_v"][0, b].reshape(NPAST, 512), f)
        m["clf"] = np.ascontiguousarray(inp["cache_logf"][0, b], f)
        in_maps.append(m)
    res = run_bass_kernel_spmd(nc, in_maps, core_ids=list(range(NCORES)))
    R = res.results

    def stack(name):
        return np.stack([np.asarray(R[b][name], f) for b in range(NCORES)], axis=0)

    def ssm(name, lo):
        a = stack(name)[:, :, lo:lo + 16]
        return np.ascontiguousarray(a.transpose(0, 2, 1).reshape(NCORES, 32, 64))[None]

    y_p = stack("y_p")
    y_s = stack("y_s")
    k_p = stack("k_p").reshape(1, NCORES, T_P, 8, 64)
    v_p = stack("v_p").reshape(1, NCORES, T_P, 8, 64)
    lf_p = stack("lf_p")[None]
    k_s = stack("k_s").reshape(1, NCORES, T_S, 8, 64)
    v_s = stack("v_s").reshape(1, NCORES, T_S, 8, 64)
    lf_s = stack("lf_s")[None]
    return (y_p, y_s, k_p, v_p, lf_p, ssm("ssm_p", 0), ssm("ssm_p", 16),
            k_s, v_s, lf_s, ssm("ssm_s", 0), ssm("ssm_s", 16))
```

```python
import contextlib
import math
import numpy as np
import ml_dtypes
import concourse.bass as bass
import concourse.mybir as mybir
from concourse.bass_utils import run_bass_kernel_spmd

F32 = mybir.dt.float32
BF = mybir.dt.bfloat16
ALU = mybir.AluOpType
AF = mybir.ActivationFunctionType
AX = mybir.AxisListType

SAME_ENG_SYNC = True
D = 1024
NCORES = 8
T_P, T_S, NPAST = 8192, 64, 2048
EPS = 1e-6
PATHS = {
    "p": dict(pi=0, T=T_P, TT=128, SP=512, npast=0),
    "s": dict(pi=1, T=T_S, TT=64, SP=64, npast=NPAST),
}
SM = {}
_o = 0
for _n, _w in (("b_adaT", 16), ("norm_gT", 8), ("bf_bc", 8), ("dT", 4), ("b_gluT", 4), ("logdtT", 16),
               ("a_reT", 16), ("a_imT", 16), ("c2T", 16), ("stT_re", 16), ("stT_im", 16)):
    SM[_n] = (_o, _o + _w)
    _o += _w
NS = _o
CF_ID, CF_TRI, CF_SEL128, CF_SEL64, CF_IOTA = 0, 128, 256, 384, 512
NCF = 1024


class Ev:
    __slots__ = ("kind", "key", "val")

    def __init__(s, kind, key, val):
        s.kind, s.key, s.val = kind, key, val


class Rec:
    __slots__ = ("call",)

    def __init__(s):
        s.call = None

    def __getattr__(s, name):
        def f(*a, **k):
            s.call = (name, a, k)
            return s
        return f


class Buf:
    __slots__ = ("w", "r")

    def __init__(s):
        s.w = None
        s.r = {}


class Sched:
    ENGS = ("pe", "act", "dve", "pool", "sp")

    def __init__(s):
        s.q = {e: [] for e in s.ENGS}
        s.waited = {e: {} for e in s.ENGS}
        s.sig = {e: set() for e in s.ENGS}
        s.dma_cnt = {}
        s.dma_last = {}
        s.capture = None

    def op(s, eng, fn, reads=(), writes=(), dma=None, after=()):
        rec = Rec()
        fn(rec)
        assert rec.call is not None
        if s.capture is not None:
            s.capture.append((eng, rec.call, tuple(reads), tuple(writes), dma, tuple(after)))
            return None
        return s._sched(eng, rec.call, reads, writes, dma, after)

    def begin_capture(s):
        s.capture = []

    def end_capture(s):
        items, s.capture = s.capture, None
        return items

    def replay(s, items):
        for it in items:
            s._sched(*it)

    def _sched(s, eng, call, reads=(), writes=(), dma=None, after=()):
        deps = list(after)
        for b in reads:
            if b.w is not None:
                deps.append(b.w)
        for b in writes:
            if b.w is not None:
                deps.append(b.w)
            deps.extend(b.r.values())
        waits = []
        wd = s.waited[eng]
        for ev in deps:
            if ev.kind == "E" and ev.key == eng and dma is None and (eng == "pe" or not SAME_ENG_SYNC):
                continue
            k = (ev.kind, ev.key)
            if wd.get(k, -1) >= ev.val:
                continue
            wd[k] = ev.val
            waits.append(ev)
            if ev.kind == "E":
                s.sig[ev.key].add(ev.val)
        idx = len(s.q[eng])
        if dma is not None:
            n = s.dma_cnt.get(dma, 0) + 1
            s.dma_cnt[dma] = n
            ev = Ev("D", dma, 16 * n)
            s.dma_last[dma] = ev
        else:
            ev = Ev("E", eng, idx)
        s.q[eng].append((call, waits, dma))
        k = (ev.kind, ev.key)
        for b in reads:
            b.r[k] = ev
        for b in writes:
            b.w = ev
            b.r = {}
        return ev

    def barrier(s, engs=None):
        evs = []
        for e in s.ENGS:
            if s.q[e]:
                for idx in range(len(s.q[e]) - 1, -1, -1):
                    fn, _, dma = s.q[e][idx]
                    if fn is not None and dma is None:
                        evs.append(Ev("E", e, idx))
                        break
        evs.extend(s.dma_last.values())
        for e in (engs or s.ENGS):
            waits = []
            wd = s.waited[e]
            for ev in evs:
                if ev.kind == "E" and ev.key == e:
                    continue
                k = (ev.kind, ev.key)
                if wd.get(k, -1) >= ev.val:
                    continue
                wd[k] = ev.val
                waits.append(ev)
                if ev.kind == "E":
                    s.sig[ev.key].add(ev.val)
            s.q[e].append((None, waits, None))

    def emit(s, nc, st):
        sems = {e: st.enter_context(nc.semaphore("c_" + e)) for e in s.ENGS}
        dsems = {k: st.enter_context(nc.semaphore("d_" + k)) for k in s.dma_cnt}
        rank = {}
        for e in s.ENGS:
            rank[e] = {idx: i + 1 for i, idx in enumerate(sorted(s.sig[e]))}

        def run(eo, e):
            rk = rank[e]
            for idx, (fn, waits, dma) in enumerate(s.q[e]):
                for ev in waits:
                    if ev.kind == "E":
                        eo.wait_ge(sems[ev.key], rank[ev.key][ev.val])
                    else:
                        eo.wait_ge(dsems[ev.key], ev.val)
                if fn is None:
                    continue
                ins = getattr(eo, fn[0])(*fn[1], **fn[2])
                if dma is not None:
                    ins.then_inc(dsems[dma], 16)
                elif idx in rk:
                    ins.then_inc(sems[e], 1)

        with nc.Block() as block:
            @block.tensor
            def _(eo):
                run(eo, "pe")

            @block.scalar
            def _(eo):
                run(eo, "act")

            @block.vector
            def _(eo):
                run(eo, "dve")

            @block.gpsimd
            def _(eo):
                run(eo, "pool")

            @block.sync
            def _(eo):
                run(eo, "sp")


class Tl:
    __slots__ = ("t", "b")

    def __init__(s, t):
        s.t = t
        s.b = Buf()


def build_program(phases="0AB2C"):
    nc = bass.Bass("TRN2", target_bir_lowering=False)
    S = Sched()
    root = contextlib.ExitStack()

    def din(name, shape, dt=F32):
        return nc.dram_tensor(name, list(shape), dt, kind="ExternalInput").ap()

    def dout(name, shape):
        return nc.dram_tensor(name, list(shape), F32, kind="ExternalOutput").ap()

    def dscr(name, shape, dt=BF):
        return nc.dram_tensor(name, list(shape), dt).ap()

    w_ada = din("w_ada", [D, 3 * D]).rearrange("(k p) n -> p k n", p=128)
    w_in = din("w_in", [D, 3080]).rearrange("(k p) n -> p k n", p=128)
    w_out = din("w_out", [D, D]).rearrange("(k p) n -> p k n", p=128)
    w_glu = din("w_glu", [512, 512]).rearrange("(k p) n -> p k n", p=128)
    bcin = din("bc", [128, 2048])
    rep3 = din("rep3", [128, 3, 2048])
    BTlay = din("BTlay", [128, 2, 2048])
    CTlay = din("CTlay", [128, 2, 2048])
    cfin = din("cf", [128, NCF])
    cbin = din("cb", [128, 256], BF)
    smin = din("smalls", [128, NS])
    xin = {"p": din("x_p", [T_P, D]), "s": din("x_s", [T_S, D])}
    ckin = din("ck", [NPAST, 512])
    cvin = din("cv", [NPAST, 512])
    clfin = din("clf", [NPAST, 8])
    O = {}
    for pk, P in PATHS.items():
        O["y_" + pk] = dout("y_" + pk, [P["T"], D])
        O["k_" + pk] = dout("k_" + pk, [P["T"], 512])
        O["v_" + pk] = dout("v_" + pk, [P["T"], 512])
        O["lf_" + pk] = dout("lf_" + pk, [P["T"], 8])
        O["ssm_" + pk] = dout("ssm_" + pk, [128, 32])
    SC = {}
    for pk, P in PATHS.items():
        nk = P["npast"] + P["T"]
        SC[pk] = dict(qTa=dscr("qTa_" + pk, [8, 65, P["T"]]), kTa=dscr("kTa_" + pk, [8, 65, nk]),
                      va=dscr("va_" + pk, [nk, 8, 128]), zaG=dscr("zaG_" + pk, [512, P["T"]]),
                      yssm=dscr("yssm_" + pk, [512, P["T"]]), hT=dscr("hTs_" + pk, [1024, P["T"]]))

    tcount = [0]

    def tile(st, name, shape, dt=F32):
        tcount[0] += 1
        return Tl(st.enter_context(nc.sbuf_tensor("sb%d_%s" % (tcount[0], name), list(shape), dt)))

    PS = [Tl(root.enter_context(nc.psum_tensor("ps%d" % i, [128, 512], F32))) for i in range(8)]

    sm = tile(root, "sm", [128, NS])
    cf = tile(root, "cf", [128, NCF])
    cb = tile(root, "cb", [128, 256], BF)
    qkg = tile(root, "qkg", [128, 1024])
    gate_bc = tile(root, "gate_bc", [128, 2, 1024])
    modb = tile(root, "modb", [128, 16, 2])
    gsT = tile(root, "gsT", [128, 8, 2])
    negb8 = tile(root, "negb8", [128, 8])
    theta = tile(root, "theta", [128, 16])
    rho = tile(root, "rho", [128, 16])
    BT = tile(root, "BT", [128, 2, 2048], BF)
    CT = tile(root, "CT", [128, 3, 2048], BF)
    biasK = {"p": tile(root, "biasK_p", [128, 64, 8]), "s": tile(root, "biasK_s", [128, 17, 8])}
    ones_bf = tile(root, "ones_bf", [128, 128], BF)
    ones_f = tile(root, "ones_f", [128, 128])
    epsc = tile(root, "epsc", [128, 1])

    def smv(name):
        a, b = SM[name]
        return sm.t[:, a:b]

    ident_bf = cb.t[:, 0:128]
    maskT = cb.t[:, 128:256]
    ident_f = cf.t[:, CF_ID:CF_ID + 128]
    tri_f = cf.t[:, CF_TRI:CF_TRI + 128]
    sel128 = cf.t[:, CF_SEL128:CF_SEL128 + 128]
    sel64 = cf.t[:, CF_SEL64:CF_SEL64 + 128]
    iota_t1 = cf.t[:, CF_IOTA:CF_IOTA + 512]

    dcount = [0]

    def dname(prefix):
        dcount[0] += 1
        return "%s%d" % (prefix, dcount[0])

    TWO_PI_HI = 6.28125
    TWO_PI_LO = 2.0 * math.pi - 6.28125

    def range_sin(dst, ang, shift, tf, ti, bd, ba, btf, bti):
        off = shift + 16.0 * math.pi
        S.op("dve", lambda e: e.tensor_scalar(out=tf, in0=ang, scalar1=off, scalar2=1.0 / (2.0 * math.pi), op0=ALU.add, op1=ALU.mult),
             reads=[ba], writes=[btf])
        S.op("dve", lambda e: e.tensor_copy(out=ti, in_=tf), reads=[btf], writes=[bti])
        S.op("dve", lambda e: e.tensor_copy(out=tf, in_=ti), reads=[bti], writes=[btf])
        S.op("dve", lambda e: e.scalar_tensor_tensor(out=dst, in0=tf, scalar=-TWO_PI_HI, in1=ang, op0=ALU.mult, op1=ALU.add),
             reads=[btf, ba], writes=[bd])
        S.op("dve", lambda e: e.scalar_tensor_tensor(out=dst, in0=tf, scalar=-TWO_PI_LO, in1=dst, op0=ALU.mult, op1=ALU.add),
             reads=[btf, bd], writes=[bd])
        S.op("dve", lambda e: e.tensor_scalar(out=dst, in0=dst, scalar1=off, scalar2=None, op0=ALU.add), reads=[bd], writes=[bd])
        S.op("dve", lambda e: e.tensor_scalar(out=tf, in0=dst, scalar1=math.pi, scalar2=-2.0 * math.pi, op0=ALU.is_gt, op1=ALU.mult),
             reads=[bd], writes=[btf])
        S.op("dve", lambda e: e.tensor_tensor(out=dst, in0=dst, in1=tf, op=ALU.add), reads=[bd, btf], writes=[bd])
        S.op("dve", lambda e: e.tensor_scalar(out=dst, in0=dst, scalar1=3.14159, scalar2=-3.14159, op0=ALU.min, op1=ALU.max),
             reads=[bd], writes=[bd])
        S.op("act", lambda e: e.activation(out=dst, in_=dst, func=AF.Sin), reads=[bd], writes=[bd])

    def phase0():
        st = contextlib.ExitStack()
        bc = tile(st, "bc", [128, 2048])
        g = "ld0"
        S.op("sp", lambda e: e.dma_start(out=sm.t[:], in_=smin), writes=[sm.b], dma=g)
        S.op("sp", lambda e: e.dma_start(out=cf.t[:], in_=cfin), writes=[cf.b], dma=g)
        S.op("sp", lambda e: e.dma_start(out=cb.t[:], in_=cbin), writes=[cb.b], dma=g)
        ev = S.op("sp", lambda e: e.dma_start(out=bc.t[:], in_=bcin), writes=[bc.b], dma=g)
        for tl in (sm, cf, cb, bc):
            tl.b.w = ev
        r3 = tile(st, "r3", [128, 3, 2048])
        S.op("sp", lambda e: e.dma_start(out=r3.t[:], in_=rep3), writes=[r3.b], dma="ld_r3")
        bl = tile(st, "bl", [128, 2, 2048])
        S.op("sp", lambda e: e.dma_start(out=bl.t[:], in_=BTlay), writes=[bl.b], dma="ld_bl")
        S.op("dve", lambda e: e.memset(ones_bf.t[:], 1.0), writes=[ones_bf.b])
        S.op("dve", lambda e: e.memset(ones_f.t[:], 1.0), writes=[ones_f.b])
        S.op("dve", lambda e: e.memset(epsc.t[:], EPS), writes=[epsc.b])
        S.op("dve", lambda e: e.tensor_copy(out=qkg.t[:], in_=bc.t[:, 1024:2048]), reads=[bc.b], writes=[qkg.b])
        sc = tile(st, "sc", [128, 8, 2])
        sg = tile(st, "sg", [128, 16])
        screp = tile(st, "screp", [128, 16, 128])
        c2 = smv("c2T")
        S.op("act", lambda e: e.activation(out=sg.t[:], in_=c2, func=AF.Sigmoid), reads=[sm.b], writes=[sg.b])
        scf = sc.t[:].rearrange("p k t -> p (k t)")
        S.op("dve", lambda e: e.tensor_tensor(out=scf, in0=c2, in1=sg.t[:], op=ALU.mult), reads=[sg.b, sm.b], writes=[sc.b])
        for i in range(16):
            S.op("dve", lambda e, i=i: e.tensor_scalar(out=screp.t[:, i, :], in0=ones_f.t[:], scalar1=scf[:, i:i + 1],
                                                     scalar2=None, op0=ALU.mult),
                 reads=[sc.b, ones_f.b], writes=[screp.b])
        wa = [tile(st, "wa%d" % i, [128, 8, 512]) for i in range(2)]
        modps = PS[0]
        for pc in range(6):
            w = wa[pc % 2]
            S.op("sp", lambda e, w=w, pc=pc: e.dma_start(out=w.t[:], in_=w_ada[:, :, pc * 512:(pc + 1) * 512]),
                 writes=[w.b], dma="wa%d" % (pc % 2))
            if pc < 4:
                for jj in range(4):
                    j = pc * 4 + jj
                    for k in range(8):
                        S.op("pe", lambda e, w=w, jj=jj, j=j, k=k: e.matmul(
                            modps.t[:, 2 * j:2 * j + 2], lhsT=w.t[:, k, jj * 128:(jj + 1) * 128], rhs=sc.t[:, k, :],
                            start=(k == 0), stop=(k == 7)), reads=[w.b, sc.b], writes=[modps.b])
            else:
                half = pc - 4
                for pi in range(2):
                    ps = PS[1 + pi * 2 + half]
                    for k in range(8):
                        S.op("pe", lambda e, w=w, ps=ps, k=k, pi=pi: e.matmul(
                            ps.t[:, :], lhsT=screp.t[:, k * 2 + pi, :], rhs=w.t[:, k, :],
                            start=(k == 0), stop=(k == 7)), reads=[w.b, screp.b], writes=[ps.b])
                    S.op("dve", lambda e, ps=ps, pi=pi, half=half: e.tensor_tensor(
                        out=gate_bc.t[:, pi, half * 512:(half + 1) * 512], in0=ps.t[:, :],
                        in1=bc.t[:, half * 512:(half + 1) * 512], op=ALU.add), reads=[ps.b, bc.b], writes=[gate_bc.b])
        S.op("dve", lambda e: e.tensor_tensor(
            out=modb.t[:], in0=modps.t[:, 0:32].rearrange("p (j t) -> p j t", t=2),
            in1=smv("b_adaT").unsqueeze(2).to_broadcast([128, 16, 2]), op=ALU.add),
            reads=[modps.b, sm.b], writes=[modb.b])
        S.op("dve", lambda e: e.scalar_tensor_tensor(
            out=gsT.t[:], in0=modb.t[:, 8:16, :], scalar=1.0,
            in1=smv("norm_gT").unsqueeze(2).to_broadcast([128, 8, 2]), op0=ALU.add, op1=ALU.mult),
            reads=[modb.b, sm.b], writes=[gsT.b])
        g2 = tile(st, "g2", [128, 128])
        m2 = tile(st, "m2", [128, 2])
        S.op("dve", lambda e: e.tensor_tensor(out=g2.t[:, 0:64], in0=qkg.t[:, 0:64], in1=qkg.t[:, 0:64], op=ALU.mult),
             reads=[qkg.b], writes=[g2.b])
        S.op("dve", lambda e: e.tensor_tensor(out=g2.t[:, 64:128], in0=qkg.t[:, 512:576], in1=qkg.t[:, 512:576], op=ALU.mult),
             reads=[qkg.b], writes=[g2.b])
        S.op("dve", lambda e: e.reduce_max(out=m2.t[:, 0:1], in_=g2.t[:, 0:64], axis=AX.X), reads=[g2.b], writes=[m2.b])
        S.op("dve", lambda e: e.reduce_max(out=m2.t[:, 1:2], in_=g2.t[:, 64:128], axis=AX.X), reads=[g2.b], writes=[m2.b])
        S.op("dve", lambda e: e.tensor_tensor(out=m2.t[:, 0:1], in0=m2.t[:, 0:1], in1=m2.t[:, 1:2], op=ALU.mult),
             reads=[m2.b], writes=[m2.b])
        S.op("act", lambda e: e.activation(out=m2.t[:, 0:1], in_=m2.t[:, 0:1], func=AF.Ln), reads=[m2.b], writes=[m2.b])
        S.op("act", lambda e: e.activation(out=m2.t[:, 0:1], in_=m2.t[:, 0:1], func=AF.Exp, scale=0.5), reads=[m2.b], writes=[m2.b])
        S.op("dve", lambda e: e.tensor_scalar(out=m2.t[:, 0:1], in0=m2.t[:, 0:1], scalar1=-8.0, scalar2=None, op0=ALU.mult),
             reads=[m2.b], writes=[m2.b])
        S.op("dve", lambda e: e.tensor_copy(out=negb8.t[:], in_=m2.t[:, 0:1].to_broadcast([128, 8])),
             reads=[m2.b], writes=[negb8.b])
        dtT = tile(st, "dtT", [128, 16])
        S.op("act", lambda e: e.activation(out=dtT.t[:], in_=smv("logdtT"), func=AF.Exp), reads=[sm.b], writes=[dtT.b])
        S.op("dve", lambda e: e.tensor_tensor(out=theta.t[:], in0=smv("a_imT"), in1=dtT.t[:], op=ALU.mult),
             reads=[sm.b, dtT.b], writes=[theta.b])
        S.op("dve", lambda e: e.tensor_tensor(out=rho.t[:], in0=smv("a_reT"), in1=dtT.t[:], op=ALU.mult),
             reads=[sm.b, dtT.b], writes=[rho.b])
        S.op("act", lambda e: e.activation(out=rho.t[:], in_=rho.t[:], func=AF.Exp), reads=[rho.b], writes=[rho.b])
        tt = [tile(st, "tt%d" % i, [128, 2048]) for i in range(7)]
        ar, ai, ldt = r3.t[:, 0, :], r3.t[:, 1, :], r3.t[:, 2, :]
        dtr, mag, ang, c1, s1, den, t6 = tt
        TWO_PI = 2.0 * math.pi
        negpi = tile(st, "negpi", [128, 1])
        S.op("dve", lambda e: e.memset(negpi.t[:], -math.pi), writes=[negpi.b])
        S.op("act", lambda e: e.activation(out=dtr.t[:], in_=ldt, func=AF.Exp), reads=[r3.b], writes=[dtr.b])
        S.op("dve", lambda e: e.tensor_tensor(out=mag.t[:], in0=ar, in1=dtr.t[:], op=ALU.mult), reads=[r3.b, dtr.b], writes=[mag.b])
        S.op("act", lambda e: e.activation(out=mag.t[:], in_=mag.t[:], func=AF.Exp), reads=[mag.b], writes=[mag.b])
        S.op("dve", lambda e: e.tensor_tensor(out=ang.t[:], in0=ai, in1=dtr.t[:], op=ALU.mult), reads=[r3.b, dtr.b], writes=[ang.b])
        tiI = tile(st, "tiI", [128, 2048], mybir.dt.int32)
        range_sin(s1.t[:], ang.t[:], 0.0, den.t[:], tiI.t[:], s1.b, ang.b, den.b, tiI.b)
        range_sin(c1.t[:], ang.t[:], 0.5 * math.pi, den.t[:], tiI.t[:], c1.b, ang.b, den.b, tiI.b)
        S.op("dve", lambda e: e.tensor_tensor(out=c1.t[:], in0=c1.t[:], in1=mag.t[:], op=ALU.mult), reads=[c1.b, mag.b], writes=[c1.b])
        S.op("dve", lambda e: e.tensor_scalar(out=c1.t[:], in0=c1.t[:], scalar1=-1.0, scalar2=None, op0=ALU.add), reads=[c1.b], writes=[c1.b])
        S.op("dve", lambda e: e.tensor_tensor(out=s1.t[:], in0=s1.t[:], in1=mag.t[:], op=ALU.mult), reads=[s1.b, mag.b], writes=[s1.b])
        S.op("dve", lambda e: e.tensor_tensor(out=den.t[:], in0=ar, in1=ar, op=ALU.mult), reads=[r3.b], writes=[den.b])
        S.op("dve", lambda e: e.tensor_tensor(out=t6.t[:], in0=ai, in1=ai, op=ALU.mult), reads=[r3.b], writes=[t6.b])
        S.op("dve", lambda e: e.tensor_tensor(out=den.t[:], in0=den.t[:], in1=t6.t[:], op=ALU.add), reads=[den.b, t6.b], writes=[den.b])
        S.op("dve", lambda e: e.reciprocal(out=den.t[:], in_=den.t[:]), reads=[den.b], writes=[den.b])
        S.op("dve", lambda e: e.tensor_tensor(out=mag.t[:], in0=c1.t[:], in1=ar, op=ALU.mult), reads=[c1.b, r3.b], writes=[mag.b])
        S.op("dve", lambda e: e.tensor_tensor(out=t6.t[:], in0=s1.t[:], in1=ai, op=ALU.mult), reads=[s1.b, r3.b], writes=[t6.b])
        S.op("dve", lambda e: e.tensor_tensor(out=mag.t[:], in0=mag.t[:], in1=t6.t[:], op=ALU.add), reads=[mag.b, t6.b], writes=[mag.b])
        S.op("dve", lambda e: e.tensor_tensor(out=mag.t[:], in0=mag.t[:], in1=den.t[:], op=ALU.mult), reads=[mag.b, den.b], writes=[mag.b])
        S.op("dve", lambda e: e.tensor_tensor(out=ang.t[:], in0=s1.t[:], in1=ar, op=ALU.mult), reads=[s1.b, r3.b], writes=[ang.b])
        S.op("dve", lambda e: e.tensor_tensor(out=t6.t[:], in0=c1.t[:], in1=ai, op=ALU.mult), reads=[c1.b, r3.b], writes=[t6.b])
        S.op("dve", lambda e: e.tensor_tensor(out=ang.t[:], in0=ang.t[:], in1=t6.t[:], op=ALU.subtract), reads=[ang.b, t6.b], writes=[ang.b])
        S.op("dve", lambda e: e.tensor_tensor(out=ang.t[:], in0=ang.t[:], in1=den.t[:], op=ALU.mult), reads=[ang.b, den.b], writes=[ang.b])
        qre, qim = mag, ang
        S.op("dve", lambda e: e.tensor_tensor(out=c1.t[:], in0=qre.t[:], in1=bl.t[:, 0, :], op=ALU.mult), reads=[qre.b, bl.b], writes=[c1.b])
        S.op("dve", lambda e: e.tensor_tensor(out=s1.t[:], in0=qim.t[:], in1=bl.t[:, 1, :], op=ALU.mult), reads=[qim.b, bl.b], writes=[s1.b])
        S.op("dve", lambda e: e.tensor_tensor(out=BT.t[:, 0, :], in0=c1.t[:], in1=s1.t[:], op=ALU.subtract), reads=[c1.b, s1.b], writes=[BT.b])
        S.op("dve", lambda e: e.tensor_tensor(out=c1.t[:], in0=qre.t[:], in1=bl.t[:, 1, :], op=ALU.mult), reads=[qre.b, bl.b], writes=[c1.b])
        S.op("dve", lambda e: e.tensor_tensor(out=s1.t[:], in0=qim.t[:], in1=bl.t[:, 0, :], op=ALU.mult), reads=[qim.b, bl.b], writes=[s1.b])
        S.op("dve", lambda e: e.tensor_tensor(out=BT.t[:, 1, :], in0=c1.t[:], in1=s1.t[:], op=ALU.add), reads=[c1.b, s1.b], writes=[BT.b])
        S.op("sp", lambda e: e.dma_start(out=bl.t[:], in_=CTlay), writes=[bl.b], dma="ld_bl")
        S.op("act", lambda e: e.copy(out=CT.t[:, 0, :], in_=bl.t[:, 0, :]), reads=[bl.b], writes=[CT.b])
        S.op("act", lambda e: e.mul(out=CT.t[:, 1, :], in_=bl.t[:, 1, :], mul=-1.0), reads=[bl.b], writes=[CT.b])
        S.op("act", lambda e: e.mul(out=CT.t[:, 2, :], in_=bl.t[:, 0, :], mul=-1.0), reads=[bl.b], writes=[CT.b])
        S.barrier()
        st.close()

    TAB = {}

    def alloc_tables(stk):
        TAB["cosT"] = tile(stk, "cosT", [128, 16, 512], BF)
        TAB["sinT"] = tile(stk, "sinT", [128, 16, 512], BF)
        TAB["lastc"] = {"p": tile(stk, "lastc_p", [128, 3, 16]), "s": tile(stk, "lastc_s", [128, 3, 16])}

    def table_ops(st3):
        cosT, sinT, lastc = TAB["cosT"], TAB["sinT"], TAB["lastc"]
        angT = tile(st3, "angT", [128, 8, 512])
        tfT = tile(st3, "tfT", [128, 8, 512])
        tiT = tile(st3, "tiT", [128, 8, 512], mybir.dt.int32)
        d32 = tile(st3, "d32", [128, 8, 512])
        fl = lambda a_: a_.rearrange("p j t -> p (j t)")
        S.begin_capture()
        for hf in range(2):
            for jj in range(8):
                j = hf * 8 + jj
                S.op("dve", lambda e, j=j, jj=jj: e.tensor_scalar(out=angT.t[:, jj, :], in0=iota_t1, scalar1=theta.t[:, j:j + 1], scalar2=None,
                                                              op0=ALU.mult), reads=[cf.b, theta.b], writes=[angT.b])
            for which, tab, shift in ((1, sinT, 0.0), (0, cosT, 0.5 * math.pi)):
                range_sin(fl(d32.t[:]), fl(angT.t[:]), shift, fl(tfT.t[:]), fl(tiT.t[:]), d32.b, angT.b, tfT.b, tiT.b)
                S.op("act", lambda e, tab=tab, hf=hf: e.copy(out=tab.t[:, hf * 8:(hf + 1) * 8, :], in_=d32.t[:]), reads=[d32.b], writes=[tab.b])
                for pk, Lc in (("p", 512), ("s", 64)):
                    S.op("dve", lambda e, pk=pk, Lc=Lc, which=which, hf=hf: e.tensor_copy(
                        out=lastc[pk].t[:, which, hf * 8:(hf + 1) * 8], in_=d32.t[:, :, Lc - 1]), reads=[d32.b], writes=[lastc[pk].b])
        for pk in ("p", "s"):
            S.op("dve", lambda e, pk=pk: e.tensor_scalar(out=lastc[pk].t[:, 2, :], in0=lastc[pk].t[:, 1, :], scalar1=-1.0, scalar2=None,
                                                       op0=ALU.mult), reads=[lastc[pk].b], writes=[lastc[pk].b])
        return S.end_capture()

    def phaseN():
        st = contextlib.ExitStack()
        NX = 4
        xt = [tile(st, "nx%d" % i, [128, D]) for i in range(NX)]
        sqj = tile(st, "nsq", [128, D], BF)
        sst = [tile(st, "nss%d" % i, [128, 2]) for i in range(NX)]
        xst = [tile(st, "nxs%d" % i, [128, D], BF) for i in range(NX)]
        hTn = [tile(st, "n_hT%d" % i, [128, 8, 512], BF) for i in range(2)]
        tps = [PS[6], PS[7]]
        tiles = []
        for pk in ("s", "p"):
            P = PATHS[pk]
            for s0 in range(0, P["T"], P["SP"]):
                for ti in range(P["SP"] // P["TT"]):
                    tiles.append((pk, s0, ti))
        spans = {}
        for (pk, s0, ti) in tiles:
            spans.setdefault((pk, s0), len(spans))

        def load(i):
            if i >= len(tiles):
                return
            pk, s0, ti = tiles[i]
            TT = PATHS[pk]["TT"]
            t0 = s0 + ti * TT
            x = xt[i % NX]
            S.op("sp", lambda e: e.dma_start(out=x.t[0:TT, :], in_=xin[pk][t0:t0 + TT, :]), writes=[x.b], dma="nx%d" % (i % NX))

        def chain(i):
            if i >= len(tiles):
                return
            pk, s0, ti = tiles[i]
            TT = PATHS[pk]["TT"]
            x, xs, ss = xt[i % NX], xst[i % NX], sst[i % NX]
            S.op("act", lambda e: e.activation(out=sqj.t[0:TT, :], in_=x.t[0:TT, :], func=AF.Square, accum_out=ss.t[0:TT, 0:1]),
                 reads=[x.b], writes=[sqj.b, ss.b])
            S.op("act", lambda e: e.activation(out=ss.t[0:TT, 1:2], in_=ss.t[0:TT, 0:1], func=AF.Ln, scale=1.0 / D, bias=epsc.t[0:TT, 0:1]),
                 reads=[ss.b, epsc.b], writes=[ss.b])
            S.op("act", lambda e: e.activation(out=ss.t[0:TT, 1:2], in_=ss.t[0:TT, 1:2], func=AF.Exp, scale=-0.5), reads=[ss.b], writes=[ss.b])
            S.op("dve", lambda e: e.tensor_scalar(out=xs.t[0:TT, :], in0=x.t[0:TT, :], scalar1=ss.t[0:TT, 1:2], scalar2=None, op0=ALU.mult),
                 reads=[x.b, ss.b], writes=[xs.b])

        def transp(i):
            pk, s0, ti = tiles[i]
            P = PATHS[pk]
            TT, pi, SPn = P["TT"], P["pi"], P["SP"]
            xs = xst[i % NX]
            tp = tps[i % 2]
            tpb = tp.t[:, :].bitcast(BF)
            hT = hTn[spans[(pk, s0)] % 2]
            for k in range(8):
                S.op("pe", lambda e, k=k: e.transpose(out=tpb[:, k * 128:k * 128 + TT], in_=xs.t[0:TT, k * 128:(k + 1) * 128],
                                                      identity=ident_bf[0:TT, 0:TT]), reads=[xs.b, cb.b], writes=[tp.b])
            for k in range(8):
                if k % 2 == 0:
                    S.op("act", lambda e, k=k: e.activation(
                        out=hT.t[:, k, ti * TT:(ti + 1) * TT], in_=tpb[:, k * 128:k * 128 + TT], func=AF.Identity,
                        scale=gsT.t[:, k, pi:pi + 1], bias=modb.t[:, k, pi:pi + 1]), reads=[tp.b, gsT.b, modb.b], writes=[hT.b])
                else:
                    S.op("dve", lambda e, k=k: e.tensor_scalar(
                        out=hT.t[:, k, ti * TT:(ti + 1) * TT], in0=tpb[:, k * 128:k * 128 + TT],
                        scalar1=gsT.t[:, k, pi:pi + 1], scalar2=modb.t[:, k, pi:pi + 1], op0=ALU.mult, op1=ALU.add),
                        reads=[tp.b, gsT.b, modb.b], writes=[hT.b])
            if ti == SPn // TT - 1:
                S.op("pool", lambda e: e.dma_start(out=SC[pk]["hT"][:, s0:s0 + SPn].rearrange("(k p) t -> p k t", p=128),
                                                   in_=hT.t[:, :, 0:SPn]), reads=[hT.b], dma="hTsp%d" % (spans[(pk, s0)] % 2))

        tab_l = table_ops(st) if TAB else []
        per = (len(tab_l) + len(tiles) - 1) // len(tiles)
        for i in range(NX - 1):
            load(i)
        chain(0)
        for i in range(len(tiles)):
            load(i + NX - 1)
            chain(i + 1)
            transp(i)
            S.replay(tab_l[i * per:(i + 1) * per])
        S.replay(tab_l[len(tiles) * per:])
        S.barrier()
        st.close()

    def load_w_bf(wt, cols0, ncols, stage, wname):
        c = 0
        i = 0
        while c < ncols:
            n = min(512, ncols - c)
            sg_ = stage[i % 2]
            S.op("sp", lambda e, sg_=sg_, c=c, n=n: e.dma_start(out=sg_.t[:, :, 0:n], in_=w_in[:, :, cols0 + c:cols0 + c + n]),
                 writes=[sg_.b], dma="%s%d" % (wname, i % 2))
            eng = "act" if i % 2 == 0 else "dve"
            if eng == "act":
                S.op("act", lambda e, sg_=sg_, c=c, n=n: e.copy(out=wt.t[:, :, c:c + n], in_=sg_.t[:, :, 0:n]),
                     reads=[sg_.b], writes=[wt.b])
            else:
                S.op("dve", lambda e, sg_=sg_, c=c, n=n: e.tensor_copy(out=wt.t[:, :, c:c + n], in_=sg_.t[:, :, 0:n]),
                     reads=[sg_.b], writes=[wt.b])
            c += n
            i += 1
        return wt

    def phaseA():
        st = contextlib.ExitStack()
        wA = tile(st, "wA", [128, 8, 2056], BF)
        st2 = contextlib.ExitStack()
        stage = [tile(st2, "wstg%d" % i, [128, 8, 512]) for i in range(2)]
        load_w_bf(wA, 1024, 2056, stage, "wsA")
        S.barrier()
        st2.close()
        hTa = [tile(st, "a_hT%d" % i, [128, 8, 512], BF) for i in range(2)]
        NSET = 2
        WS = []
        for i in range(NSET):
            d = {}
            for nm in ("qf", "kf", "sq", "tmpn", "ko", "vo"):
                d[nm] = tile(st, "a_%s%d" % (nm, i), [128, 512])
            d["st8"] = tile(st, "a_st8%d" % i, [128, 4, 8])
            d["lft"] = tile(st, "a_lft%d" % i, [128, 8])
            d["lf"] = tile(st, "a_lf%d" % i, [128, 8])
            d["cum"] = tile(st, "a_cum%d" % i, [128, 8])
            d["q_aug"] = tile(st, "a_qaug%d" % i, [128, 8, 65], BF)
            d["k_aug"] = tile(st, "a_kaug%d" % i, [128, 8, 65], BF)
            d["v_aug"] = tile(st, "a_vaug%d" % i, [128, 8, 128], BF)
            d["i"] = i
            WS.append(d)
            S.op("dve", lambda e, d=d: e.memset(d["k_aug"].t[:, :, 64:65], 1.0), writes=[d["k_aug"].b])
            va = d["v_aug"]
            S.op("dve", lambda e, va=va: e.memset(va.t[:], 0.0), writes=[va.b])
            S.op("dve", lambda e, va=va: e.memset(va.t[:, 0::2, 64:65], 1.0), writes=[va.b])
            S.op("dve", lambda e, va=va: e.memset(va.t[:, 1::2, 0:1], 1.0), writes=[va.b])
        qst = [tile(st, "a_qst%d" % i, [65, 8, 512], BF) for i in range(2)]
        kst = [tile(st, "a_kst%d" % i, [65, 8, 512], BF) for i in range(2)]
        zsg = tile(st, "a_zsg", [128, 512])
        zaS = [tile(st, "a_zaS%d" % i, [128, 4, 512], BF) for i in range(2)]
        qps, kps, vps, fps, qTps, kTps = PS[0], PS[1], PS[2], PS[3], PS[4], PS[5]
        qTb = qTps.t[:, :].bitcast(BF)
        kTb = kTps.t[:, :].bitcast(BF)
        cnt = [0]
        state = dict(cum_prev=None, cum_TT=None)

        def h3(ap_):
            return ap_.rearrange("p (h d) -> p h d", d=64)

        def evac_tile(TT, W):
            S.op("act", lambda e: e.copy(out=W["qf"].t[0:TT, :], in_=qps.t[0:TT, :]), reads=[qps.b], writes=[W["qf"].b])
            S.op("act", lambda e: e.copy(out=W["kf"].t[0:TT, :], in_=kps.t[0:TT, :]), reads=[kps.b], writes=[W["kf"].b])
            S.op("act", lambda e: e.copy(out=W["vo"].t[0:TT, :], in_=vps.t[0:TT, :]), reads=[vps.b], writes=[W["vo"].b])
            S.op("dve", lambda e: e.tensor_tensor(out=W["lft"].t[0:TT, :], in0=fps.t[0:TT, 0:8], in1=smv("bf_bc")[0:TT, :], op=ALU.add),
                 reads=[fps.b, sm.b], writes=[W["lft"].b])

        def kv_tile(pk, blk, TT, tok0, col0, from_cache, kstg, qstg, W, part="abc"):
            P = PATHS[pk]
            c = W["i"]
            ko, vo, lf_, cm, va = W["ko"], W["vo"], W["lf"], W["cum"], W["v_aug"]
            q_aug, k_aug, sq, st8, tmpn, lft = W["q_aug"], W["k_aug"], W["sq"], W["st8"], W["tmpn"], W["lft"]
            if "a" not in part:
                pass
            elif from_cache:
                ck_, cv_ = W["kf"], W["vo"]
                S.op("sp", lambda e: e.dma_start(out=ck_.t[0:TT, :], in_=ckin[tok0:tok0 + TT, :]), writes=[ck_.b], dma="ck%d" % c)
                S.op("sp", lambda e: e.dma_start(out=cv_.t[0:TT, :], in_=cvin[tok0:tok0 + TT, :]), writes=[cv_.b], dma="cv%d" % c)
                S.op("sp", lambda e: e.dma_start(out=lf_.t[0:TT, :], in_=clfin[tok0:tok0 + TT, :]), writes=[lf_.b], dma="clf%d" % c)
                S.op("act", lambda e: e.copy(out=k_aug.t[0:TT, :, 0:64], in_=h3(ck_.t[0:TT, :])), reads=[ck_.b], writes=[k_aug.b])
            else:
                nq = tok0 - P["npast"]
                for which, src in ((0, W["qf"]), (1, W["kf"])):
                    S.op("act", lambda e, src=src: e.activation(out=sq.t[0:TT, :], in_=src.t[0:TT, :], func=AF.Square),
                         reads=[src.b], writes=[sq.b])
                    S.op("dve", lambda e, which=which: e.reduce_sum(out=st8.t[0:TT, 2 * which, :], in_=h3(sq.t[0:TT, :]), axis=AX.X),
                         reads=[sq.b], writes=[st8.b])
                    S.op("act", lambda e, which=which: e.activation(
                        out=st8.t[0:TT, 2 * which + 1, :], in_=st8.t[0:TT, 2 * which, :], func=AF.Ln, scale=1.0 / 64, bias=epsc.t[0:TT, 0:1]),
                        reads=[st8.b, epsc.b], writes=[st8.b])
                    S.op("act", lambda e, which=which: e.activation(
                        out=st8.t[0:TT, 2 * which + 1, :], in_=st8.t[0:TT, 2 * which + 1, :], func=AF.Exp, scale=-0.5),
                        reads=[st8.b], writes=[st8.b])
                    S.op("dve", lambda e, which=which, src=src: e.tensor_tensor(
                        out=h3(tmpn.t[0:TT, :]), in0=h3(src.t[0:TT, :]),
                        in1=st8.t[0:TT, 2 * which + 1, :].unsqueeze(2).to_broadcast([TT, 8, 64]), op=ALU.mult),
                        reads=[src.b, st8.b], writes=[tmpn.b])
                    if which == 0:
                        S.op("dve", lambda e: e.tensor_tensor(out=q_aug.t[0:TT, :, 0:64], in0=h3(tmpn.t[0:TT, :]),
                                                              in1=h3(qkg.t[0:TT, 0:512]), op=ALU.mult),
                             reads=[tmpn.b, qkg.b], writes=[q_aug.b])
                    else:
                        S.op("dve", lambda e: e.tensor_tensor(out=ko.t[0:TT, :], in0=tmpn.t[0:TT, :], in1=qkg.t[0:TT, 512:1024],
                                                              op=ALU.mult), reads=[tmpn.b, qkg.b], writes=[ko.b])
                        S.op("sp", lambda e: e.dma_start(out=O["k_" + pk][nq:nq + TT, :], in_=ko.t[0:TT, :]),
                             reads=[ko.b], dma="ko%d" % c)
                        S.op("act", lambda e: e.copy(out=k_aug.t[0:TT, :, 0:64], in_=h3(ko.t[0:TT, :])),
                             reads=[ko.b], writes=[k_aug.b])
                S.op("sp", lambda e: e.dma_start(out=O["v_" + pk][nq:nq + TT, :], in_=vo.t[0:TT, :]),
                     reads=[vo.b], dma="vo%d" % c)
                S.op("act", lambda e: e.activation(out=lft.t[0:TT, :], in_=lft.t[0:TT, :], func=AF.Exp, scale=-1.0),
                     reads=[lft.b], writes=[lft.b])
                S.op("act", lambda e: e.activation(out=lft.t[0:TT, :], in_=lft.t[0:TT, :], func=AF.Ln, bias=ones_f.t[0:TT, 0:1]),
                     reads=[lft.b, ones_f.b], writes=[lft.b])
                S.op("dve", lambda e: e.tensor_scalar(out=lf_.t[0:TT, :], in0=lft.t[0:TT, :], scalar1=-1.0, scalar2=None, op0=ALU.mult),
                     reads=[lft.b], writes=[lf_.b])
                S.op("sp", lambda e: e.dma_start(out=O["lf_" + pk][nq:nq + TT, :], in_=lf_.t[0:TT, :]),
                     reads=[lf_.b], dma="lfo%d" % c)
            v4 = h3(vo.t[0:TT, :])
            if "a" in part:
                S.op("dve", lambda e: e.tensor_copy(out=va.t[0:TT, 0::2, 0:64], in_=v4[:, 0::2, :]), reads=[vo.b], writes=[va.b])
                S.op("dve", lambda e: e.tensor_copy(out=va.t[0:TT, 1::2, 64:128], in_=v4[:, 1::2, :]), reads=[vo.b], writes=[va.b])
                S.op("pool", lambda e: e.dma_start(out=SC[pk]["va"][tok0:tok0 + TT, :, :], in_=va.t[0:TT, :, :]),
                     reads=[va.b], dma="vao%d" % c)
            if "b" in part:
                kv_tile_b(pk, blk, TT, from_cache, W)
            if "c" in part:
                kv_tile_c(TT, col0, from_cache, kstg, qstg, W)

        def kv_tile_b(pk, blk, TT, from_cache, W):
            lf_, cm, q_aug = W["lf"], W["cum"], W["q_aug"]
            cps = fps.t[0:TT, 16:24]
            prev = state["cum_prev"]
            S.op("pe", lambda e: e.matmul(cps, lhsT=tri_f[0:TT, 0:TT], rhs=lf_.t[0:TT, :], start=True, stop=(prev is None)),
                 reads=[lf_.b, cf.b], writes=[fps.b])
            if prev is not None:
                pTT = state["cum_TT"]
                sel = sel128 if pTT == 128 else sel64
                S.op("pe", lambda e: e.matmul(cps, lhsT=sel[0:pTT, 0:TT], rhs=prev.t[0:pTT, :], start=False, stop=True),
                     reads=[prev.b, cf.b], writes=[fps.b])
            S.op("act", lambda e: e.copy(out=cm.t[0:TT, :], in_=cps), reads=[fps.b], writes=[cm.b])
            state["cum_prev"], state["cum_TT"] = cm, TT
            S.op("dve", lambda e: e.scalar_tensor_tensor(out=biasK[pk].t[0:TT, blk, :], in0=cm.t[0:TT, :], scalar=-1.0,
                                                         in1=negb8.t[0:TT, :], op0=ALU.mult, op1=ALU.add),
                 reads=[cm.b, negb8.b], writes=[biasK[pk].b])
            if not from_cache:
                S.op("dve", lambda e: e.tensor_scalar(out=q_aug.t[0:TT, :, 64:65], in0=cm.t[0:TT, :].unsqueeze(2), scalar1=8.0,
                                                      scalar2=None, op0=ALU.mult), reads=[cm.b], writes=[q_aug.b])

        def kv_tile_c(TT, col0, from_cache, kstg, qstg, W):
            q_aug, k_aug = W["q_aug"], W["k_aug"]
            for h in range(8):
                S.op("pe", lambda e, h=h: e.transpose(out=kTb[0:65, h * 128:h * 128 + TT], in_=k_aug.t[0:TT, h, :],
                                                      identity=ident_bf[0:TT, 0:TT]), reads=[k_aug.b, cb.b], writes=[kTps.b])
            S.op("act", lambda e: e.copy(out=kstg.t[0:65, :, col0:col0 + TT],
                                         in_=kTb[0:65, :].rearrange("p (h t) -> p h t", t=128)[:, :, 0:TT]),
                 reads=[kTps.b], writes=[kstg.b])
            if not from_cache:
                for h in range(8):
                    S.op("pe", lambda e, h=h: e.transpose(out=qTb[0:65, h * 128:h * 128 + TT], in_=q_aug.t[0:TT, h, :],
                                                          identity=ident_bf[0:TT, 0:TT]), reads=[q_aug.b, cb.b], writes=[qTps.b])
                S.op("dve", lambda e: e.tensor_copy(out=qstg.t[0:65, :, col0:col0 + TT],
                                                    in_=qTb[0:65, :].rearrange("p (h t) -> p h t", t=128)[:, :, 0:TT]),
                     reads=[qTps.b], writes=[qstg.b])

        def load_hT(pk_, s0_, hT_):
            n_ = PATHS[pk_]["SP"]
            S.op("sp", lambda e: e.dma_start(out=hT_.t[:, :, 0:n_], in_=SC[pk_]["hT"][:, s0_:s0_ + n_].rearrange("(k p) t -> p k t", p=128)),
                 writes=[hT_.b], dma="ahT%d" % (0 if hT_ is hTa[0] else 1))

        spc = [0]
        LAG = 2
        pending = []
        pend_b = []

        def flush(n_keep):
            while len(pending) > n_keep:
                S.replay(pending.pop(0))

        for pk in ("s", "p"):
            P = PATHS[pk]
            TT, SPn, T, npast = P["TT"], P["SP"], P["T"], P["npast"]
            state["cum_prev"] = None
            for s0 in range(0, npast, 512):
                sc_ = spc[0]
                spc[0] += 1
                kstg = kst[sc_ % 2]
                for ti in range(4):
                    W = WS[cnt[0] % NSET]
                    cnt[0] += 1
                    kv_tile(pk, (s0 + ti * 128) // 128, 128, s0 + ti * 128, ti * 128, True, kstg, None, W)
                S.op("pool", lambda e, kstg=kstg, s0=s0: e.dma_start(
                    out=SC[pk]["kTa"][:, :, s0:s0 + 512].rearrange("h r t -> r h t"), in_=kstg.t[0:65, :, 0:512]),
                    reads=[kstg.b], dma="kst%d" % (sc_ % 2))
            span_list = list(range(0, T, SPn))
            load_hT(pk, span_list[0], hTa[spc[0] % 2])
            for sidx, s0 in enumerate(span_list):
                sc_ = spc[0]
                spc[0] += 1
                kstg, qstg, zas = kst[sc_ % 2], qst[sc_ % 2], zaS[sc_ % 2]
                hT = hTa[sc_ % 2]
                if sidx + 1 < len(span_list):
                    load_hT(pk, span_list[sidx + 1], hTa[(sc_ + 1) % 2])
                ntile = SPn // TT
                for ti in range(ntile):
                    cols = slice(ti * TT, (ti + 1) * TT)
                    W = WS[cnt[0] % NSET]
                    cnt[0] += 1
                    for (ps, c0, n) in ((qps, 0, 512), (kps, 512, 512), (vps, 1024, 512), (fps, 2048, 8)):
                        for k in range(8):
                            S.op("pe", lambda e, ps=ps, c0=c0, n=n, k=k, cols=cols: e.matmul(
                                ps.t[0:TT, 0:n], lhsT=hT.t[:, k, cols], rhs=wA.t[:, k, c0:c0 + n],
                                start=(k == 0), stop=(k == 7)), reads=[hT.b, wA.b], writes=[ps.b])
                    evac_tile(TT, W)
                    tok0 = npast + s0 + ti * TT
                    if pend_b:
                        S.replay(pend_b.pop(0))
                    kv_tile(pk, tok0 // 128, TT, tok0, ti * TT, False, kstg, qstg, W, part="a")
                    flush(0)
                    S.begin_capture()
                    kv_tile(pk, tok0 // 128, TT, tok0, ti * TT, False, kstg, qstg, W, part="b")
                    pend_b.append(S.end_capture())
                    S.begin_capture()
                    kv_tile(pk, tok0 // 128, TT, tok0, ti * TT, False, kstg, qstg, W, part="c")
                    lst = S.end_capture()
                    if ti == ntile - 1:
                        S.begin_capture()
                        k0 = npast + s0
                        S.op("pool", lambda e, kstg=kstg, k0=k0: e.dma_start(
                            out=SC[pk]["kTa"][:, :, k0:k0 + SPn].rearrange("h r t -> r h t"), in_=kstg.t[0:65, :, 0:SPn]),
                            reads=[kstg.b], dma="kst%d" % (sc_ % 2))
                        S.op("pool", lambda e, qstg=qstg, s0=s0: e.dma_start(
                            out=SC[pk]["qTa"][:, :, s0:s0 + SPn].rearrange("h r t -> r h t"), in_=qstg.t[0:65, :, 0:SPn]),
                            reads=[qstg.b], dma="qst%d" % (sc_ % 2))
                        lst = lst + S.end_capture()
                    pending.append(lst)
                for cch in range(4):
                    zps = PS[6 + cch % 2]
                    for k in range(8):
                        S.op("pe", lambda e, cch=cch, k=k: e.matmul(
                            zps.t[:, 0:SPn], lhsT=wA.t[:, k, 1536 + cch * 128:1536 + (cch + 1) * 128], rhs=hT.t[:, k, 0:SPn],
                            start=(k == 0), stop=(k == 7)), reads=[hT.b, wA.b], writes=[zps.b])
                    S.op("act", lambda e, cch=cch, zas=zas: e.activation(out=zas.t[:, cch, 0:SPn], in_=zps.t[:, 0:SPn], func=AF.Silu),
                         reads=[zps.b], writes=[zas.b])
                S.op("pool", lambda e, zas=zas, s0=s0: e.dma_start(
                    out=SC[pk]["zaG"][:, s0:s0 + SPn].rearrange("(c p) t -> p c t", p=128), in_=zas.t[:, :, 0:SPn]),
                    reads=[zas.b], dma="zas%d" % (sc_ % 2))
            while pend_b:
                S.replay(pend_b.pop(0))
            flush(0)
        S.barrier()
        st.close()

    def phaseB():
        st = contextlib.ExitStack()
        wB = tile(st, "wB", [128, 8, 1024], BF)
        wg = tile(st, "wg", [128, 4, 512], BF)
        cosT, sinT, lastc = TAB["cosT"], TAB["sinT"], TAB["lastc"]
        st2 = contextlib.ExitStack()
        stage = [tile(st2, "wstgB%d" % i, [128, 8, 512]) for i in range(2)]
        wgs = tile(st2, "wgs", [128, 4, 512])
        load_w_bf(wB, 0, 1024, stage, "wsB")
        S.op("sp", lambda e: e.dma_start(out=wgs.t[:], in_=w_glu), writes=[wgs.b], dma="ld_wg")
        S.op("act", lambda e: e.copy(out=wg.t[:], in_=wgs.t[:]), reads=[wgs.b], writes=[wg.b])
        S.barrier()
        st2.close()
        hTb = [tile(st, "b_hT%d" % i, [128, 8, 512], BF) for i in range(2)]
        uTs = [tile(st, "uT%d" % i, [128, 4, 512], BF) for i in range(2)]
        zsGs = [tile(st, "zsG%d" % i, [128, 4, 512], BF) for i in range(2)]
        zsg = tile(st, "b_zsg", [128, 512])
        bX = [[{nm: tile(st, "b_%s_%d%d" % (nm, par, tl_), [128, 512], BF) for nm in ("brb", "bib")} for tl_ in range(2)] for par in range(2)]
        NSET = 2
        TS = []
        for i in range(NSET):
            d = {}
            for nm in ("brb", "bib", "t1", "t2", "t3", "t4", "wre", "wim", "zrb", "zib", "p1", "p2", "p3", "p4"):
                d[nm] = tile(st, "b_%s%d" % (nm, i), [128, 512], BF)
            for nm in ("zre", "zim"):
                d[nm] = tile(st, "b_%s%d" % (nm, i), [128, 512])
            d["ctmp"] = tile(st, "b_ctmp%d" % i, [128, 2])
            d["bre_ps"], d["bim_ps"] = PS[1 + 2 * i], PS[2 + 2 * i]
            TS.append(d)
        hst = tile(st, "b_hst", [128, 32])
        hb = [Buf() for _ in range(16)]
        yT = tile(st, "b_yT", [128, 512])
        x2 = tile(st, "b_x2", [128, 512])
        gTs = [tile(st, "b_gT%d" % i, [128, 4, 512], BF) for i in range(2)]
        carry_bg = []
        sgl = tile(st, "b_sgl", [128, 512])
        y2 = tile(st, "b_y2", [128, 512])
        yS = [tile(st, "b_yS%d" % i, [128, 4, 512], BF) for i in range(2)]
        ups, yps, gps = PS[0], PS[5], PS[6]
        spc = [0]

        def tile_ops(pk, L, j, cq, jj, d, uT):
            cs, sn = cosT.t[:, j, 0:L], sinT.t[:, j, 0:L]
            lc = lastc[pk]
            ops_ = []
            A = ops_.append
            A(lambda: S.op("pe", lambda e: e.matmul(d["bre_ps"].t[:, 0:L], lhsT=BT.t[:, 0, j * 128:(j + 1) * 128], rhs=uT.t[:, cq, 0:L],
                                                    start=True, stop=True), reads=[BT.b, uT.b], writes=[d["bre_ps"].b]))
            A(lambda: S.op("pe", lambda e: e.matmul(d["bim_ps"].t[:, 0:L], lhsT=BT.t[:, 1, j * 128:(j + 1) * 128], rhs=uT.t[:, cq, 0:L],
                                                    start=True, stop=True), reads=[BT.b, uT.b], writes=[d["bim_ps"].b]))
            A(lambda: S.op("act", lambda e: e.copy(out=d["brb"].t[:, 0:L], in_=d["bre_ps"].t[:, 0:L]), reads=[d["bre_ps"].b], writes=[d["brb"].b]))
            A(lambda: S.op("act", lambda e: e.copy(out=d["bib"].t[:, 0:L], in_=d["bim_ps"].t[:, 0:L]), reads=[d["bim_ps"].b], writes=[d["bib"].b]))

            def tt(o, a_, b_, op, tab=None):
                rd = [d[a_].b] + ([d[b_].b] if isinstance(b_, str) else [tab.b])
                bb = d[b_].t[:, 0:L] if isinstance(b_, str) else b_
                return lambda: S.op("dve", lambda e: e.tensor_tensor(out=d[o].t[:, 0:L], in0=d[a_].t[:, 0:L], in1=bb, op=op),
                                    reads=rd, writes=[d[o].b])
            A(tt("t1", "brb", cs, ALU.mult, cosT))
            A(tt("t2", "bib", sn, ALU.mult, sinT))
            A(tt("wre", "t1", "t2", ALU.add))
            A(tt("t3", "bib", cs, ALU.mult, cosT))
            A(tt("t4", "brb", sn, ALU.mult, sinT))
            A(tt("wim", "t3", "t4", ALU.subtract))
            rb = rho.t[:, j:j + 1].to_broadcast([128, L])
            A(lambda: S.op("dve", lambda e: e.tensor_tensor_scan(out=d["zre"].t[:, 0:L], data0=rb, data1=d["wre"].t[:, 0:L],
                                                                 initial=hst.t[:, j:j + 1], op0=ALU.mult, op1=ALU.add),
                           reads=[d["wre"].b, hb[j], rho.b], writes=[d["zre"].b]))
            A(lambda: S.op("dve", lambda e: e.tensor_tensor_scan(out=d["zim"].t[:, 0:L], data0=rb, data1=d["wim"].t[:, 0:L],
                                                                 initial=hst.t[:, 16 + j:17 + j], op0=ALU.mult, op1=ALU.add),
                           reads=[d["wim"].b, hb[j], rho.b], writes=[d["zim"].b]))
            A(lambda: S.op("act", lambda e: e.copy(out=d["zrb"].t[:, 0:L], in_=d["zre"].t[:, 0:L]), reads=[d["zre"].b], writes=[d["zrb"].b]))
            A(lambda: S.op("act", lambda e: e.copy(out=d["zib"].t[:, 0:L], in_=d["zim"].t[:, 0:L]), reads=[d["zim"].b], writes=[d["zib"].b]))
            ct = d["ctmp"]
            A(lambda: S.op("act", lambda e: e.activation(out=ct.t[:, 0:1], in_=d["zim"].t[:, L - 1:L], func=AF.Identity,
                                                         scale=lc.t[:, 2, j:j + 1]), reads=[d["zim"].b, lc.b], writes=[ct.b]))
            A(lambda: S.op("act", lambda e: e.activation(out=ct.t[:, 1:2], in_=d["zim"].t[:, L - 1:L], func=AF.Identity,
                                                         scale=lc.t[:, 0, j:j + 1]), reads=[d["zim"].b, lc.b], writes=[ct.b]))
            A(lambda: S.op("act", lambda e: e.activation(out=hst.t[:, j:j + 1], in_=d["zre"].t[:, L - 1:L], func=AF.Identity,
                                                         scale=lc.t[:, 0, j:j + 1], bias=ct.t[:, 0:1]),
                           reads=[d["zre"].b, lc.b, ct.b], writes=[hb[j]]))
            A(lambda: S.op("act", lambda e: e.activation(out=hst.t[:, 16 + j:17 + j], in_=d["zre"].t[:, L - 1:L], func=AF.Identity,
                                                         scale=lc.t[:, 1, j:j + 1], bias=ct.t[:, 1:2]),
                           reads=[d["zre"].b, lc.b, ct.b], writes=[hb[j]]))
            A(tt("p1", "zrb", cs, ALU.mult, cosT))
            A(tt("p2", "zib", sn, ALU.mult, sinT))
            A(tt("p3", "zrb", sn, ALU.mult, sinT))
            A(tt("p4", "zib", cs, ALU.mult, cosT))
            for q_, (pl, nm) in enumerate(((0, "p1"), (2, "p2"), (1, "p3"), (1, "p4"))):
                A(lambda pl=pl, nm=nm, q_=q_: S.op("pe", lambda e: e.matmul(
                    yps.t[:, 0:L], lhsT=CT.t[:, pl, j * 128:(j + 1) * 128], rhs=d[nm].t[:, 0:L],
                    start=(jj == 0 and q_ == 0), stop=(jj == 3 and q_ == 3)), reads=[CT.b, d[nm].b], writes=[yps.b]))
            return ops_

        for pk in ("s", "p"):
            P = PATHS[pk]
            SPn, T = P["SP"], P["T"]
            L = SPn
            if pk == "s":
                S.op("dve", lambda e: e.tensor_copy(out=hst.t[:, 0:16], in_=smv("stT_re")), reads=[sm.b] + hb, writes=hb)
                S.op("dve", lambda e: e.tensor_copy(out=hst.t[:, 16:32], in_=smv("stT_im")), reads=[sm.b] + hb, writes=hb)
            else:
                S.op("dve", lambda e: e.memset(hst.t[:], 0.0), reads=hb, writes=hb)
            def prologue(s0, uT, zsG):
                hT = hTb[(s0 // SPn) % 2]
                S.begin_capture()
                S.op("sp", lambda e: e.dma_start(out=hT.t[:, :, 0:L], in_=SC[pk]["hT"][:, s0:s0 + L].rearrange("(k p) t -> p k t", p=128)),
                     writes=[hT.b], dma="bhT%d" % ((s0 // SPn) % 2))
                norm_l = S.end_capture()
                groups = []
                for sec in (0, 1):
                    for cch in range(4):
                        S.begin_capture()
                        for k in range(8):
                            S.op("pe", lambda e, sec=sec, cch=cch, k=k: e.matmul(
                                ups.t[:, 0:L], lhsT=wB.t[:, k, sec * 512 + cch * 128:sec * 512 + (cch + 1) * 128],
                                rhs=hT.t[:, k, 0:L], start=(k == 0), stop=(k == 7)), reads=[hT.b, wB.b], writes=[ups.b])
                        if sec == 0:
                            S.op("act", lambda e, cch=cch: e.copy(out=uT.t[:, cch, 0:L], in_=ups.t[:, 0:L]), reads=[ups.b], writes=[uT.b])
                        else:
                            S.op("act", lambda e, cch=cch: e.activation(out=zsG.t[:, cch, 0:L], in_=ups.t[:, 0:L], func=AF.Silu),
                                 reads=[ups.b], writes=[zsG.b])
                        groups.append(S.end_capture())
                return norm_l, groups

            span_list = list(range(0, T, SPn))
            pro_n, pro_g = prologue(span_list[0], uTs[spc[0] % 2], zsGs[spc[0] % 2])
            S.replay(pro_n)
            for g_ in pro_g:
                S.replay(g_)
            for sidx, s0 in enumerate(span_list):
                sc_ = spc[0]
                spc[0] += 1
                ys, uT, zsG = yS[sc_ % 2], uTs[sc_ % 2], zsGs[sc_ % 2]
                nxt, nxt_groups = [], []
                if sidx + 1 < len(span_list):
                    nxt, nxt_groups = prologue(span_list[sidx + 1], uTs[(sc_ + 1) % 2], zsGs[(sc_ + 1) % 2])
                pairs = [(cq, pr) for cq in range(4) for pr in range(2)]

                def pair_ops(n):
                    cq, pr = pairs[n]
                    outl = []
                    for tl_ in range(2):
                        dd = dict(TS[tl_])
                        dd.update(bX[n % 2][tl_])
                        outl.append(tile_ops(pk, L, cq * 4 + pr * 2 + tl_, cq, pr * 2 + tl_, dd, uT))
                    return outl
                cur = pair_ops(0)
                for i in range(4):
                    cur[0][i]()
                    cur[1][i]()
                gT = gTs[sc_ % 2]
                bg = list(carry_bg)
                bg_norm = list(nxt)
                del carry_bg[:]

                def bg_step():
                    if bg:
                        S.replay([bg.pop(0)])
                    if bg_norm:
                        S.replay([bg_norm.pop(0)])

                def gelu_ops(cq):
                    S.begin_capture()
                    dcol = smv("dT")[:, cq:cq + 1]
                    S.op("dve", lambda e: e.scalar_tensor_tensor(
                        out=yT.t[:, 0:L], in0=uT.t[:, cq, 0:L], scalar=dcol, in1=yps.t[:, 0:L], op0=ALU.mult, op1=ALU.add),
                        reads=[uT.b, yps.b, sm.b], writes=[yT.b])
                    S.op("act", lambda e: e.activation(out=x2.t[:, 0:L], in_=yT.t[:, 0:L], func=AF.Square), reads=[yT.b], writes=[x2.b])
                    S.op("pool", lambda e: e.tensor_scalar(out=x2.t[:, 0:L], in0=x2.t[:, 0:L], scalar1=0.044715, scalar2=1.0,
                                                          op0=ALU.mult, op1=ALU.add), reads=[x2.b], writes=[x2.b])
                    S.op("pool", lambda e: e.tensor_tensor(out=x2.t[:, 0:L], in0=x2.t[:, 0:L], in1=yT.t[:, 0:L], op=ALU.mult),
                         reads=[x2.b, yT.b], writes=[x2.b])
                    S.op("act", lambda e: e.activation(out=x2.t[:, 0:L], in_=x2.t[:, 0:L], func=AF.Sigmoid,
                                                       scale=2.0 * math.sqrt(2.0 / math.pi)), reads=[x2.b], writes=[x2.b])
                    S.op("pool", lambda e: e.tensor_tensor(out=gT.t[:, cq, 0:L], in0=x2.t[:, 0:L], in1=yT.t[:, 0:L], op=ALU.mult),
                         reads=[x2.b, yT.b], writes=[gT.b])
                    return S.end_capture()

                for n, (cq, pr) in enumerate(pairs):
                    la, lb = cur
                    if n + 1 < len(pairs):
                        cur = pair_ops(n + 1)
                        for i in range(4):
                            cur[0][i]()
                            cur[1][i]()
                    if n >= 4 and nxt_groups:
                        while bg_norm:
                            S.replay([bg_norm.pop(0)])
                        for g_ in nxt_groups[2 * (n - 4):2 * (n - 4) + 2]:
                            S.replay(g_)
                    na = len(la) - 4
                    for i in range(4, na):
                        la[i]()
                        lb[i]()
                        bg_step()
                    for i in range(na, len(la)):
                        la[i]()
                    for i in range(na, len(lb)):
                        lb[i]()
                    if pr == 1:
                        g_ = gelu_ops(cq)
                        if cq < 3:
                            merged = []
                            rest = bg[:]
                            del bg[:]
                            for op_ in g_:
                                merged.append(op_)
                                merged.extend(rest[:2])
                                rest = rest[2:]
                            bg.extend(merged + rest)
                        else:
                            last_gelu = g_
                while bg or bg_norm:
                    bg_step()
                S.begin_capture()
                for co in range(4):
                    for ci in range(4):
                        S.op("pe", lambda e, co=co, ci=ci: e.matmul(gps.t[:, 0:L], lhsT=wg.t[:, ci, co * 128:(co + 1) * 128],
                                                                    rhs=gT.t[:, ci, 0:L], start=(ci == 0), stop=(ci == 3)),
                             reads=[wg.b, gT.b], writes=[gps.b])
                    S.op("act", lambda e, co=co: e.activation(out=sgl.t[:, 0:L], in_=gps.t[:, 0:L], func=AF.Sigmoid,
                                                              bias=smv("b_gluT")[:, co:co + 1]), reads=[gps.b, sm.b], writes=[sgl.b])
                    S.op("pool", lambda e, co=co: e.tensor_tensor(out=y2.t[:, 0:L], in0=sgl.t[:, 0:L], in1=gT.t[:, co, 0:L], op=ALU.mult),
                         reads=[sgl.b, gT.b], writes=[y2.b])
                    S.op("pool", lambda e, co=co, ys=ys, zsG=zsG: e.tensor_tensor(out=ys.t[:, co, 0:L], in0=y2.t[:, 0:L], in1=zsG.t[:, co, 0:L],
                                                                               op=ALU.mult), reads=[y2.b, zsG.b], writes=[ys.b])
                S.op("pool", lambda e, ys=ys, s0=s0: e.dma_start(
                    out=SC[pk]["yssm"][:, s0:s0 + SPn].rearrange("(c p) t -> p c t", p=128), in_=ys.t[:, :, 0:SPn]),
                    reads=[ys.b], dma="ys%d" % (sc_ % 2))
                glu = S.end_capture()
                if sidx + 1 < len(span_list):
                    carry_bg.extend(last_gelu + glu)
                else:
                    S.replay(last_gelu + glu)
            S.op("pool", lambda e, pk=pk: e.dma_start(out=O["ssm_" + pk], in_=hst.t[:, :]), reads=hb, dma="hsto")
        S.barrier()
        st.close()

    def phase2C():
        st = contextlib.ExitStack()
        yatt = {"p": tile(st, "yatt_p", [128, 4, T_P], BF), "s": tile(st, "yatt_s", [128, 4, T_S], BF)}
        st2 = contextlib.ExitStack()
        kT = [tile(st2, "c_kT%d" % i, [65, T_P], BF) for i in range(2)]
        vA = [tile(st2, "c_vA%d" % i, [128, 64, 128], BF) for i in range(2)]
        qT = [tile(st2, "c_qT%d" % i, [65, 512], BF) for i in range(2)]
        zg = [tile(st2, "c_zg%d" % i, [128, 512], BF) for i in range(2)]
        NSB = 4
        kTb2 = [Buf(), Buf()]
        vAb2 = [Buf(), Buf()]
        pT = [tile(st2, "c_pT%d" % i, [128, 512], BF) for i in range(NSB)]
        rdens = [tile(st2, "c_rden%d" % i, [128, 512]) for i in range(2)]
        rdh = [tile(st2, "c_rdh%d" % i, [128, 512], BF) for i in range(2)]
        rdl = [tile(st2, "c_rdl%d" % i, [128, 512], BF) for i in range(2)]
        epi_q = []
        gbc = tile(st2, "c_gbc", [128, 512])
        sps = [PS[0], PS[1], PS[2], PS[6]]
        ops = [PS[3], PS[4]]
        bcps = PS[5]
        LA = 3
        heads = [(pk, h) for pk in ("s", "p") for h in range(8)]
        spans = []
        tasks = []
        for hi, (pk, h) in enumerate(heads):
            P = PATHS[pk]
            T, SPn, npast = P["T"], P["SP"], P["npast"]
            nk = npast + T
            for s0 in range(0, T, SPn):
                si = len(spans)
                qpos0 = npast + s0
                last_blk = (qpos0 + SPn - 1) // 128
                spans.append(dict(pk=pk, h=h, hi=hi, s0=s0, SPn=SPn, si=si, last_blk=last_blk, nk=nk))
                for j in range(last_blk + 1):
                    kk = min(128, nk - j * 128)
                    c0 = max(0, j * 128 - qpos0)
                    N = SPn - c0
                    diag = (j * 128 + kk - 1) > (qpos0 + c0)
                    tasks.append(dict(si=si, j=j, kk=kk, c0=c0, N=N, diag=diag, last=(j == last_blk)))
        ntasks_of = [sp["last_blk"] + 1 for sp in spans]
        last_span_of_head = {}
        for sp in spans:
            last_span_of_head[sp["hi"]] = sp["si"]

        def head_load(hi):
            if hi >= len(heads):
                return
            pk, h = heads[hi]
            hb = hi % 2
            kt, va = kT[hb], vA[hb]
            nk = PATHS[pk]["npast"] + PATHS[pk]["T"]
            nh = (nk // 256) * 128
            S.op("sp", lambda e: e.dma_start(out=kt.t[0:65, 0:nh], in_=SC[pk]["kTa"][h, :, 0:nh]), writes=[kt.b], dma="ckT%d" % hb)
            S.op("pool", lambda e: e.dma_start(out=kt.t[0:65, nh:nk], in_=SC[pk]["kTa"][h, :, nh:nk]), writes=[kTb2[hb]], dma="ckTb%d" % hb)
            nfull = nk // 128
            nfh = nfull // 2
            S.op("sp", lambda e: e.dma_start(
                out=va.t[:, 0:nfh, :], in_=SC[pk]["va"][0:nfh * 128, h, :].rearrange("(b p) c -> p b c", p=128)),
                writes=[va.b], dma="cvA%d" % hb)
            S.op("pool", lambda e: e.dma_start(
                out=va.t[:, nfh:nfull, :], in_=SC[pk]["va"][nfh * 128:nfull * 128, h, :].rearrange("(b p) c -> p b c", p=128)),
                writes=[vAb2[hb]], dma="cvAb%d" % hb)
            if nk % 128:
                r = nk % 128
                S.op("sp", lambda e: e.dma_start(out=va.t[0:r, nfull, :], in_=SC[pk]["va"][nfull * 128:nk, h, :]),
                     writes=[va.b], dma="cvA%d" % hb)

        def span_load(si):
            if si >= len(spans):
                return
            sp = spans[si]
            pk, h, s0, SPn = sp["pk"], sp["h"], sp["s0"], sp["SPn"]
            qb = si % 2
            prw = slice(64, 128) if h % 2 else slice(0, 64)
            S.op("sp", lambda e: e.dma_start(out=qT[qb].t[0:65, 0:SPn], in_=SC[pk]["qTa"][h, :, s0:s0 + SPn]),
                 writes=[qT[qb].b], dma="cqT%d" % qb)
            S.op("sp", lambda e: e.dma_start(out=zg[qb].t[prw, 0:SPn], in_=SC[pk]["zaG"][h * 64:(h + 1) * 64, s0:s0 + SPn]),
                 writes=[zg[qb].b], dma="czg%d" % qb)

        def emit_S(i):
            t = tasks[i]
            sp = spans[t["si"]]
            kt, qt = kT[sp["hi"] % 2], qT[sp["si"] % 2]
            sp_ = sps[i % NSB]
            j, kk, c0, N, diag = t["j"], t["kk"], t["c0"], t["N"], t["diag"]
            S.op("pe", lambda e: e.matmul(sp_.t[0:kk, 0:N], lhsT=kt.t[0:65, j * 128:j * 128 + kk], rhs=qt.t[0:65, c0:c0 + N],
                                          start=True, stop=(not diag)), reads=[kt.b, kTb2[sp["hi"] % 2], qt.b], writes=[sp_.b])
            if diag:
                nd = min(kk, N)
                S.op("pe", lambda e: e.matmul(sp_.t[0:kk, 0:nd], lhsT=ident_bf[0:kk, 0:kk], rhs=maskT[0:kk, 0:nd], start=False, stop=True),
                     reads=[cb.b], writes=[sp_.b])

        def emit_PV(i):
            t = tasks[i]
            sp = spans[t["si"]]
            pk, h, s0, SPn, si = sp["pk"], sp["h"], sp["s0"], sp["SPn"], sp["si"]
            va = vA[sp["hi"] % 2]
            sp_, ptl, opsb = sps[i % NSB], pT[i % NSB], ops[si % 2]
            j, kk, c0, N = t["j"], t["kk"], t["c0"], t["N"]
            odd = h % 2
            prw = slice(64, 128) if odd else slice(0, 64)
            drow = 0 if odd else 64
            M = 128 if odd else 65
            S.op("act", lambda e: e.activation(out=ptl.t[0:kk, 0:N], in_=sp_.t[0:kk, 0:N], func=AF.Exp, scale=0.125,
                                               bias=biasK[pk].t[0:kk, j, h:h + 1]), reads=[sp_.b, biasK[pk].b], writes=[ptl.b])
            S.op("pe", lambda e: e.matmul(opsb.t[0:M, c0:c0 + N], lhsT=va.t[0:kk, j, 0:M], rhs=ptl.t[0:kk, 0:N],
                                          start=(j == 0), stop=t["last"]), reads=[va.b, vAb2[sp["hi"] % 2], ptl.b], writes=[opsb.b])
            if t["last"]:
                zgt = zg[si % 2]
                rd = rdens[si % 2]
                rh, rl = rdh[si % 2], rdl[si % 2]
                S.op("dve", lambda e: e.reciprocal(out=rd.t[drow:drow + 1, 0:SPn], in_=opsb.t[drow:drow + 1, 0:SPn]),
                     reads=[opsb.b], writes=[rd.b])
                S.op("dve", lambda e: e.tensor_copy(out=rh.t[drow:drow + 1, 0:SPn], in_=rd.t[drow:drow + 1, 0:SPn]),
                     reads=[rd.b], writes=[rh.b])
                S.op("dve", lambda e: e.tensor_tensor(out=rl.t[drow:drow + 1, 0:SPn], in0=rd.t[drow:drow + 1, 0:SPn],
                                                      in1=rh.t[drow:drow + 1, 0:SPn], op=ALU.subtract),
                     reads=[rd.b, rh.b], writes=[rl.b])

                def epi():
                    S.op("pe", lambda e: e.matmul(bcps.t[:, 0:SPn], lhsT=ones_bf.t[drow:drow + 1, 0:128], rhs=rh.t[drow:drow + 1, 0:SPn],
                                                  start=True, stop=False), reads=[rh.b, ones_bf.b], writes=[bcps.b])
                    S.op("pe", lambda e: e.matmul(bcps.t[:, 0:SPn], lhsT=ones_bf.t[drow:drow + 1, 0:128], rhs=rl.t[drow:drow + 1, 0:SPn],
                                                  start=False, stop=True), reads=[rl.b, ones_bf.b], writes=[bcps.b])
                    S.op("dve", lambda e: e.tensor_tensor(out=gbc.t[prw, 0:SPn], in0=bcps.t[prw, 0:SPn], in1=zgt.t[prw, 0:SPn], op=ALU.mult),
                         reads=[bcps.b, zgt.b], writes=[gbc.b])
                    S.op("dve", lambda e: e.tensor_tensor(out=yatt[pk].t[prw, h // 2, s0:s0 + SPn], in0=opsb.t[prw, 0:SPn],
                                                          in1=gbc.t[prw, 0:SPn], op=ALU.mult), reads=[opsb.b, gbc.b], writes=[yatt[pk].b])
                    span_load(si + 2)
                    if last_span_of_head[sp["hi"]] == si:
                        head_load(sp["hi"] + 2)
                nxt_n = ntasks_of[si + 1] if si + 1 < len(spans) else 8
                epi_q.append([max(1, min(5, nxt_n - LA)), epi])

        head_load(0)
        head_load(1)
        span_load(0)
        span_load(1)
        for i in range(len(tasks) + LA):
            if i < len(tasks):
                emit_S(i)
            for it in epi_q:
                it[0] -= 1
            while epi_q and epi_q[0][0] <= 0:
                epi_q.pop(0)[1]()
            if i >= LA:
                emit_PV(i - LA)
        while epi_q:
            epi_q.pop(0)[1]()
        S.barrier()
        st2.close()
        wos = [tile(st, "c_wos%d" % i, [128, 1024]) for i in range(2)]
        wo = {"p": tile(st, "c_wo_p", [128, 8, 1024], BF), "s": tile(st, "c_wo_s", [128, 8, 1024], BF)}
        for k in range(8):
            wsl = wos[k % 2]
            S.op("sp", lambda e, wsl=wsl, k=k: e.dma_start(out=wsl.t[:], in_=w_out[:, k, :]), writes=[wsl.b], dma="cwo%d" % (k % 2))
            for pk in ("p", "s"):
                pi = PATHS[pk]["pi"]
                S.op("dve", lambda e, wsl=wsl, k=k, pk=pk, pi=pi: e.tensor_tensor(out=wo[pk].t[:, k, :], in0=wsl.t[:], in1=gate_bc.t[:, pi, :],
                                                                               op=ALU.mult), reads=[wsl.b, gate_bc.b], writes=[wo[pk].b])
        ysm = [tile(st, "c_ysm%d" % i, [128, 4, 512], BF) for i in range(2)]
        xr = [tile(st, "c_xr%d" % i, [128, D]) for i in range(2)]
        yo = [tile(st, "c_yo%d" % i, [128, D]) for i in range(2)]
        o_ps = [PS[0], PS[1], PS[2], PS[3]]
        sc2 = [0]
        tc2 = [0]
        for pk in ("s", "p"):
            P = PATHS[pk]
            T, SPn, TT = P["T"], P["SP"], P["TT"]
            for s0 in range(0, T, SPn):
                sb = sc2[0] % 2
                sc2[0] += 1
                ym = ysm[sb]
                S.op("sp", lambda e, ym=ym, s0=s0: e.dma_start(
                    out=ym.t[:, :, 0:SPn], in_=SC[pk]["yssm"][:, s0:s0 + SPn].rearrange("(c p) t -> p c t", p=128)),
                    writes=[ym.b], dma="cys%d" % sb)
                for ti in range(SPn // TT):
                    tb = tc2[0] % 2
                    tc2[0] += 1
                    t0 = s0 + ti * TT
                    xt, yt = xr[tb], yo[tb]
                    S.op("sp", lambda e, xt=xt, t0=t0: e.dma_start(out=xt.t[0:TT, :], in_=xin[pk][t0:t0 + TT, :]), writes=[xt.b], dma="cxr%d" % tb)
                    for half in range(2):
                        op_ = o_ps[tb * 2 + half]
                        for c in range(8):
                            if c < 4:
                                lh, lb = ym.t[:, c, ti * TT:(ti + 1) * TT], ym.b
                            else:
                                lh, lb = yatt[pk].t[:, c - 4, t0:t0 + TT], yatt[pk].b
                            S.op("pe", lambda e, op_=op_, lh=lh, c=c, half=half: e.matmul(
                                op_.t[0:TT, :], lhsT=lh, rhs=wo[pk].t[:, c, half * 512:(half + 1) * 512], start=(c == 0), stop=(c == 7)),
                                reads=[lb, wo[pk].b], writes=[op_.b])
                        S.op("dve", lambda e, op_=op_, xt=xt, yt=yt, half=half: e.tensor_tensor(
                            out=yt.t[0:TT, half * 512:(half + 1) * 512], in0=op_.t[0:TT, :], in1=xt.t[0:TT, half * 512:(half + 1) * 512],
                            op=ALU.add), reads=[op_.b, xt.b], writes=[yt.b])
                    S.op("pool", lambda e, yt=yt, t0=t0: e.dma_start(out=O["y_" + pk][t0:t0 + TT, :], in_=yt.t[0:TT, :]),
                         reads=[yt.b], dma="cyo%d" % tb)
        S.barrier()
        st.close()

    if "0" in phases:
        phase0()
    stTab = contextlib.ExitStack()
    if "A" in phases:
        alloc_tables(stTab)
        phaseN()
        phaseA()
    if "B" in phases:
        phaseB()
    stTab.close()
    if "2" in phases:
        phase2C()
    S.barrier()
    S.emit(nc, root)
    root.close()
    return nc


def _common_inputs(inp):
    f = np.float32
    d = {}
    d["w_ada"] = np.ascontiguousarray(inp["w_ada"][0], f)
    d["w_in"] = np.ascontiguousarray(inp["w_in"][0], f)
    d["w_out"] = np.ascontiguousarray(inp["w_out"][0], f)
    d["w_glu"] = np.ascontiguousarray(inp["w_glu"][0], f)
    bc = np.zeros((128, 2048), f)
    bc[:, 0:1024] = inp["b_ada"][0][2048:3072][None, :]
    bc[:, 1024:1536] = np.tile(inp["q_norm_g"][0], 8)[None, :]
    bc[:, 1536:2048] = np.tile(inp["k_norm_g"][0], 8)[None, :]
    d["bc"] = bc
    rep3 = np.zeros((128, 3, 2048), f)
    rep3[:, 0, :] = inp["ssm_a_re"][0].reshape(-1)[None, :]
    rep3[:, 1, :] = inp["ssm_a_im"][0].reshape(-1)[None, :]
    rep3[:, 2, :] = np.repeat(inp["ssm_log_dt"][0], 64)[None, :]
    d["rep3"] = rep3
    BTl = np.zeros((128, 2, 16, 2, 64), f)
    CTl = np.zeros((128, 2, 16, 128), f)
    for pl, (bsrc, csrc) in enumerate(((inp["ssm_b_re"][0], inp["ssm_c_re"][0]), (inp["ssm_b_im"][0], inp["ssm_c_im"][0]))):
        for g in range(32):
            j, g2, gl = g // 2, g % 2, g % 8
            BTl[gl * 16:(gl + 1) * 16, pl, j, g2, :] = bsrc[g].T
            CTl[g2 * 64:(g2 + 1) * 64, pl, j, gl * 16:(gl + 1) * 16] = csrc[g].T
    d["BTlay"] = BTl.reshape(128, 2, 2048)
    d["CTlay"] = CTl.reshape(128, 2, 2048)
    cf = np.zeros((128, NCF), f)
    cf[:, CF_ID:CF_ID + 128] = np.eye(128, dtype=f)
    cf[:, CF_TRI:CF_TRI + 128] = np.triu(np.ones((128, 128), f))
    cf[127, CF_SEL128:CF_SEL128 + 128] = 1.0
    cf[63, CF_SEL64:CF_SEL64 + 128] = 1.0
    cf[:, CF_IOTA:CF_IOTA + 512] = np.arange(1, 513, dtype=f)[None, :]
    d["cf"] = cf
    cb = np.zeros((128, 256), f)
    cb[:, 0:128] = np.eye(128, dtype=f)
    cb[:, 128:256] = np.where(np.arange(128)[:, None] > np.arange(128)[None, :], -240000.0, 0.0)
    d["cb"] = cb.astype(ml_dtypes.bfloat16)
    return d


def _smalls(inp, b):
    f = np.float32
    sm = np.zeros((128, NS), f)

    def put(name, arr):
        a, e = SM[name]
        sm[:, a:e] = arr

    put("b_adaT", inp["b_ada"][0][0:2048].reshape(16, 128).T)
    put("norm_gT", inp["norm_g"][0].reshape(8, 128).T)
    put("bf_bc", np.broadcast_to(inp["b_f"][0][None, :], (128, 8)))
    put("dT", inp["ssm_d"][0].reshape(4, 128).T)
    put("b_gluT", inp["b_glu"][0].reshape(4, 128).T)
    put("logdtT", np.repeat(inp["ssm_log_dt"][0], 64).reshape(16, 128).T)
    put("a_reT", inp["ssm_a_re"][0].reshape(16, 128).T)
    put("a_imT", inp["ssm_a_im"][0].reshape(16, 128).T)
    c2 = np.stack([inp["c_prompt"][b].reshape(8, 128).T, inp["c_sample"][b].reshape(8, 128).T], axis=-1)
    put("c2T", c2.reshape(128, 16))
    put("stT_re", inp["state_ssm_re"][0, b].reshape(16, 128).T)
    put("stT_im", inp["state_ssm_im"][0, b].reshape(16, 128).T)
    return sm


_NC_CACHE = {}


def kernel(**inp):
    inp = {k: np.asarray(v) for k, v in inp.items()}
    if "nc" not in _NC_CACHE:
        _NC_CACHE["nc"] = build_program()
    nc = _NC_CACHE["nc"]
    common = _common_inputs(inp)
    in_maps = []
    f = np.float32
    for b in range(NCORES):
        m = dict(common)
        m["smalls"] = _smalls(inp, b)
        m["x_p"] = np.ascontiguousarray(inp["x_prompt"][b], f)
        m["x_s"] = np.ascontiguousarray(inp["x_sample"][b], f)
        m["ck"] = np.ascontiguousarray(inp["cache_k"][0, b].reshape(NPAST, 512), f)
        m["cv"] = np.ascontiguousarray(inp["cache_v"][0, b].reshape(NPAST, 512), f)
        m["clf"] = np.ascontiguousarray(inp["cache_logf"][0, b], f)
        in_maps.append(m)
    res = run_bass_kernel_spmd(nc, in_maps, core_ids=list(range(NCORES)))
    R = res.results

    def stack(name):
        return np.stack([np.asarray(R[b][name], f) for b in range(NCORES)], axis=0)

    def ssm(name, lo):
        a = stack(name)[:, :, lo:lo + 16]
        return np.ascontiguousarray(a.transpose(0, 2, 1).reshape(NCORES, 32, 64))[None]

    y_p = stack("y_p")
    y_s = stack("y_s")
    k_p = stack("k_p").reshape(1, NCORES, T_P, 8, 64)
    v_p = stack("v_p").reshape(1, NCORES, T_P, 8, 64)
    lf_p = stack("lf_p")[None]
    k_s = stack("k_s").reshape(1, NCORES, T_S, 8, 64)
    v_s = stack("v_s").reshape(1, NCORES, T_S, 8, 64)
    lf_s = stack("lf_s")[None]
    return (y_p, y_s, k_p, v_p, lf_p, ssm("ssm_p", 0), ssm("ssm_p", 16),
            k_s, v_s, lf_s, ssm("ssm_s", 0), ssm("ssm_s", 16))
```
